# Optimizing a Trainium2 kernel written in Bass

```python
import math
import jax
import jax.numpy as jnp
from jax import lax
import numpy as np

D_MODEL = 1024
BATCH = 8
SEQ = 4096
DEPTH = 1
DEC_BATCH = 2
DEC_SEQ = 16384
PAST_LEN = 128

ATTN_WIDTH = D_MODEL // 2
HYENA_WIDTH = D_MODEL - ATTN_WIDTH
HEAD_DIM = 64
N_HEADS = ATTN_WIDTH // HEAD_DIM
DILATED_PATTERNS = ((128, 1), (512, 4), (2048, 16))
HYENA_ORDER = 2
SHORT_CONV_WIDTH = 3
FILTER_EMB_DIM = 33
FILTER_HIDDEN = 64
FILTER_FAST_DECAY_PCT = 0.3
FILTER_SLOW_DECAY_PCT = 1.5
FILTER_DECAY_TARGET = 1e-2
PROJ_WIDTH = 4 * ATTN_WIDTH + (HYENA_ORDER + 2) * HYENA_WIDTH
LN_EPS = 1e-5
RMS_EPS = 1e-6
NEG_INF = -1e30

kernel_name = 'hymba_hyena_dilated_alibi_deepnorm_encoder'


def _alibi_slopes(n):
    return jnp.asarray([2.0 ** (-8.0 * (i + 1) / n) for i in range(n)], jnp.float32)


def _dilated_window_attention(q, k, v, slopes, window, dilation):
    B, S, H, Dh = q.shape
    half = window // (2 * dilation)
    blk = half
    s_sub = S // dilation
    n_blk = -(-s_sub // blk)
    pad = n_blk * blk - s_sub
    N = B * dilation

    def to_sub(t):
        return t.reshape(B, s_sub, dilation, H, Dh).transpose(0, 2, 1, 3, 4).reshape(N, s_sub, H, Dh)

    qs, ks, vs = to_sub(q), to_sub(k), to_sub(v)
    qb = jnp.pad(qs, ((0, 0), (0, pad), (0, 0), (0, 0))).reshape(N, n_blk, blk, H, Dh)

    def neighbours(t):
        t = jnp.pad(t, ((0, 0), (blk, pad + blk), (0, 0), (0, 0))).reshape(N, n_blk + 2, blk, H, Dh)
        return jnp.concatenate([t[:, :-2], t[:, 1:-1], t[:, 2:]], axis=2)

    kb, vb = neighbours(ks), neighbours(vs)
    scores = jnp.einsum('nbqhd,nbkhd->nbhqk', qb, kb) * (Dh ** -0.5)
    q_idx = jnp.arange(n_blk)[:, None] * blk + jnp.arange(blk)[None, :]
    k_idx = jnp.arange(n_blk)[:, None] * blk - blk + jnp.arange(3 * blk)[None, :]
    rel = jnp.abs(k_idx[:, None, :] - q_idx[:, :, None])
    valid = (rel <= half) & (k_idx[:, None, :] >= 0) & (k_idx[:, None, :] < s_sub)
    dist = (rel * dilation).astype(jnp.float32)
    bias = -slopes[None, :, None, None] * dist[:, None]
    scores = jnp.where(valid[:, None], scores + bias, NEG_INF)
    m = jnp.max(scores, axis=-1, keepdims=True)
    p = jnp.exp(scores - m)
    l = jnp.sum(p, axis=-1, keepdims=True)
    o = jnp.einsum('nbhqk,nbkhd->nbqhd', p, vb) / jnp.swapaxes(l, 2, 3)
    lse = jnp.swapaxes((m + jnp.log(l))[..., 0], 2, 3)
    o = o.reshape(N, n_blk * blk, H, Dh)[:, :s_sub]
    o = o.reshape(B, dilation, s_sub, H, Dh).transpose(0, 2, 1, 3, 4).reshape(B, S, H, Dh)
    lse = lse.reshape(N, n_blk * blk, H)[:, :s_sub]
    lse = lse.reshape(B, dilation, s_sub, H).transpose(0, 2, 1, 3).reshape(B, S, H)
    return o, lse


def _hyena_filters(L, w1, b1, w2, b2, w3, b3, freq, w4):
    f32 = jnp.float32
    t = jnp.linspace(0.0, 1.0, L, dtype=f32)[:, None]
    bands = (FILTER_EMB_DIM - 1) // 2
    w = 2.0 * math.pi * jnp.arange(L, dtype=f32)[:, None] / L
    fr = jnp.linspace(1e-4, bands - 1, bands, dtype=f32)[None, :]
    z = jnp.concatenate([t, jnp.cos(fr * w), -jnp.sin(fr * w)], axis=-1)
    freq = freq.astype(f32)
    h = jnp.sin(freq[0] * (z @ w1.astype(f32) + b1.astype(f32)))
    h = jnp.sin(freq[1] * (h @ w2.astype(f32) + b2.astype(f32)))
    h = jnp.sin(freq[2] * (h @ w3.astype(f32) + b3.astype(f32)))
    h = (h @ w4.astype(f32)).reshape(L, HYENA_ORDER, 2, HYENA_WIDTH)
    max_decay = math.log(FILTER_DECAY_TARGET) / FILTER_FAST_DECAY_PCT
    min_decay = math.log(FILTER_DECAY_TARGET) / FILTER_SLOW_DECAY_PCT
    deltas = jnp.abs(jnp.linspace(min_decay, max_decay, HYENA_WIDTH, dtype=f32))
    decay = jnp.exp(-t * deltas[None, :])
    return h * decay[:, None, None, :]


def _two_sided_fftconv(u, h_fwd, h_bwd):
    L, C = h_fwd.shape
    filt = jnp.concatenate([h_fwd, jnp.zeros((1, C), h_fwd.dtype), h_bwd[1:][::-1]], axis=0)
    F = jnp.fft.rfft(filt, n=2 * L, axis=0)
    U = jnp.fft.rfft(u, n=2 * L, axis=1)
    return jnp.fft.irfft(U * F[None], n=2 * L, axis=1)[:, :L]


def _short_conv(u, w, b):
    r = SHORT_CONV_WIDTH // 2
    L = u.shape[1]
    up = jnp.pad(u, ((0, 0), (r, r), (0, 0)))
    out = b
    for j in range(SHORT_CONV_WIDTH):
        out = out + up[:, j:j + L] * w[j]
    return out


def _rms(y, g):
    y = y.astype(jnp.float32)
    return y * lax.rsqrt(jnp.mean(y * y, axis=-1, keepdims=True) + RMS_EPS) * g.astype(jnp.float32)


def _layer(x, w_in, conv_w, conv_b, filt_w1, filt_b1, filt_w2, filt_b2, filt_w3, filt_b3,
           filt_freq, filt_w4, hyena_d, attn_norm_g, hyena_norm_g, w_out, ln_g, ln_b):
    f32 = jnp.float32
    B, L, _ = x.shape
    A, C = ATTN_WIDTH, HYENA_WIDTH
    proj = x @ w_in
    q, k, v, g_attn = (proj[..., i * A:(i + 1) * A] for i in range(4))
    u_h = proj[..., 4 * A:4 * A + (HYENA_ORDER + 1) * C]
    g_hyena = proj[..., 4 * A + (HYENA_ORDER + 1) * C:]

    heads = lambda t: t.reshape(B, L, N_HEADS, HEAD_DIM).astype(f32)
    qh, kh, vh = heads(q), heads(k), heads(v)
    slopes = _alibi_slopes(N_HEADS)
    outs, lses = [], []
    for window, dilation in DILATED_PATTERNS:
        o, lse = _dilated_window_attention(qh, kh, vh, slopes, window, dilation)
        outs.append(o)
        lses.append(lse)
    wts = jax.nn.softmax(jnp.stack(lses), axis=0)
    attn = jnp.einsum('pblh,pblhd->blhd', wts, jnp.stack(outs)).reshape(B, L, A)

    u = _short_conv(u_h, conv_w, conv_b).astype(f32)
    z = u[..., :C]
    filters = _hyena_filters(L, filt_w1, filt_b1, filt_w2, filt_b2, filt_w3, filt_b3, filt_freq, filt_w4)
    for n in range(HYENA_ORDER):
        gate = u[..., (n + 1) * C:(n + 2) * C]
        z = gate * (_two_sided_fftconv(z, filters[:, n, 0], filters[:, n, 1]) + hyena_d[n].astype(f32) * z)

    mixed = jnp.concatenate([
        _rms(attn, attn_norm_g) * jax.nn.silu(g_attn.astype(f32)),
        _rms(z, hyena_norm_g) * jax.nn.silu(g_hyena.astype(f32)),
    ], axis=-1).astype(x.dtype)
    out = mixed @ w_out
    alpha = (2.0 * DEPTH) ** 0.25
    h = alpha * x.astype(f32) + out.astype(f32)
    mu = jnp.mean(h, axis=-1, keepdims=True)
    var = jnp.mean(jnp.square(h - mu), axis=-1, keepdims=True)
    y = (h - mu) * lax.rsqrt(var + LN_EPS) * ln_g.astype(f32) + ln_b.astype(f32)
    return y.astype(x.dtype)


def setup_inputs(seed: int = 0) -> dict:
    key = jax.random.key(seed)
    ks = jax.random.split(key, 24)
    f32 = jnp.float32
    nrm = lambda kk, shape, scale: jax.random.normal(kk, shape, f32) * scale
    beta = (8.0 * DEPTH) ** -0.25
    A, C = ATTN_WIDTH, HYENA_WIDTH
    col_scale = jnp.concatenate([
        jnp.ones((2 * A,), f32), jnp.full((A,), beta, f32), jnp.ones((A,), f32),
        jnp.full((C,), beta, f32), jnp.ones(((HYENA_ORDER + 1) * C,), f32)])
    return {
        'x_prompt': nrm(ks[0], (BATCH, SEQ, D_MODEL), 1.0),
        'x_sample': nrm(ks[1], (DEC_BATCH, DEC_SEQ, D_MODEL), 1.0),
        'w_in': nrm(ks[2], (DEPTH, D_MODEL, PROJ_WIDTH), D_MODEL ** -0.5) * col_scale,
        'conv_w': nrm(ks[3], (DEPTH, SHORT_CONV_WIDTH, (HYENA_ORDER + 1) * C), SHORT_CONV_WIDTH ** -0.5),
        'conv_b': nrm(ks[4], (DEPTH, (HYENA_ORDER + 1) * C), 0.01),
        'filt_w1': nrm(ks[5], (DEPTH, FILTER_EMB_DIM, FILTER_HIDDEN), FILTER_EMB_DIM ** -0.5),
        'filt_b1': nrm(ks[6], (DEPTH, FILTER_HIDDEN), 0.1),
        'filt_w2': nrm(ks[7], (DEPTH, FILTER_HIDDEN, FILTER_HIDDEN), FILTER_HIDDEN ** -0.5),
        'filt_b2': nrm(ks[8], (DEPTH, FILTER_HIDDEN), 0.1),
        'filt_w3': nrm(ks[9], (DEPTH, FILTER_HIDDEN, FILTER_HIDDEN), FILTER_HIDDEN ** -0.5),
        'filt_b3': nrm(ks[10], (DEPTH, FILTER_HIDDEN), 0.1),
        'filt_freq': 1.0 + nrm(ks[11], (DEPTH, 3, FILTER_HIDDEN), 0.01),
        'filt_w4': nrm(ks[12], (DEPTH, FILTER_HIDDEN, HYENA_ORDER * 2 * C), FILTER_HIDDEN ** -0.5),
        'hyena_d': nrm(ks[13], (DEPTH, HYENA_ORDER, C), 1.0),
        'attn_norm_g': 1.0 + nrm(ks[14], (DEPTH, A), 0.01),
        'hyena_norm_g': 1.0 + nrm(ks[15], (DEPTH, C), 0.01),
        'w_out': nrm(ks[16], (DEPTH, D_MODEL, D_MODEL), D_MODEL ** -0.5 * beta),
        'ln_g': 1.0 + nrm(ks[17], (DEPTH, D_MODEL), 0.01),
        'ln_b': nrm(ks[18], (DEPTH, D_MODEL), 0.01),
    }


def reference(x_prompt, x_sample, w_in, conv_w, conv_b, filt_w1, filt_b1, filt_w2, filt_b2,
              filt_w3, filt_b3, filt_freq, filt_w4, hyena_d, attn_norm_g, hyena_norm_g,
              w_out, ln_g, ln_b):
    def trunk(x):
        for l in range(DEPTH):
            x = _layer(x, w_in[l], conv_w[l], conv_b[l], filt_w1[l], filt_b1[l], filt_w2[l],
                       filt_b2[l], filt_w3[l], filt_b3[l], filt_freq[l], filt_w4[l], hyena_d[l],
                       attn_norm_g[l], hyena_norm_g[l], w_out[l], ln_g[l], ln_b[l])
        return x
    y_prompt = trunk(x_prompt)
    y_sample = trunk(x_sample)
    return (y_prompt, y_sample)
```

```python
import math
from contextlib import ExitStack

import numpy as np
import ml_dtypes

import concourse.bass as bass
import concourse.mybir as mybir
from concourse.bass_utils import run_bass_kernel_spmd

F32 = mybir.dt.float32
BF16 = mybir.dt.bfloat16
AF = mybir.ActivationFunctionType
ALU = mybir.AluOpType
NPBF = ml_dtypes.bfloat16

SEM_EPOCH = 30000
TWO_PI = 2.0 * math.pi

D_MODEL = 1024
SEQ = 4096
DEC_SEQ = 16384
NCH = 512
T = 4096
NFFT = 8192
HALO = 1024
EXT = T + 2 * HALO
PATTERNS = ((128, 1), (512, 4), (2048, 16))
LN_EPS = 1e-5
RMS_EPS = 1e-6
ALPHA = 2.0 ** 0.25


class Buf:
    __slots__ = ("name", "w", "r")

    def __init__(self, name):
        self.name = name
        self.w = {}
        self.r = {}


class Eng:
    def __init__(self, kb, name, handle):
        self.kb = kb
        self.name = name
        self.h = handle
        self.ops = []
        self.sem = None
        self.cnt = 0
        self.waited = {}
        self.last_tok = None
        self.pending = False

    def new_epoch(self):
        self.sem = self.kb.new_sem(self.name)
        self.cnt = 0


class KB:
    def __init__(self, nc, n_lanes=8):
        self.nc = nc
        self.sems = []
        self._stack = None
        self.eng = {}
        for name, h in (("pe", nc.tensor), ("act", nc.scalar), ("dve", nc.vector),
                        ("pool", nc.gpsimd), ("sp", nc.sync)):
            self.eng[name] = Eng(self, name, h)
        self.lanes = {}
        self.n_lanes = n_lanes
        self.lane_rr = {}

    def new_sem(self, name):
        cm = self.nc.semaphore(f"s{len(self.sems)}_{name}")
        s = self._stack.enter_context(cm)
        self.sems.append(s)
        return len(self.sems) - 1

    def _wait(self, e, tok):
        if tok is None:
            return
        sid, val = tok
        if e.waited.get(sid, 0) >= val:
            return
        e.waited[sid] = val
        sem = self.sems[sid]
        e.ops.append(lambda h, sem=sem, val=val: h.wait_ge(sem, val))

    def _deps(self, e, reads, writes, dwrites=()):
        toks = []
        for b in reads:
            toks.extend(b.w.items())
        for b in writes:
            toks.extend(b.w.items())
            toks.extend(b.r.items())
        for b in dwrites:
            toks.extend(b.r.items())
        for t in toks:
            if e.name == "pe" and e.sem is not None and t[0] == e.sem:
                continue
            self._wait(e, t)

    def _mark(self, tok, reads, writes, dwrites=()):
        sid, val = tok
        for b in list(writes) + list(dwrites):
            if b.w.get(sid, 0) < val:
                b.w[sid] = val
        for b in reads:
            if b.r.get(sid, 0) < val:
                b.r[sid] = val

    def op(self, eng, fn, reads=(), writes=(), inc=True, dwrites=()):
        e = self.eng[eng]
        if e.sem is None or e.cnt >= SEM_EPOCH:
            e.new_epoch()
        self._deps(e, reads, writes, dwrites)
        if inc:
            e.cnt += 1
            tok = (e.sem, e.cnt)
            sem = self.sems[e.sem]
            e.ops.append(lambda h, fn=fn, sem=sem: fn(h).then_inc(sem, 1))
            e.last_tok = tok
            e.pending = False
        else:
            tok = (e.sem, e.cnt + 1)
            e.ops.append(lambda h, fn=fn: fn(h))
            e.pending = True
        self._mark(tok, reads, writes, dwrites)
        return tok

    def dma(self, q, out, in_, reads=(), writes=(), slow=False, dwrites=()):
        e = self.eng[q]
        if q not in self.lanes:
            self.lanes[q] = [{"sem": None, "val": 0} for _ in range(self.n_lanes)]
            self.lane_rr[q] = 0
        ln = self.lanes[q][self.lane_rr[q] % self.n_lanes]
        self.lane_rr[q] += 1
        if ln["sem"] is None or ln["val"] >= 60000:
            ln["sem"] = self.new_sem(f"dma_{q}")
            ln["val"] = 0
        else:
            self._wait(e, (ln["sem"], ln["val"]))
        self._deps(e, reads, writes, dwrites)
        ln["val"] += 16
        tok = (ln["sem"], ln["val"])
        sem = self.sems[ln["sem"]]
        if slow:
            e.ops.append(lambda h, out=out, in_=in_, sem=sem:
                         h.dma_start(out=out, in_=in_, allow_slow_non_contiguous=True).then_inc(sem, 16))
        else:
            e.ops.append(lambda h, out=out, in_=in_, sem=sem: h.dma_start(out=out, in_=in_).then_inc(sem, 16))
        self._mark(tok, reads, writes, dwrites)
        return tok

    def all_tokens(self):
        toks = []
        for e in self.eng.values():
            assert not e.pending, f"engine {e.name} has a trailing non-inc op"
            if e.last_tok is not None:
                toks.append(e.last_tok)
        for lanes in self.lanes.values():
            for ln in lanes:
                if ln["sem"] is not None and ln["val"] > 0:
                    toks.append((ln["sem"], ln["val"]))
        return toks

    def barrier(self, engines=None):
        toks = self.all_tokens()
        for name, e in self.eng.items():
            if engines is not None and name not in engines:
                continue
            for t in toks:
                if e.sem is not None and t[0] == e.sem:
                    continue
                self._wait(e, t)

    def run(self, stack, build):
        self._stack = stack
        build(self)
        self.barrier()
        block = stack.enter_context(self.nc.Block())
        e = self.eng

        @block.sync
        def _(h):
            for f in e["sp"].ops:
                f(h)

        @block.tensor
        def _(h):
            for f in e["pe"].ops:
                f(h)

        @block.scalar
        def _(h):
            for f in e["act"].ops:
                f(h)

        @block.vector
        def _(h):
            for f in e["dve"].ops:
                f(h)

        @block.gpsimd
        def _(h):
            for f in e["pool"].ops:
                f(h)


class Arena:
    def __init__(self, handle, words):
        self.h = handle
        self.words = words
        self.top = 0

    def alloc(self, shape, dt, name="t"):
        free = int(np.prod(shape[1:]))
        nbytes = free * (2 if dt == BF16 else 4)
        w = (nbytes + 3) // 4
        w = (w + 7) // 8 * 8
        assert self.top + w <= self.words, f"arena overflow: {name} {shape} top={self.top} w={w}"
        ap = self.h[:, self.top:self.top + w]
        self.top += w
        if dt == BF16:
            ap = ap.bitcast(BF16)[:, 0:free]
        else:
            ap = ap[:, 0:free]
        if shape[0] < 128:
            ap = ap[0:shape[0], :]
        if len(shape) > 2:
            names = " ".join(f"d{i}" for i in range(len(shape) - 1))
            kw = {f"d{i}": shape[i + 1] for i in range(len(shape) - 1)}
            ap = ap.rearrange(f"p ({names}) -> p {names}", **kw)
        return ap, Buf(name)


def fft_consts():
    n1 = np.arange(128)[:, None]
    k1 = np.arange(64)[None, :]
    th = 2 * np.pi * n1 * (k1 + 0.5) / 128
    FA = np.concatenate([np.cos(th), -np.sin(th)], axis=1)
    n2 = np.arange(64)[:, None, None]
    k1b = np.arange(64)[None, :, None]
    k2 = np.arange(64)[None, None, :]
    ph = 2 * np.pi * (n2 * (k1b + 0.5) / 8192 + n2 * k2 / 64)
    wr = np.cos(ph)
    wi = -np.sin(ph)

    def mk(a0, a1, b0, b1):
        M = np.zeros((64, 64, 2, 128))
        M[:, :, 0, :64] = a0
        M[:, :, 0, 64:] = a1
        M[:, :, 1, :64] = b0
        M[:, :, 1, 64:] = b1
        M = M.reshape(64, -1)
        return np.concatenate([M, M], axis=0)

    FB = mk(wr, wi, -wi, wr)
    FBG1 = mk(wr, wr, -wi, -wi)
    FBG2 = mk(wi, wi, wr, wr)
    k2c = np.arange(64)[:, None]
    n2c = np.arange(64)[None, :]
    t = 2 * np.pi * n2c * k2c / 64
    c, s = np.cos(t), np.sin(t)
    FinvB1 = np.block([[c, s], [-s, c]])
    FinvB2 = np.block([[-s, c], [-c, -s]])
    k1a = np.arange(64)[:, None, None]
    n2a = np.arange(64)[None, :, None]
    n1a = np.arange(64)[None, None, :]
    phi = 2 * np.pi * (k1a + 0.5) * (64 * n1a + n2a) / 8192
    FinvA = np.zeros((64, 64, 2, 64))
    FinvA[:, :, 0, :] = (2.0 / NFFT) * np.cos(phi)
    FinvA[:, :, 1, :] = -(2.0 / NFFT) * np.sin(phi)
    FinvA = FinvA.reshape(64, -1)
    FinvA = np.concatenate([FinvA, FinvA], axis=0)
    bf = lambda a: np.ascontiguousarray(a.astype(np.float32).astype(NPBF))
    return dict(FA=bf(FA), FB=bf(FB), FBG1=bf(FBG1), FBG2=bf(FBG2), FinvB1=bf(FinvB1),
                FinvB2=bf(FinvB2), FinvA=bf(FinvA))


def filter_slot_tables(L, d):
    m = np.arange(NFFT)
    lam = np.where(m < T, d * T + m, d * T - (NFFT - m)).astype(np.int64)
    sign = np.where(m < T, 1.0, -1.0)
    sign[T] = 0.0
    tpos = np.abs(lam)
    valid = tpos < L
    sign = sign * valid
    tpos = np.minimum(tpos, L - 1)
    fwd = lam >= 0
    f32 = np.float32
    tl = np.linspace(0.0, 1.0, L, dtype=f32)
    bands = 16
    w = (f32(2.0 * math.pi) * np.arange(L, dtype=f32) / f32(L)).astype(f32)
    fr = np.linspace(1e-4, bands - 1, bands, dtype=f32)
    ang = (fr[None, :] * w[:, None]).astype(f32)
    z = np.concatenate([tl[:, None], np.cos(ang), -np.sin(ang)], axis=1).astype(f32)
    zT = np.ascontiguousarray(z[tpos].T.astype(f32))
    sel = np.zeros((128, NFFT), np.float32)
    sel[:64] = (sign * fwd)[None, :]
    sel[64:] = (sign * (~fwd))[None, :]
    ntl = (-tl[tpos]).astype(f32).reshape(128, 64)
    flag = 1.0 if d == 0 else 0.0
    return zT, sel.astype(NPBF), np.ascontiguousarray(ntl), flag


def decay_deltas():
    f32 = np.float32
    max_decay = math.log(1e-2) / 0.3
    min_decay = math.log(1e-2) / 1.5
    return np.abs(np.linspace(min_decay, max_decay, NCH, dtype=f32)).astype(f32)


def attn_tables(left_ok, right_ok):
    slopes = np.array([2.0 ** (-8.0 * (i + 1) / 8) for i in range(8)], np.float64)
    p = np.arange(128)[:, None]
    col = np.arange(256)[None, :]
    rel = np.abs(col - 64 - p)
    E = np.zeros((128, 3, 8, 256), np.float32)
    EL = np.zeros((64, 3, 8, 64), np.float32)
    ER = np.zeros((64, 3, 8, 64), np.float32)
    pk = np.arange(64)[:, None]
    q = np.arange(64)[None, :]
    relL = q + 64 - pk
    relR = pk + 64 - q
    for pi, (_, r) in enumerate(PATTERNS):
        for h in range(8):
            E[:, pi, h, :] = np.where(rel <= 64, np.exp(-slopes[h] * r * rel), 0.0)
            if left_ok:
                EL[:, pi, h, :] = np.where(relL <= 64, np.exp(-slopes[h] * r * relL), 0.0)
            if right_ok:
                ER[:, pi, h, :] = np.where(relR <= 64, np.exp(-slopes[h] * r * relR), 0.0)
    ELR = np.zeros((128, 3, 8, 64), np.float32)
    bf = lambda a: np.ascontiguousarray(a.astype(NPBF))
    return bf(E.reshape(128, -1)), bf(EL.reshape(64, -1)), bf(ER.reshape(64, -1))


class Prog:
    def __init__(self, nc, st, phases, ext_in=(), ext_out=()):
        self.ext_in = set(ext_in)
        self.ext_out = set(ext_out)
        self.nc = nc
        self.st = st
        self.phases = phases
        self.d = {}
        self.ps_i = 0
        self.ev_i = 0
        self.fresh = {}
        self.pools = {"acc": [0, 1], "tmp": [2, 3, 4, 5, 6, 7]}
        self.pool_i = {}

    def inp(self, name, shape, dt=F32):
        self.d[name] = self.nc.dram_tensor(name, list(shape), dt, kind="ExternalInput").ap()
        return self.d[name]

    def outp(self, name, shape, dt=F32):
        self.d[name] = self.nc.dram_tensor(name, list(shape), dt, kind="ExternalOutput").ap()
        return self.d[name]

    def scr(self, name, shape, dt=BF16):
        kind = "Internal"
        if name in self.ext_in:
            kind = "ExternalInput"
        if name in self.ext_out:
            kind = "ExternalOutput"
        self.d[name] = self.nc.dram_tensor(name, list(shape), dt, kind=kind).ap()
        return self.d[name]

    def ps(self, pool=None):
        if pool is None:
            i = self.ps_i % 8
            self.ps_i += 1
        else:
            lst = self.pools[pool]
            k = self.pool_i.get(pool, 0)
            self.pool_i[pool] = k + 1
            i = lst[k % len(lst)]
        self.fresh[self.bankb[i]] = True
        return self.bank[i], self.bankb[i]

    def evac(self, out, in_, reads, writes, eng=None, disjoint=True):
        kb = self.kb
        if eng is None:
            eng = ("act", "dve")[self.ev_i % 2]
            self.ev_i += 1
        w, dw = ((), writes) if disjoint else (writes, ())
        if eng == "act":
            kb.op("act", lambda e: e.activation(out=out, in_=in_, func=AF.Copy), reads, w, dwrites=dw)
        else:
            kb.op(eng, lambda e: e.tensor_copy(out=out, in_=in_), reads, w, dwrites=dw)

    def mm(self, out, lhsT, rhs, start, stop, reads, writes, inc):
        bb = writes[0]
        start = self.fresh.get(bb, False)
        self.fresh[bb] = False
        self.kb.op("pe", lambda e: e.matmul(out, lhsT=lhsT, rhs=rhs, start=start, stop=stop, skip_group_check=True),
                   reads, writes, inc=inc)

    def tr(self, out, in_, ident, reads, writes, inc):
        self.kb.op("pe", lambda e: e.transpose(out, in_, ident), reads, writes, inc=inc)

    def reset(self):
        self.kb.barrier()
        self.hiwater = max(getattr(self, "hiwater", 0), self.A.top)
        self.A.top = self.A_base

    def declare(self):
        inp, scr = self.inp, self.scr
        inp("xp", [SEQ, D_MODEL]); inp("xsf", [DEC_SEQ, D_MODEL]); inp("xso", [EXT, D_MODEL])
        inp("w_in", [D_MODEL, 4096]); inp("w_out", [D_MODEL, D_MODEL])
        inp("conv_w", [3, 1536]); inp("conv_b", [1536])
        inp("filt_w1", [33, 64]); inp("filt_b1", [64]); inp("filt_w2", [64, 64]); inp("filt_b2", [64])
        inp("filt_w3", [64, 64]); inp("filt_b3", [64]); inp("filt_freq", [3, 64]); inp("filt_w4", [64, 2048])
        inp("hyena_d", [2, NCH]); inp("attn_norm_g", [512]); inp("hyena_norm_g", [512])
        inp("ln_g", [D_MODEL]); inp("ln_b", [D_MODEL])
        inp("ident", [128, 128], BF16); inp("ones", [128, 128], BF16); inp("onesz", [128, 256], BF16)
        inp("FA", [128, 128], BF16); inp("FB", [128, 64 * 256], BF16)
        inp("FBG1", [128, 64 * 256], BF16); inp("FBG2", [128, 64 * 256], BF16)
        inp("FinvB1", [128, 128], BF16); inp("FinvB2", [128, 128], BF16); inp("FinvA", [128, 64 * 128], BF16)
        inp("f_zT", [12, 33, NFFT]); inp("f_sel", [12, 128, NFFT], BF16); inp("f_ntl", [12, 128, 64])
        inp("f_flag", [1, 12]); inp("delta", [NCH])
        inp("E_P", [128, 3 * 8 * 256], BF16)
        inp("EL_S", [64, 3 * 8 * 64], BF16); inp("ER_S", [64, 3 * 8 * 64], BF16)
        self.outp("yp", [SEQ, D_MODEL]); self.outp("ys", [T, D_MODEL])
        scr("xT_P", [8, 128, SEQ]); scr("xT_SF", [8, 128, DEC_SEQ]); scr("xT_SO", [8, 128, EXT])
        for nm, nb in (("P", 1), ("S", 4)):
            scr(f"UT0_{nm}", [nb, 64, NCH * 64]); scr(f"X1T_{nm}", [nb, 64, NCH * 64])
            scr(f"X2T_{nm}", [64, NCH * 64]); scr(f"Z1T_{nm}", [nb, 64, NCH * 64]); scr(f"Z2T_{nm}", [64, NCH * 64])
            scr(f"XS_{nm}", [nb, 128, NCH * 64])
            scr(f"MA_{nm}", [128, 4 * T]); scr(f"MH_{nm}", [128, 4 * T])
        scr("HAUG", [12, 128, NFFT])
        scr("G_P", [2, 2, 128, NCH * 64])
        scr("G_S1", [7, 2, 128, NCH * 64])
        scr("G_S2", [4, 2, 128, NCH * 64])

    def setup(self):
        A, kb, d = self.A, self.kb, self.d
        self.ident, self.identb = A.alloc([128, 128], BF16, "ident")
        kb.dma("sp", self.ident, d["ident"], writes=[self.identb])
        self.ones, self.onesb = A.alloc([128, 128], BF16, "ones")
        kb.dma("sp", self.ones, d["ones"], writes=[self.onesb])
        self.A_base = A.top

    def phase_xt(self, xname, xTname, ntok):
        A, kb, d = self.A, self.kb, self.d
        x, xT = d[xname], d[xTname]
        x32 = [A.alloc([128, 1024], F32, f"x32_{i}") for i in range(4)]
        xb = [A.alloc([128, 1024], BF16, f"xb_{i}") for i in range(4)]
        xT4 = [A.alloc([128, 8, 512], BF16, f"xT4_{i}") for i in range(2)]
        for t in range(ntok // 128):
            a32, b32 = x32[t % 4]
            ab, bb = xb[t % 4]
            kb.dma("sp", a32, x[t * 128:(t + 1) * 128, :], writes=[b32])
            kb.op(("pool", "dve")[t % 2], lambda e, o=ab, i=a32: e.tensor_copy(out=o, in_=i), [b32], [bb])
            bank, bankb = self.ps()
            bbf = bank.bitcast(BF16)
            for kc in range(8):
                self.tr(bbf[:, kc * 128:(kc + 1) * 128], ab[:, kc * 128:(kc + 1) * 128], self.ident,
                        [bb, self.identb], [bankb], inc=(kc == 7))
            g = t // 4
            q = t % 4
            a4, b4 = xT4[g % 2]
            self.evac(a4[:, :, q * 128:(q + 1) * 128], bbf.rearrange("p (k t) -> p k t", k=8), [bankb], [b4], eng="act")
            if q == 3:
                kb.dma("act", xT[:, :, g * 512:(g + 1) * 512].rearrange("k p t -> p k t"), a4, reads=[b4])

    def phase_h1(self, xTname, tok0, zero_l, zero_r, groups):
        A, kb, d = self.A, self.kb, self.d
        xT = d[xTname]
        xblk, xblkb = A.alloc([128, 8, T + 2], BF16, "xblk")
        src = xT[:, :, tok0:tok0 + T].rearrange("k p t -> p k t")
        kb.dma("sp", xblk[:, :, 1:T + 1], src, writes=[xblkb])
        if zero_l:
            kb.op("pool", lambda e: e.memset(xblk[:, :, 0:1], 0.0), [], [xblkb])
        else:
            kb.dma("sp", xblk[:, :, 0:1], xT[:, :, tok0 - 1:tok0].rearrange("k p t -> p k t"), writes=[xblkb], slow=True)
        if zero_r:
            kb.op("pool", lambda e: e.memset(xblk[:, :, T + 1:T + 2], 0.0), [], [xblkb])
        else:
            kb.dma("sp", xblk[:, :, T + 1:T + 2], xT[:, :, tok0 + T:tok0 + T + 1].rearrange("k p t -> p k t"),
                   writes=[xblkb], slow=True)
        w32 = [A.alloc([128, 8, 128], F32, f"w32_{i}") for i in range(3)]
        wb = [A.alloc([128, 8, 128], BF16, f"wb_{i}") for i in range(3)]
        cw = [A.alloc([128, 4], F32, f"cw_{i}") for i in range(3)]
        pfs = [A.alloc([128, T + 2], BF16, f"pf{i}") for i in range(2)]
        ufs = [A.alloc([128, T], F32, f"uf{i}") for i in range(2)]
        ubs = [A.alloc([128, T], BF16, f"ub{i}") for i in range(2)]
        utg = [A.alloc([64, 128, 64], BF16, f"utg_{i}") for i in range(2)]
        tps_ = [A.alloc([128, T], BF16, f"tp{i}") for i in range(2)]
        def load_w(gi):
            if gi >= len(groups):
                return
            co, cc, dst, dg = groups[gi]
            a32, b32 = w32[gi % 3]
            awb, bwb = wb[gi % 3]
            acw, bcw = cw[gi % 3]
            kb.dma("sp", a32, d["w_in"][:, co:co + 128].rearrange("(k p) c -> p k c", p=128), writes=[b32])
            kb.op("pool", lambda e, o=awb, i=a32: e.tensor_copy(out=o, in_=i), [b32], [bwb])
            kb.dma("sp", acw[:, 0:3], d["conv_w"][:, cc:cc + 128].rearrange("j c -> c j"), writes=[bcw], slow=True)
            kb.dma("sp", acw[:, 3:4], d["conv_b"][cc:cc + 128].rearrange("(c o) -> c o", o=1), writes=[bcw], slow=True)

        def stage1(gi):
            co, cc, dst, dg = groups[gi]
            pf, pfb = pfs[gi % 2]
            uf, ufb = ufs[gi % 2]
            ub, ubb = ubs[gi % 2]
            tp, tpb = tps_[gi % 2]
            awb, bwb = wb[gi % 3]
            acw, bcw = cw[gi % 3]
            load_w(gi + 1)
            nchunk = (T + 2 + 511) // 512
            for ch in range(nchunk):
                c0 = ch * 512
                n = min(512, T + 2 - c0)
                bank, bankb = self.ps()
                for kc in range(8):
                    self.mm(bank[:, 0:n], awb[:, kc, :], xblk[:, kc, c0:c0 + n], kc == 0, kc == 7,
                            [bwb, xblkb], [bankb], inc=(kc == 7))
                self.evac(pf[:, c0:c0 + n], bank[:, 0:n], [bankb], [pfb])
            kb.op("act", lambda e: e.activation(out=uf, in_=pf[:, 1:T + 1], func=AF.Identity,
                                                bias=acw[:, 3:4], scale=acw[:, 1:2]), [pfb, bcw], [ufb])
            kb.op("dve", lambda e: e.scalar_tensor_tensor(out=uf, in0=pf[:, 0:T], scalar=acw[:, 0:1], in1=uf,
                                                          op0=ALU.mult, op1=ALU.add), [pfb, bcw, ufb], [ufb])
            kb.op("pool", lambda e: e.tensor_scalar(out=tp, in0=pf[:, 2:T + 2], scalar1=acw[:, 2:3], scalar2=None,
                                                    op0=ALU.mult), [pfb, bcw], [tpb])
            kb.op("pool", lambda e: e.tensor_tensor(out=ub, in0=tp, in1=uf, op=ALU.add), [tpb, ufb], [ubb])

        def stage2(gi):
            co, cc, dst, dg = groups[gi]
            ub, ubb = ubs[gi % 2]
            ubv = ub.rearrange("p (a b) -> p a b", b=64)
            autg, butg = utg[gi % 2]
            for ng in range(8):
                bank, bankb = self.ps()
                bbf = bank.bitcast(BF16)
                for nn in range(8):
                    n2 = ng * 8 + nn
                    self.tr(bbf[0:64, nn * 128:(nn + 1) * 128], ubv[:, :, n2], self.ident,
                            [ubb, self.identb], [bankb], inc=(nn == 7))
                self.evac(autg[:, :, ng * 8:(ng + 1) * 8], bbf[0:64, :].rearrange("p (n c) -> p c n", n=8),
                          [bankb], [butg])
            kb.dma("act", dst[:, dg * 8192:(dg + 1) * 8192], autg.rearrange("p c n -> p (c n)"), reads=[butg])

        ng_ = len(groups)
        load_w(0)
        for gi in range(ng_ + 1):
            if gi < ng_:
                stage1(gi)
            if gi >= 1:
                stage2(gi - 1)

    def fwd_cg(self, ut, utb, K, FA, FAb, variants, ast, astb, emit):
        astv = ast.rearrange("p (c r k) -> p c r k", r=2, k=64)
        for g in range(16):
            bank, bankb = self.ps()
            for q in range(4):
                cp = g * 4 + q
                self.mm(bank[:, q * 128:(q + 1) * 128], ut[:, cp * 128:(cp + 1) * 128], FA[0:K, :], True, True,
                        [utb, FAb], [bankb], inc=(q == 3))
            self.evac(ast[:, g * 512:(g + 1) * 512], bank, [bankb], [astb])
        for v, (FBt, FBb) in enumerate(variants):
            def fill(Xv, Xb, FBt=FBt, FBb=FBb):
                for kg in range(8):
                    b0, b0b = self.ps()
                    b1, b1b = self.ps()
                    for kk in range(8):
                        k1 = kg * 8 + kk
                        for par, (bk, bkb) in enumerate(((b0, b0b), (b1, b1b))):
                            for ri in range(2):
                                self.mm(bk[:, kk * 64:(kk + 1) * 64], FBt[par * 64:(par + 1) * 64, k1, ri, :],
                                        astv[par * 64:(par + 1) * 64, :, ri, k1], ri == 0, ri == 1,
                                        [FBb, astb], [bkb], inc=(kk == 7 and ri == 1))
                    for par, (bk, bkb) in enumerate(((b0, b0b), (b1, b1b))):
                        self.evac(Xv[:, :, par, kg * 8:(kg + 1) * 8], bk.rearrange("p (k c) -> p c k", k=8), [bkb], [Xb])
            emit(v, fill)

    def phase_hx(self, srcs, dsts):
        A, kb, d = self.A, self.kb, self.d
        FA, FAb = A.alloc([128, 128], BF16, "FA")
        kb.dma("sp", FA, d["FA"], writes=[FAb])
        FB, FBb = A.alloc([128, 64, 2, 128], BF16, "FB")
        kb.dma("sp", FB.rearrange("p a b c -> p (a b c)"), d["FB"], writes=[FBb])
        uts = [A.alloc([64, 8192], BF16, f"ut_{i}") for i in range(2)]
        asts = [A.alloc([128, 8192], BF16, f"ast_{i}") for i in range(2)]
        xts = [A.alloc([128, 64, 2, 64], BF16, f"xt_{i}") for i in range(2)]
        it = 0
        for src, dst in zip(srcs, dsts):
            for cg in range(4):
                ut, utb = uts[it % 2]
                ast, astb = asts[it % 2]
                xt, xtb = xts[it % 2]
                it += 1
                kb.dma("sp", ut, src[:, cg * 8192:(cg + 1) * 8192], writes=[utb])

                def emit(v, fill, xt=xt, xtb=xtb, dst=dst, cg=cg):
                    fill(xt, xtb)
                    kb.dma("act", dst[:, cg * 8192:(cg + 1) * 8192], xt.rearrange("p a b c -> p (a b c)"), reads=[xtb])
                self.fwd_cg(ut, utb, 64, FA, FAb, [(FB, FBb)], ast, astb, emit)

    def phase_hmlp(self, slot_ids):
        A, kb, d = self.A, self.kb, self.d
        inv2pi = 1.0 / TWO_PI
        fq, fqb = A.alloc([128, 3, 64], F32, "fq")
        kb.dma("sp", fq.rearrange("p a b -> p (a b)"), d["filt_freq"].rearrange("a b -> (a b)").partition_broadcast(128),
               writes=[fqb])
        fqc, fqcb = A.alloc([128, 3], F32, "fqc")
        for half in range(2):
            kb.dma("sp", fqc[half * 64:(half + 1) * 64, :], d["filt_freq"].rearrange("a b -> b a"), writes=[fqcb], slow=True)
        w1p, w1pb = A.alloc([33, 64], F32, "w1p")
        kb.dma("sp", w1p, d["filt_w1"], writes=[w1pb])
        kb.op("dve", lambda e: e.scalar_tensor_tensor(out=w1p, in0=w1p, scalar=inv2pi, in1=fq[0:33, 0, :],
                                                       op0=ALU.mult, op1=ALU.mult), [w1pb, fqb], [w1pb])
        w2f, w2fb = A.alloc([64, 64], F32, "w2f")
        kb.dma("sp", w2f, d["filt_w2"], writes=[w2fb])
        w2p, w2pb = A.alloc([64, 64], BF16, "w2p")
        kb.op("dve", lambda e: e.scalar_tensor_tensor(out=w2p, in0=w2f, scalar=inv2pi, in1=fq[0:64, 1, :],
                                                       op0=ALU.mult, op1=ALU.mult), [w2fb, fqb], [w2pb])
        w3f, w3fb = A.alloc([64, 64], F32, "w3f")
        kb.dma("sp", w3f, d["filt_w3"], writes=[w3fb])
        w3p, w3pb = A.alloc([64, 2, 64], BF16, "w3p")
        for dup in range(2):
            kb.op("dve", lambda e, dup=dup: e.scalar_tensor_tensor(out=w3p[:, dup, :], in0=w3f, scalar=inv2pi,
                                                                    in1=fq[0:64, 2, :], op0=ALU.mult, op1=ALU.mult),
                  [w3fb, fqb], [w3pb])
        bp, bpb = A.alloc([128, 3], F32, "bp")
        for l, nm in enumerate(("filt_b1", "filt_b2", "filt_b3")):
            for half in range(2):
                kb.dma("sp", bp[half * 64:(half + 1) * 64, l:l + 1], d[nm].rearrange("(c o) -> c o", o=1), writes=[bpb], slow=True)
        kb.op("dve", lambda e: e.scalar_tensor_tensor(out=bp, in0=bp, scalar=inv2pi, in1=fqc, op0=ALU.mult, op1=ALU.mult),
              [bpb, fqcb], [bpb])
        haugs = [A.alloc([128, NFFT], BF16, f"haug{i}") for i in range(2)]
        GC = 8
        zts = [A.alloc([33, GC * 512], F32, f"zt_{i}") for i in range(2)]
        sels = [A.alloc([128, GC * 512], BF16, f"sel_{i}") for i in range(2)]
        uu = [A.alloc([128, 512], F32, f"uu_{i}") for i in range(GC)]
        vv = [A.alloc([128, 512], F32, f"vv_{i}") for i in range(GC)]
        hh = [[A.alloc([128, 512], BF16, f"hh_{l}_{i}") for i in range(GC)] for l in range(3)]
        gi = 0
        for si, slot in enumerate(slot_ids):
            haug, haugb = haugs[si % 2]
            for cgp in range(16 // GC):
                zt, ztb = zts[gi % 2]
                sl, slb = sels[gi % 2]
                gi += 1
                kb.dma("sp", zt, d["f_zT"][slot][:, cgp * GC * 512:(cgp + 1) * GC * 512], writes=[ztb])
                kb.dma("sp", sl, d["f_sel"][slot][:, cgp * GC * 512:(cgp + 1) * GC * 512], writes=[slb])
                prev = [(zt[:, c * 512:(c + 1) * 512], ztb) for c in range(GC)]
                for l in range(3):
                    P_ = 128 if l == 2 else 64
                    lhs, lhsb = ((w1p, w1pb), (w2p, w2pb), (w3p.rearrange("p a b -> p (a b)"), w3pb))[l]
                    new = []
                    for c in range(GC):
                        pin, pinb = prev[c]
                        bank, bankb = self.ps()
                        self.mm(bank[0:P_, :], lhs, pin, True, True, [lhsb, pinb], [bankb], inc=True)
                        u, ub_ = uu[c]
                        kb.op("act", lambda e, u=u, bank=bank, P_=P_, l=l: e.activation(
                            out=u[0:P_, :], in_=bank[0:P_, :], func=AF.Identity, bias=bp[0:P_, l:l + 1], scale=1.0),
                            [bankb, bpb], [ub_])
                    for c in range(GC):
                        u, ub_ = uu[c]
                        v_, vb_ = vv[c]
                        kb.op("dve", lambda e, u=u, v_=v_, P_=P_: e.scalar_tensor_tensor(
                            out=v_[0:P_, :], in0=u[0:P_, :], scalar=0.5, in1=u[0:P_, :], op0=ALU.is_gt, op1=ALU.subtract),
                            [ub_], [vb_])
                    for c in range(GC):
                        u, ub_ = uu[c]
                        v_, vb_ = vv[c]
                        kb.op("dve", lambda e, u=u, v_=v_, P_=P_: e.scalar_tensor_tensor(
                            out=v_[0:P_, :], in0=u[0:P_, :], scalar=-0.5, in1=v_[0:P_, :], op0=ALU.is_lt, op1=ALU.subtract),
                            [ub_, vb_], [vb_])
                    for c in range(GC):
                        v_, vb_ = vv[c]
                        h_, hb_ = hh[l][c]
                        kb.op("act", lambda e, h_=h_, v_=v_, P_=P_: e.activation(
                            out=h_[0:P_, :], in_=v_[0:P_, :], func=AF.Sin, scale=TWO_PI), [vb_], [hb_])
                        new.append((h_[0:P_, :], hb_))
                    prev = new
                for c in range(GC):
                    pin, pinb = prev[c]
                    ch = cgp * GC + c
                    kb.op("pool", lambda e, pin=pin, sl=sl, c=c, ch=ch, haug=haug: e.tensor_tensor(
                        out=haug[:, ch * 512:(ch + 1) * 512], in0=pin, in1=sl[:, c * 512:(c + 1) * 512], op=ALU.mult),
                        [pinb, slb], dwrites=[haugb])
            kb.dma("act", d["HAUG"][slot], haug, reads=[haugb])

    def phase_hf(self, slots):
        A, kb, d = self.A, self.kb, self.d
        FA, FAb = A.alloc([128, 128], BF16, "FA")
        kb.dma("sp", FA, d["FA"], writes=[FAb])
        FB1, FB1b = A.alloc([128, 64, 2, 128], BF16, "FBG1")
        kb.dma("sp", FB1.rearrange("p a b c -> p (a b c)"), d["FBG1"], writes=[FB1b])
        FB2, FB2b = A.alloc([128, 64, 2, 128], BF16, "FBG2")
        kb.dma("sp", FB2.rearrange("p a b c -> p (a b c)"), d["FBG2"], writes=[FB2b])
        FBs = ((FB1, FB1b), (FB2, FB2b))
        w4f, w4fb = A.alloc([128, 2, 512], F32, "w4f")
        w4v = d["filt_w4"].rearrange("h (o dd c) -> h o dd c", o=2, dd=2)
        for dirn in range(2):
            kb.dma("sp", w4f[dirn * 64:(dirn + 1) * 64, :, :], w4v[:, :, dirn, :], writes=[w4fb])
        w4t, w4tb = A.alloc([128, 2, 512], BF16, "w4t")
        kb.op("dve", lambda e: e.tensor_copy(out=w4t, in_=w4f), [w4fb], [w4tb])
        dl, dlb = A.alloc([128, NCH], F32, "dl")
        kb.dma("sp", dl, d["delta"].partition_broadcast(128), writes=[dlb])
        drow, drowb = A.alloc([1, 2, NCH], F32, "drow")
        kb.dma("sp", drow.rearrange("p a b -> p (a b)"), d["hyena_d"].rearrange("(x a) b -> x (a b)", x=1), writes=[drowb])
        flg, flgb = A.alloc([1, 12], F32, "flg")
        kb.dma("sp", flg, d["f_flag"], writes=[flgb])
        haugs = [A.alloc([128, NFFT], BF16, f"haug{i}") for i in range(2)]
        ntls = [A.alloc([128, 64], F32, f"ntl{i}") for i in range(2)]
        dec4 = [A.alloc([128, 4, 128], F32, f"dec_{i}") for i in range(2)]
        gts = [A.alloc([128, 128, 64], BF16, f"gt{i}") for i in range(2)]
        ast, astb = A.alloc([128, 8192], BF16, "ast")
        astv = ast.rearrange("p (c r k) -> p c r k", r=2, k=64)
        xts = [A.alloc([128, 64, 2, 64], BF16, f"xt_{i}") for i in range(2)]
        units = []
        for si, (slot, outs) in enumerate(slots):
            for (o, g1dst, g2dst) in outs:
                for cg in range(4):
                    units.append(dict(si=si, slot=slot, o=o, cg=cg, dst=(g1dst, g2dst)))
        loaded = set()

        def load_slot(si):
            if si in loaded or si >= len(slots):
                return
            loaded.add(si)
            slot = slots[si][0]
            kb.dma("sp", haugs[si % 2][0], d["HAUG"][slot], writes=[haugs[si % 2][1]])
            kb.dma("sp", ntls[si % 2][0], d["f_ntl"][slot], writes=[ntls[si % 2][1]])

        dci = [0]

        def fg_step(ui, ng):
            u = units[ui]
            if ng == 0:
                load_slot(u["si"])
                load_slot(u["si"] + 1)
            haug, haugb = haugs[u["si"] % 2]
            ntl, ntlb = ntls[u["si"] % 2]
            haugv = haug.rearrange("p (a b) -> p a b", b=64)
            gt, gtb = gts[ui % 2]
            o, cg = u["o"], u["cg"]
            bank, bankb = self.ps()
            dc, dcb = dec4[dci[0] % 2]
            dci[0] += 1
            for nn in range(4):
                n2 = ng * 4 + nn
                kb.op("act", lambda e, nn=nn, n2=n2: e.activation(
                    out=dc[:, nn, :], in_=dl[:, cg * 128:(cg + 1) * 128], func=AF.Exp,
                    scale=ntl[:, n2:n2 + 1]), [dlb, ntlb], [dcb])
                self.mm(bank[:, nn * 128:(nn + 1) * 128], haugv[:, :, n2], w4t[:, o, cg * 128:(cg + 1) * 128],
                        True, True, [haugb, w4tb], [bankb], inc=(nn == 3))
            kb.op("dve", lambda e: e.tensor_tensor(
                out=gt[:, :, ng * 4:(ng + 1) * 4], in0=bank.rearrange("p (n c) -> p c n", n=4),
                in1=dc.rearrange("p n c -> p c n"), op=ALU.mult), [bankb, dcb], dwrites=[gtb])
            if ng == 15:
                slot = u["slot"]
                kb.op("dve", lambda e: e.scalar_tensor_tensor(
                    out=gt[0:1, :, 0], in0=drow[0:1, o, cg * 128:(cg + 1) * 128], scalar=flg[0:1, slot:slot + 1],
                    in1=gt[0:1, :, 0], op0=ALU.mult, op1=ALU.add), [drowb, flgb, gtb], [gtb])

        def da(ui):
            gt, gtb = gts[ui % 2]
            ut = gt.rearrange("p c n -> p (c n)")
            for g in range(16):
                bank, bankb = self.ps()
                for q in range(4):
                    cp = g * 4 + q
                    self.mm(bank[:, q * 128:(q + 1) * 128], ut[:, cp * 128:(cp + 1) * 128], FA, True, True,
                            [gtb, FAb], [bankb], inc=(q == 3))
                self.evac(ast[:, g * 512:(g + 1) * 512], bank, [bankb], [astb])

        def mb_step(ui, step):
            u = units[ui]
            v, kg = step // 8, step % 8
            FBt, FBb = FBs[v]
            Xv, Xb = xts[(ui * 2 + v) % 2]
            b0, b0b = self.ps()
            b1, b1b = self.ps()
            for kk in range(8):
                k1 = kg * 8 + kk
                for par, (bk, bkb) in enumerate(((b0, b0b), (b1, b1b))):
                    for ri in range(2):
                        self.mm(bk[:, kk * 64:(kk + 1) * 64], FBt[par * 64:(par + 1) * 64, k1, ri, :],
                                astv[par * 64:(par + 1) * 64, :, ri, k1], ri == 0, ri == 1,
                                [FBb, astb], [bkb], inc=(kk == 7 and ri == 1))
            for par, (bk, bkb) in enumerate(((b0, b0b), (b1, b1b))):
                self.evac(Xv[:, :, par, kg * 8:(kg + 1) * 8], bk.rearrange("p (k c) -> p c k", k=8), [bkb], [Xb])
            if kg == 7:
                cg = u["cg"]
                kb.dma("act", u["dst"][v][:, cg * 8192:(cg + 1) * 8192], Xv.rearrange("p a b c -> p (a b c)"), reads=[Xb])

        n = len(units)
        for ng in range(16):
            fg_step(0, ng)
        da(0)
        for ui in range(n):
            for step in range(16):
                if ui + 1 < n:
                    fg_step(ui + 1, step)
                mb_step(ui, step)
            if ui + 1 < n:
                da(ui + 1)

    def phase_hp(self, outs, xs, nin):
        A, kb, d = self.A, self.kb, self.d
        FiB1, FiB1b = A.alloc([128, 128], BF16, "FiB1")
        kb.dma("sp", FiB1, d["FinvB1"], writes=[FiB1b])
        FiB2, FiB2b = A.alloc([128, 128], BF16, "FiB2")
        kb.dma("sp", FiB2, d["FinvB2"], writes=[FiB2b])
        FiA, FiAb = A.alloc([128, 64, 2, 64], BF16, "FiA")
        kb.dma("sp", FiA.rearrange("p a b c -> p (a b c)"), d["FinvA"], writes=[FiAb])
        NB = 3
        xres = [A.alloc([128, 8192], BF16, f"xres_{j}") for j in range(nin)]
        g1 = [A.alloc([128, 2048], BF16, f"g1_{i}") for i in range(NB)]
        g2 = [A.alloc([128, 2048], BF16, f"g2_{i}") for i in range(NB)]
        t1 = [A.alloc([128, 2048], BF16, f"t1_{i}") for i in range(2)]
        t2 = [A.alloc([128, 2048], BF16, f"t2_{i}") for i in range(2)]
        bsts = [A.alloc([128, 64, 2, 64], BF16, f"bst_{i}") for i in range(2)]
        xgs = [A.alloc([64, 64, 2, 64], BF16, f"xg_{i}") for i in range(1)]
        zts = [A.alloc([64, 64, 2, 64], BF16, f"zt_{i}") for i in range(2)]
        li = 0
        ti = 0
        ci = 0
        for cg in range(4):
            for j in range(nin):
                kb.dma("sp", xres[j][0], xs[j][:, cg * 8192:(cg + 1) * 8192], writes=[xres[j][1]])
            for (gl, gate, dst) in outs:
                bst, bstb = bsts[ci % 2]
                xg, xgb = xgs[0]
                zt, ztb = zts[ci % 2]
                ci += 1
                kb.dma("sp", xg.rearrange("p a b c -> p (a b c)"), gate[:, cg * 8192:(cg + 1) * 8192], writes=[xgb])
                bstf = bst.rearrange("p a b c -> p (a b c)")
                for cq in range(4):
                    c0 = (cg * 128 + cq * 32) * 64
                    banks = [self.ps() for _ in range(4)]
                    for j in range(nin):
                        ax, bx = xres[j][0][:, cq * 2048:(cq + 1) * 2048], xres[j][1]
                        a1, b1 = g1[li % NB]
                        a2, b2 = g2[li % NB]
                        li += 1
                        kb.dma("sp", a1, gl[j][0][:, c0:c0 + 2048], writes=[b1])
                        kb.dma("sp", a2, gl[j][1][:, c0:c0 + 2048], writes=[b2])
                        at1, bt1 = t1[ti % 2]
                        at2, bt2 = t2[ti % 2]
                        ti += 1
                        kb.op("dve", lambda e, o=at1, a=a1, b=ax: e.tensor_tensor(out=o, in0=a, in1=b, op=ALU.mult),
                              [b1, bx], [bt1])
                        kb.op("pool", lambda e, o=at2, a=a2, b=ax: e.tensor_tensor(out=o, in0=a, in1=b, op=ALU.mult),
                              [b2, bx], [bt2])
                        for q in range(16):
                            bk, bkb = banks[q // 4]
                            sl = (q % 4) * 128
                            self.mm(bk[:, sl:sl + 128], at1[:, q * 128:(q + 1) * 128], FiB1, j == 0, False,
                                    [bt1, FiB1b], [bkb], inc=False)
                            self.mm(bk[:, sl:sl + 128], at2[:, q * 128:(q + 1) * 128], FiB2, False, j == nin - 1,
                                    [bt2, FiB2b], [bkb], inc=(q % 4 == 3 or q == 15))
                    for b4 in range(4):
                        bk, bkb = banks[b4]
                        off = (cq * 16 + b4 * 4) * 128
                        self.evac(bstf[:, off:off + 512], bk, [bkb], [bstb], eng="act")
                for ng in range(8):
                    b0, b0b = self.ps()
                    b1_, b1b = self.ps()
                    for nn in range(8):
                        n2 = ng * 8 + nn
                        for par, (bk, bkb) in enumerate(((b0, b0b), (b1_, b1b))):
                            for ri in range(2):
                                self.mm(bk[0:64, nn * 64:(nn + 1) * 64], FiA[par * 64:(par + 1) * 64, n2, ri, :],
                                        bst[par * 64:(par + 1) * 64, :, ri, n2], ri == 0, ri == 1,
                                        [FiAb, bstb], [bkb], inc=(nn == 7 and ri == 1))
                    for par, (bk, bkb) in enumerate(((b0, b0b), (b1_, b1b))):
                        kb.op("dve", lambda e, bk=bk, par=par, ng=ng, zt=zt, xg=xg: e.tensor_tensor(
                            out=zt[:, :, par, ng * 8:(ng + 1) * 8], in0=bk[0:64, :].rearrange("p (n c) -> p c n", n=8),
                            in1=xg[:, :, par, ng * 8:(ng + 1) * 8], op=ALU.mult), [bkb, xgb], dwrites=[ztb])
                kb.dma("act", dst[:, cg * 8192:(cg + 1) * 8192], zt.rearrange("p a b c -> p (a b c)"), reads=[ztb])

    def inproj_fm(self, xTname, tok0, ntok, co, wst, xts, consume):
        A, kb, d = self.A, self.kb, self.d
        (a32, b32), (awb, bwb) = wst
        kb.dma("sp", a32, d["w_in"][:, co:co + 128].rearrange("(k p) c -> p k c", p=128), writes=[b32])
        kb.op("pool", lambda e: e.tensor_copy(out=awb, in_=a32), [b32], [bwb])
        xT = d[xTname]
        for ch in range(ntok // 512):
            ax, bx = xts[self.xi % len(xts)]
            self.xi += 1
            kb.dma("sp", ax, xT[:, :, tok0 + ch * 512: tok0 + (ch + 1) * 512].rearrange("k p t -> p k t"), writes=[bx])
            bank, bankb = self.ps()
            for kc in range(8):
                self.mm(bank, awb[:, kc, :], ax[:, kc, :], kc == 0, kc == 7, [bwb, bx], [bankb], inc=(kc == 7))
            consume(ch * 512, 512, bank, bankb)

    def rms_gate(self, val, valb, gate, gateb, gcol, gcolb):
        A, kb = self.A, self.kb
        sq = [A.alloc([128, 4, 512], BF16, f"sq_{i}") for i in range(2)]
        rs = [A.alloc([128, 512], F32, f"rs_{i}") for i in range(2)]
        tm = [A.alloc([128, 512], F32, f"tm_{i}") for i in range(4)]
        epsc, epsb = A.alloc([128, 1], F32, "epsc")
        kb.op("pool", lambda e: e.memset(epsc, RMS_EPS), [], [epsb])
        outb = []
        ti = 0
        for ch in range(T // 512):
            asq, bsq = sq[ch % 2]
            ars, brs = rs[ch % 2]
            cb_ = Buf(f"rmsch{ch}")
            outb.append(cb_)
            sl = slice(ch * 512, (ch + 1) * 512)
            kb.op("pool", lambda e, asq=asq, sl=sl: e.tensor_tensor(out=asq, in0=val[:, :, sl], in1=val[:, :, sl], op=ALU.mult),
                  [valb], [bsq])
            bank, bankb = self.ps()
            for hp in range(4):
                self.mm(bank, self.ones, asq[:, hp, :], hp == 0, hp == 3, [self.onesb, bsq], [bankb], inc=(hp == 3))
            kb.op("act", lambda e, ars=ars, bank=bank: e.activation(out=ars, in_=bank, func=AF.Sqrt, bias=epsc, scale=1.0 / 512.0),
                  [bankb, epsb], [brs])
            kb.op("dve", lambda e, ars=ars: e.reciprocal(out=ars, in_=ars), [brs], [brs])
            for hp in range(4):
                atm, btm = tm[ti % 4]
                ti += 1
                kb.op("dve", lambda e, atm=atm, hp=hp, sl=sl, ars=ars: e.scalar_tensor_tensor(
                    out=atm, in0=val[:, hp, sl], scalar=gcol[:, hp:hp + 1], in1=ars, op0=ALU.mult, op1=ALU.mult),
                    [valb, gcolb, brs], [btm])
                kb.op("pool", lambda e, atm=atm, hp=hp, sl=sl: e.tensor_tensor(
                    out=val[:, hp, sl], in0=atm, in1=gate[:, hp, sl], op=ALU.mult), [btm, gateb, bsq], [cb_])
        return outb

    def _att_job(self, g, r_):
        kb = self.kb
        st = {}

        def A_():
            kind, J, qlo, qhi, rho, pidx, hp = g["kind"], g["J"], g["qlo"], g["qhi"], g["rho"], g["pidx"], g["hp"]
            koff, n_r = g["koff"], g["n_r"]
            wq = qhi - qlo
            if kind == "main":
                nk, k0 = 128, koff + 128 * J
                col0 = qlo - (128 * J - 64)
                Et = r_["E"][:, pidx, 2 * hp:2 * hp + 2, col0:col0 + wq]
                Etb = r_["Eb"]
            elif kind == "left":
                nk, k0 = 64, koff - 64
                Et = r_["EL"][:, pidx, 2 * hp:2 * hp + 2, 0:wq]
                Etb = r_["ELb"]
            else:
                nk, k0 = 64, koff + n_r
                Et = r_["ER"][:, pidx, 2 * hp:2 * hp + 2, 0:wq]
                Etb = r_["ERb"]
            QTv, KTv, VTv = r_["QTv"], r_["KTv"], r_["VTv"]
            s0, s0b = self.ps("tmp")
            s1, s1b = self.ps("tmp")
            self.mm(s0[0:nk, 0:wq], KTv[0:64, k0:k0 + nk, rho], QTv[0:64, qlo:qhi, rho], True, True,
                    [r_["KTb"], r_["QTb"]], [s0b], inc=True)
            self.mm(s1[0:nk, 0:wq], KTv[64:128, k0:k0 + nk, rho], QTv[64:128, qlo:qhi, rho], True, True,
                    [r_["KTb"], r_["QTb"]], [s1b], inc=True)
            i = self.att_i
            self.att_i += 1
            ape, bpe = r_["pes"][i % 6]
            apm, bpm = r_["pms"][i % 6]
            avz, bvz = r_["vzs"][i % 6]
            kb.op("act", lambda e: e.activation(out=ape[0:nk, 0, 0:wq], in_=s0[0:nk, 0:wq], func=AF.Exp, scale=0.125),
                  [s0b], [bpe])
            kb.op("act", lambda e: e.activation(out=ape[0:nk, 1, 0:wq], in_=s1[0:nk, 0:wq], func=AF.Exp, scale=0.125),
                  [s1b], [bpe])
            kb.op("pool", lambda e: e.tensor_tensor(out=apm[0:nk, :, 0:wq], in0=ape[0:nk, :, 0:wq], in1=Et, op=ALU.mult),
                  [bpe, Etb], [bpm])
            tb, tbb = self.ps("tmp")
            tbf = tb.bitcast(BF16)
            self.tr(tbf[0:nk, 0:128], VTv[:, k0:k0 + nk, rho], self.ident, [r_["VTb"], self.identb], [tbb], inc=True)
            kb.op("dve", lambda e: e.tensor_copy(out=avz[0:nk, 0, 0:64], in_=tbf[0:nk, 0:64]), [tbb], [bvz])
            kb.op("dve", lambda e: e.tensor_copy(out=avz[0:nk, 1, 64:128], in_=tbf[0:nk, 64:128]), [tbb], [bvz])
            st.update(nk=nk, wq=wq, apm=apm, bpm=bpm, avz=avz, bvz=bvz)

        def B_():
            ch = g["chunk"]
            c0, nq, qlo, qhi, rho = g["c0"], g["nq"], g["qlo"], g["qhi"], g["rho"]
            if g["first"]:
                ch["num"] = self.ps("acc")
                ch["den"] = self.ps("acc")
            numb, numbb = ch["num"]
            denb, denbb = ch["den"]
            nk, wq, apm, bpm, avz, bvz = st["nk"], st["wq"], st["apm"], st["bpm"], st["avz"], st["bvz"]
            onz, onzb = r_["onz"], r_["onzb"]
            for hs in range(2):
                self.mm(numb[:, qlo - c0:qhi - c0], avz[0:nk, hs, :], apm[0:nk, hs, 0:wq], False, False,
                        [bvz, bpm], [numbb], inc=False)
                self.mm(denb[:, qlo - c0:qhi - c0], onz[0:nk, hs, :], apm[0:nk, hs, 0:wq], False, False,
                        [onzb, bpm], [denbb], inc=(hs == 1))
            if g["last"]:
                NUMv, DENv, NUMb, DENb = r_["NUMv"], r_["DENv"], r_["NUMb"], r_["DENb"]
                if g["first_pat"]:
                    kb.op("act", lambda e: e.activation(out=NUMv[:, c0:c0 + nq, rho], in_=numb[:, 0:nq], func=AF.Copy),
                          [numbb], [NUMb])
                    kb.op("dve", lambda e: e.tensor_copy(out=DENv[:, c0:c0 + nq, rho], in_=denb[:, 0:nq]),
                          [denbb], [DENb])
                else:
                    kb.op("dve", lambda e: e.tensor_tensor(out=NUMv[:, c0:c0 + nq, rho], in0=numb[:, 0:nq],
                                                            in1=NUMv[:, c0:c0 + nq, rho], op=ALU.add), [numbb, NUMb], [NUMb])
                    kb.op("dve", lambda e: e.tensor_tensor(out=DENv[:, c0:c0 + nq, rho], in0=denb[:, 0:nq],
                                                            in1=DENv[:, c0:c0 + nq, rho], op=ALU.add), [denbb, DENb], [DENb])
        return (A_, B_)

    def phase_att(self, xTname, off, next_, halo, Ename, ELname, ERname, MAname):
        A, kb, d = self.A, self.kb, self.d
        self.xi = 0
        E, Eb = A.alloc([128, 3, 8, 256], BF16, "E")
        kb.dma("sp", E.rearrange("p a b c -> p (a b c)"), d[Ename], writes=[Eb])
        if halo:
            EL, ELb = A.alloc([64, 3, 8, 64], BF16, "EL")
            kb.dma("sp", EL.rearrange("p a b c -> p (a b c)"), d[ELname], writes=[ELb])
            ER, ERb = A.alloc([64, 3, 8, 64], BF16, "ER")
            kb.dma("sp", ER.rearrange("p a b c -> p (a b c)"), d[ERname], writes=[ERb])
        onz, onzb = A.alloc([128, 2, 128], BF16, "onz")
        kb.dma("sp", onz.rearrange("p a b -> p (a b)"), d["onesz"], writes=[onzb])
        gcol, gcolb = A.alloc([128, 4], F32, "gcol")
        kb.dma("sp", gcol, d["attn_norm_g"].rearrange("(h p) -> p h", p=128), writes=[gcolb], slow=True)
        ATT, ATTb = A.alloc([128, 4, T], BF16, "ATT")
        GA, GAb = A.alloc([128, 4, T], BF16, "GA")
        QT, QTb = A.alloc([128, T], BF16, "QT")
        KT, KTb = A.alloc([128, next_], BF16, "KT")
        VT, VTb = A.alloc([128, next_], BF16, "VT")
        NUM, NUMb = A.alloc([128, T], F32, "NUM")
        DEN, DENb = A.alloc([128, T], F32, "DEN")
        w32 = A.alloc([128, 8, 128], F32, "w32")
        wbs = A.alloc([128, 8, 128], BF16, "wbs")
        xts = [A.alloc([128, 8, 512], BF16, f"xt_{i}") for i in range(2)]
        pes = [A.alloc([128, 2, 256], BF16, f"pe_{i}") for i in range(6)]
        pms = [A.alloc([128, 2, 256], BF16, f"pm_{i}") for i in range(6)]
        vzs = [A.alloc([128, 2, 128], BF16, f"vz_{i}") for i in range(6)]
        self.att_i = 0
        for (avz, bvz) in vzs:
            kb.op("pool", lambda e, avz=avz: e.memset(avz, 0.0), [], [bvz])
        for hp in range(4):
            def to_tile(dstt, dstb, base, fn=None):
                def consume(c0, n, bank, bankb):
                    if fn is None:
                        self.evac(dstt[:, base + c0: base + c0 + n], bank, [bankb], [dstb])
                    else:
                        kb.op("act", lambda e: e.activation(out=dstt[:, base + c0: base + c0 + n], in_=bank, func=fn),
                              [bankb], dwrites=[dstb])
                return consume
            self.inproj_fm(xTname, off, T, 0 + hp * 128, (w32, wbs), xts, to_tile(QT, QTb, 0))
            self.inproj_fm(xTname, 0, next_, 512 + hp * 128, (w32, wbs), xts, to_tile(KT, KTb, 0))
            self.inproj_fm(xTname, 0, next_, 1024 + hp * 128, (w32, wbs), xts, to_tile(VT, VTb, 0))
            self.inproj_fm(xTname, off, T, 1536 + hp * 128, (w32, wbs), xts, to_tile(GA[:, hp, :], GAb, 0, AF.Silu))
            jobs = []
            for pidx, (_, r) in enumerate(PATTERNS):
                n_r = T // r
                ntile = n_r // 128
                QTv = QT.rearrange("p (i r) -> p i r", r=r)
                KTv = KT.rearrange("p (i r) -> p i r", r=r)
                VTv = VT.rearrange("p (i r) -> p i r", r=r)
                NUMv = NUM.rearrange("p (i r) -> p i r", r=r)
                DENv = DEN.rearrange("p (i r) -> p i r", r=r)
                koff = off // r
                for rho in range(r):
                    for c0 in range(0, n_r, 512):
                        nq = min(512, n_r - c0)
                        segs = []
                        for J in range(c0 // 128 - 1, (c0 + nq) // 128 + 1):
                            if 0 <= J < ntile:
                                kind = "main"
                            elif halo and J == -1:
                                kind = "left"
                            elif halo and J == ntile:
                                kind = "right"
                            else:
                                continue
                            wlo, whi = 128 * J - 64, 128 * J + 192
                            if kind == "left":
                                wlo, whi = 0, 64
                            if kind == "right":
                                wlo, whi = n_r - 64, n_r
                            qlo, qhi = max(wlo, c0), min(whi, c0 + nq)
                            if qhi > qlo:
                                segs.append((kind, J, qlo, qhi))
                        cover = [0] * (nq // 64)
                        for (_, _, qlo, qhi) in segs:
                            for b_ in range((qlo - c0) // 64, (qhi - c0) // 64):
                                cover[b_] += 1
                        assert all(c_ > 0 for c_ in cover), cover
                        chunk = {}
                        for si, (kind, J, qlo, qhi) in enumerate(segs):
                            jobs.append(self._att_job(
                                dict(kind=kind, J=J, qlo=qlo, qhi=qhi, c0=c0, nq=nq, rho=rho, pidx=pidx, hp=hp,
                                     koff=koff, n_r=n_r, first=(si == 0), last=(si == len(segs) - 1),
                                     first_pat=(pidx == 0), chunk=chunk),
                                dict(QTv=QTv, KTv=KTv, VTv=VTv, NUMv=NUMv, DENv=DENv, QTb=QTb, KTb=KTb, VTb=VTb,
                                     NUMb=NUMb, DENb=DENb, E=E, Eb=Eb, EL=EL if halo else None, ELb=ELb if halo else None,
                                     ER=ER if halo else None, ERb=ERb if halo else None, onz=onz, onzb=onzb,
                                     pes=pes, pms=pms, vzs=vzs)))
            SK = 4
            for i in range(len(jobs) + SK):
                if i < len(jobs):
                    jobs[i][0]()
                if i - SK >= 0:
                    jobs[i - SK][1]()
            kb.op("dve", lambda e: e.reciprocal(out=DEN, in_=DEN), [DENb], [DENb])
            kb.op("dve", lambda e, hp=hp: e.tensor_tensor(out=ATT[:, hp, :], in0=NUM, in1=DEN, op=ALU.mult),
                  [NUMb, DENb], dwrites=[ATTb])
        ob = self.rms_gate(ATT, ATTb, GA, GAb, gcol, gcolb)
        kb.dma("act", d[MAname], ATT.rearrange("p a b -> p (a b)"), reads=[ATTb] + ob)

    def phase_mh(self, xTname, off, Z2name, MHname):
        A, kb, d = self.A, self.kb, self.d
        self.xi = 0
        ZF, ZFb = A.alloc([128, 4, T], BF16, "ZF")
        GH, GHb = A.alloc([128, 4, T], BF16, "GH")
        gcol, gcolb = A.alloc([128, 4], F32, "gcolh")
        kb.dma("sp", gcol, d["hyena_norm_g"].rearrange("(h p) -> p h", p=128), writes=[gcolb], slow=True)
        w32 = A.alloc([128, 8, 128], F32, "w32")
        wbs = A.alloc([128, 8, 128], BF16, "wbs")
        xts = [A.alloc([128, 8, 512], BF16, f"xt_{i}") for i in range(2)]
        zts = [A.alloc([64, 128, 64], BF16, f"ztm_{i}") for i in range(2)]
        ZFv = ZF.rearrange("p a (n m) -> p a n m", m=64)
        for cb in range(4):
            zt, ztb = zts[cb % 2]
            kb.dma("sp", zt.rearrange("p a b -> p (a b)"), d[Z2name][:, cb * 8192:(cb + 1) * 8192], writes=[ztb])
            for ng in range(4):
                bank, bankb = self.ps()
                bbf = bank.bitcast(BF16)
                for nn in range(16):
                    n2 = ng * 16 + nn
                    self.tr(bbf[:, nn * 64:(nn + 1) * 64], zt[:, :, n2], self.ident[0:64, 0:64], [ztb, self.identb], [bankb],
                            inc=(nn == 15))
                self.evac(ZFv[:, cb, :, ng * 16:(ng + 1) * 16], bbf.rearrange("p (n m) -> p m n", n=16), [bankb], [ZFb])

            def consume(c0, n, bank, bankb, cb=cb):
                kb.op("act", lambda e: e.activation(out=GH[:, cb, c0:c0 + n], in_=bank, func=AF.Silu), [bankb], dwrites=[GHb])
            self.inproj_fm(xTname, off, T, 3584 + cb * 128, (w32, wbs), xts, consume)
        ob = self.rms_gate(ZF, ZFb, GH, GHb, gcol, gcolb)
        kb.dma("act", d[MHname], ZF.rearrange("p a b -> p (a b)"), reads=[ZFb] + ob)

    def phase_out(self, xname, xrow0, MAname, MHname, yname):
        A, kb, d = self.A, self.kb, self.d
        MA, MAb = A.alloc([128, 4, T], BF16, "MA")
        kb.dma("sp", MA.rearrange("p a b -> p (a b)"), d[MAname], writes=[MAb])
        ZF, ZFb = A.alloc([128, 4, T], BF16, "MH")
        kb.dma("sp", ZF.rearrange("p a b -> p (a b)"), d[MHname], writes=[ZFb])
        wo, wob = A.alloc([128, 8, 1024], BF16, "wo")
        wst = [A.alloc([128, 1024], F32, f"wst_{i}") for i in range(2)]
        for kc in range(8):
            a, b = wst[kc % 2]
            kb.dma("sp", a, d["w_out"][kc * 128:(kc + 1) * 128, :], writes=[b])
            kb.op("pool", lambda e, a=a, kc=kc: e.tensor_copy(out=wo[:, kc, :], in_=a), [b], [wob])
        lg, lgb = A.alloc([128, 1024], F32, "lg")
        kb.dma("sp", lg, d["ln_g"].partition_broadcast(128), writes=[lgb])
        lb, lbb = A.alloc([128, 1024], F32, "lb")
        kb.dma("sp", lb, d["ln_b"].partition_broadcast(128), writes=[lbb])
        epsc, epsb = A.alloc([128, 1], F32, "epsl")
        kb.op("pool", lambda e: e.memset(epsc, LN_EPS), [], [epsb])
        xin = [A.alloc([128, 1024], F32, f"xin_{i}") for i in range(4)]
        hs = [A.alloc([128, 1024], F32, f"h_{i}") for i in range(2)]
        ys = [A.alloc([128, 1024], F32, f"y_{i}") for i in range(2)]
        sts = [A.alloc([128, 2, 6], F32, f"st_{i}") for i in range(2)]
        mvs = [A.alloc([128, 2], F32, f"mv_{i}") for i in range(2)]
        x, y = d[xname], d[yname]

        def ldx(t):
            if t < T // 128:
                kb.dma("sp", xin[t % 4][0], x[xrow0 + t * 128: xrow0 + (t + 1) * 128, :], writes=[xin[t % 4][1]])
        ldx(0)
        ldx(1)
        for t in range(T // 128):
            ax, bx = xin[t % 4]
            ah, bh = hs[t % 2]
            ay, by = ys[t % 2]
            ast_, bst_ = sts[t % 2]
            amv, bmv = mvs[t % 2]
            ldx(t + 2)
            for half in range(2):
                bank, bankb = self.ps()
                for kc in range(8):
                    src, srcb = (MA, MAb) if kc < 4 else (ZF, ZFb)
                    self.mm(bank, src[:, kc % 4, t * 128:(t + 1) * 128], wo[:, kc, half * 512:(half + 1) * 512],
                            kc == 0, kc == 7, [srcb, wob], [bankb], inc=(kc == 7))
                kb.op("dve", lambda e, ah=ah, ax=ax, bank=bank, half=half: e.scalar_tensor_tensor(
                    out=ah[:, half * 512:(half + 1) * 512], in0=ax[:, half * 512:(half + 1) * 512], scalar=ALPHA, in1=bank,
                    op0=ALU.mult, op1=ALU.add), [bx, bankb], [bh])
                kb.op("dve", lambda e, ast_=ast_, ah=ah, half=half: e.bn_stats(out=ast_[:, half, :], in_=ah[:, half * 512:(half + 1) * 512]),
                      [bh], [bst_])
            kb.op("dve", lambda e, amv=amv, ast_=ast_: e.bn_aggr(out=amv, in_=ast_.rearrange("p a b -> p (a b)")), [bst_], [bmv])
            kb.op("act", lambda e, amv=amv: e.activation(out=amv[:, 1:2], in_=amv[:, 1:2], func=AF.Sqrt, bias=epsc, scale=1.0),
                  [bmv, epsb], [bmv])
            kb.op("dve", lambda e, amv=amv: e.reciprocal(out=amv[:, 1:2], in_=amv[:, 1:2]), [bmv], [bmv])
            kb.op("dve", lambda e, ay=ay, ah=ah, amv=amv: e.tensor_scalar(out=ay, in0=ah, scalar1=amv[:, 0:1], scalar2=amv[:, 1:2],
                                                                        op0=ALU.subtract, op1=ALU.mult), [bh, bmv], [by])
            kb.op("pool", lambda e, ay=ay: e.tensor_tensor(out=ay, in0=ay, in1=lg, op=ALU.mult), [by, lgb], [by])
            kb.op("pool", lambda e, ay=ay: e.tensor_tensor(out=ay, in0=ay, in1=lb, op=ALU.add), [by, lbb], [by])
            kb.dma("act", y[t * 128:(t + 1) * 128, :], ay, reads=[by])

    def build(self, kb):
        self.kb = kb
        d = self.d
        ph = self.phases
        self.setup()
        if "xtp" in ph:
            self.phase_xt("xp", "xT_P", SEQ); self.reset()
        if "xts" in ph:
            self.phase_xt("xsf", "xT_SF", DEC_SEQ); self.reset()
            self.phase_xt("xso", "xT_SO", EXT); self.reset()
        if "h1p" in ph:
            gP = []
            for k in range(4):
                gP.append((2048 + k * 128, k * 128, d["UT0_P"][0], k))
            for k in range(4):
                gP.append((2560 + k * 128, 512 + k * 128, d["X1T_P"][0], k))
            for k in range(4):
                gP.append((3072 + k * 128, 1024 + k * 128, d["X2T_P"], k))
            self.phase_h1("xT_P", 0, True, True, gP); self.reset()
        if "h1s" in ph:
            for b in range(4):
                g = []
                for k in range(4):
                    g.append((2048 + k * 128, k * 128, d["UT0_S"][b], k))
                for k in range(4):
                    g.append((2560 + k * 128, 512 + k * 128, d["X1T_S"][b], k))
                self.phase_h1("xT_SF", b * T, b == 0, b == 3, g); self.reset()
            g = [(3072 + k * 128, 1024 + k * 128, d["X2T_S"], k) for k in range(4)]
            self.phase_h1("xT_SO", HALO, False, False, g); self.reset()
        slots = []
        if "hfp" in ph:
            slots.append((0, [(0, d["G_P"][0][0], d["G_P"][0][1]), (1, d["G_P"][1][0], d["G_P"][1][1])]))
        if "hfs" in ph:
            for i in range(7):
                slots.append((1 + i, [(0, d["G_S1"][i][0], d["G_S1"][i][1])]))
            for j in range(4):
                slots.append((8 + j, [(1, d["G_S2"][j][0], d["G_S2"][j][1])]))
        if slots:
            self.phase_hmlp([sl_[0] for sl_ in slots]); self.reset()
            self.phase_hf(slots); self.reset()
        if "hyp" in ph:
            self.phase_hx([d["UT0_P"][0]], [d["XS_P"][0]]); self.reset()
            self.phase_hp([([(d["G_P"][0][0], d["G_P"][0][1])], d["X1T_P"][0], d["Z1T_P"][0])], [d["XS_P"][0]], 1); self.reset()
            self.phase_hx([d["Z1T_P"][0]], [d["XS_P"][0]]); self.reset()
            self.phase_hp([([(d["G_P"][1][0], d["G_P"][1][1])], d["X2T_P"], d["Z2T_P"])], [d["XS_P"][0]], 1); self.reset()
        if "hys" in ph:
            self.phase_hx([d["UT0_S"][j] for j in range(4)], [d["XS_S"][j] for j in range(4)]); self.reset()
            outs = []
            for i in range(4):
                gl = [(d["G_S1"][i - j + 3][0], d["G_S1"][i - j + 3][1]) for j in range(4)]
                outs.append((gl, d["X1T_S"][i], d["Z1T_S"][i]))
            self.phase_hp(outs, [d["XS_S"][j] for j in range(4)], 4); self.reset()
            self.phase_hx([d["Z1T_S"][j] for j in range(4)], [d["XS_S"][j] for j in range(4)]); self.reset()
            gl = [(d["G_S2"][j][0], d["G_S2"][j][1]) for j in range(4)]
            self.phase_hp([(gl, d["X2T_S"], d["Z2T_S"])], [d["XS_S"][j] for j in range(4)], 4); self.reset()
        if "attp" in ph:
            self.phase_att("xT_P", 0, SEQ, False, "E_P", None, None, "MA_P"); self.reset()
        if "atts" in ph:
            self.phase_att("xT_SO", HALO, EXT, True, "E_P", "EL_S", "ER_S", "MA_S"); self.reset()
        if "outp" in ph:
            self.phase_mh("xT_P", 0, "Z2T_P", "MH_P"); self.reset()
            self.phase_out("xp", 0, "MA_P", "MH_P", "yp"); self.reset()
        if "outs" in ph:
            self.phase_mh("xT_SO", HALO, "Z2T_S", "MH_S"); self.reset()
            self.phase_out("xso", HALO, "MA_S", "MH_S", "ys"); self.reset()


ALL_PHASES = ("xtp", "xts", "h1p", "h1s", "hfp", "hfs", "hyp", "hys", "attp", "atts", "outp", "outs")


def build_program(phases=ALL_PHASES, extra=None, ext_in=(), ext_out=()):
    nc = bass.Bass("TRN2", target_bir_lowering=False)
    st = ExitStack()
    P = Prog(nc, st, phases, ext_in, ext_out)
    P.declare()
    if extra is not None:
        extra(P)
    words = 53000
    ar = st.enter_context(nc.sbuf_tensor("arena", [128, words], F32))
    P.A = Arena(ar, words)
    P.bank = []
    P.bankb = []
    for i in range(8):
        P.bank.append(st.enter_context(nc.psum_tensor(f"pb{i}", [128, 512], F32))[:, :])
        P.bankb.append(Buf(f"pb{i}"))
    kb = KB(nc)
    kb.run(st, P.build)
    st.close()
    return nc, P


_CONST_CACHE = {}


def const_inputs():
    if "c" in _CONST_CACHE:
        return _CONST_CACHE["c"]
    c = dict(fft_consts())
    c["ident"] = np.eye(128, dtype=np.float32).astype(NPBF)
    c["ones"] = np.ones((128, 128), np.float32).astype(NPBF)
    oz = np.zeros((128, 2, 128), np.float32)
    oz[:, 0, :64] = 1.0
    oz[:, 1, 64:] = 1.0
    c["onesz"] = oz.reshape(128, 256).astype(NPBF)
    c["delta"] = decay_deltas()
    _CONST_CACHE["c"] = c
    return c


def core_inputs(core, inputs):
    c = dict(const_inputs())
    sb, blk = core // 4, core % 4
    f32 = lambda a: np.ascontiguousarray(np.asarray(a, dtype=np.float32))
    c["xp"] = f32(inputs["x_prompt"][core])
    xs = np.asarray(inputs["x_sample"][sb], dtype=np.float32)
    c["xsf"] = np.ascontiguousarray(xs)
    ext = np.zeros((EXT, D_MODEL), np.float32)
    lo, hi = blk * T - HALO, (blk + 1) * T + HALO
    slo, shi = max(lo, 0), min(hi, DEC_SEQ)
    ext[slo - lo: shi - lo] = xs[slo:shi]
    c["xso"] = ext
    for k in ("w_in", "w_out", "conv_w", "conv_b", "filt_w1", "filt_b1", "filt_w2", "filt_b2", "filt_w3",
              "filt_b3", "filt_freq", "filt_w4", "hyena_d", "attn_norm_g", "hyena_norm_g", "ln_g", "ln_b"):
        c[k] = f32(inputs[k][0])
    lags = [(SEQ, 0)] + [(DEC_SEQ, dd) for dd in range(-3, 4)] + [(DEC_SEQ, blk - j) for j in range(4)]
    key = ("slots", blk)
    if key not in _CONST_CACHE:
        zT = np.zeros((12, 33, NFFT), np.float32)
        sel = np.zeros((12, 128, NFFT), NPBF)
        ntl = np.zeros((12, 128, 64), np.float32)
        flag = np.zeros((1, 12), np.float32)
        for s, (L, dd) in enumerate(lags):
            k2 = ("slot", L, dd)
            if k2 not in _CONST_CACHE:
                _CONST_CACHE[k2] = filter_slot_tables(L, dd)
            zT[s], sel[s], ntl[s], flag[0, s] = _CONST_CACHE[k2]
        _CONST_CACHE[key] = (zT, sel, ntl, flag)
    c["f_zT"], c["f_sel"], c["f_ntl"], c["f_flag"] = _CONST_CACHE[key]
    ek = ("E", blk)
    if ek not in _CONST_CACHE:
        _CONST_CACHE[ek] = attn_tables(blk > 0, blk < 3)
    c["E_P"], c["EL_S"], c["ER_S"] = _CONST_CACHE[ek]
    return c


_PROG = {}


def kernel(**inputs):
    if "nc" not in _PROG:
        _PROG["nc"] = build_program()[0]
    nc = _PROG["nc"]
    in_maps = [core_inputs(core, inputs) for core in range(8)]
    res = run_bass_kernel_spmd(nc, in_maps, core_ids=list(range(8)))
    yp = np.stack([np.asarray(res.results[c]["yp"], dtype=np.float32) for c in range(8)], axis=0)
    ys = np.zeros((2, DEC_SEQ, D_MODEL), np.float32)
    for c in range(8):
        ys[c // 4, (c % 4) * T:(c % 4 + 1) * T] = np.asarray(res.results[c]["ys"], dtype=np.float32)
    return (yp, ys)
```

```python
import math
from contextlib import ExitStack

import numpy as np
import ml_dtypes

import concourse.bass as bass
import concourse.mybir as mybir
from concourse.bass_utils import run_bass_kernel_spmd

F32 = mybir.dt.float32
BF16 = mybir.dt.bfloat16
AF = mybir.ActivationFunctionType
ALU = mybir.AluOpType
NPBF = ml_dtypes.bfloat16

SEM_EPOCH = 30000
TWO_PI = 2.0 * math.pi

D_MODEL = 1024
SEQ = 4096
DEC_SEQ = 16384
NCH = 512
T = 4096
NFFT = 8192
HALO = 1024
EXT = T + 2 * HALO
PATTERNS = ((128, 1), (512, 4), (2048, 16))
LN_EPS = 1e-5
RMS_EPS = 1e-6
ALPHA = 2.0 ** 0.25


class Buf:
    __slots__ = ("name", "w", "r")

    def __init__(self, name):
        self.name = name
        self.w = {}
        self.r = {}


class Eng:
    def __init__(self, kb, name, handle):
        self.kb = kb
        self.name = name
        self.h = handle
        self.ops = []
        self.sem = None
        self.cnt = 0
        self.waited = {}
        self.last_tok = None
        self.pending = False

    def new_epoch(self):
        self.sem = self.kb.new_sem(self.name)
        self.cnt = 0


class KB:
    def __init__(self, nc, n_lanes=8):
        self.nc = nc
        self.sems = []
        self._stack = None
        self.eng = {}
        for name, h in (("pe", nc.tensor), ("act", nc.scalar), ("dve", nc.vector),
                        ("pool", nc.gpsimd), ("sp", nc.sync)):
            self.eng[name] = Eng(self, name, h)
        self.lanes = {}
        self.n_lanes = n_lanes
        self.lane_rr = {}

    def new_sem(self, name):
        cm = self.nc.semaphore(f"s{len(self.sems)}_{name}")
        s = self._stack.enter_context(cm)
        self.sems.append(s)
        return len(self.sems) - 1

    def _wait(self, e, tok):
        if tok is None:
            return
        sid, val = tok
        if e.waited.get(sid, 0) >= val:
            return
        e.waited[sid] = val
        sem = self.sems[sid]
        e.ops.append(lambda h, sem=sem, val=val: h.wait_ge(sem, val))

    def _deps(self, e, reads, writes, dwrites=()):
        toks = []
        for b in reads:
            toks.extend(b.w.items())
        for b in writes:
            toks.extend(b.w.items())
            toks.extend(b.r.items())
        for b in dwrites:
            toks.extend(b.r.items())
        for t in toks:
            if e.name == "pe" and e.sem is not None and t[0] == e.sem:
                continue
            self._wait(e, t)

    def _mark(self, tok, reads, writes, dwrites=()):
        sid, val = tok
        for b in list(writes) + list(dwrites):
            if b.w.get(sid, 0) < val:
                b.w[sid] = val
        for b in reads:
            if b.r.get(sid, 0) < val:
                b.r[sid] = val

    def op(self, eng, fn, reads=(), writes=(), inc=True, dwrites=()):
        e = self.eng[eng]
        if e.sem is None or e.cnt >= SEM_EPOCH:
            e.new_epoch()
        self._deps(e, reads, writes, dwrites)
        if inc:
            e.cnt += 1
            tok = (e.sem, e.cnt)
            sem = self.sems[e.sem]
            e.ops.append(lambda h, fn=fn, sem=sem: fn(h).then_inc(sem, 1))
            e.last_tok = tok
            e.pending = False
        else:
            tok = (e.sem, e.cnt + 1)
            e.ops.append(lambda h, fn=fn: fn(h))
            e.pending = True
        self._mark(tok, reads, writes, dwrites)
        return tok

    def dma(self, q, out, in_, reads=(), writes=(), slow=False, dwrites=()):
        e = self.eng[q]
        if q not in self.lanes:
            self.lanes[q] = [{"sem": None, "val": 0} for _ in range(self.n_lanes)]
            self.lane_rr[q] = 0
        ln = self.lanes[q][self.lane_rr[q] % self.n_lanes]
        self.lane_rr[q] += 1
        if ln["sem"] is None or ln["val"] >= 60000:
            ln["sem"] = self.new_sem(f"dma_{q}")
            ln["val"] = 0
        else:
            self._wait(e, (ln["sem"], ln["val"]))
        self._deps(e, reads, writes, dwrites)
        ln["val"] += 16
        tok = (ln["sem"], ln["val"])
        sem = self.sems[ln["sem"]]
        if slow:
            e.ops.append(lambda h, out=out, in_=in_, sem=sem:
                         h.dma_start(out=out, in_=in_, allow_slow_non_contiguous=True).then_inc(sem, 16))
        else:
            e.ops.append(lambda h, out=out, in_=in_, sem=sem: h.dma_start(out=out, in_=in_).then_inc(sem, 16))
        self._mark(tok, reads, writes, dwrites)
        return tok

    def all_tokens(self):
        toks = []
        for e in self.eng.values():
            assert not e.pending, f"engine {e.name} has a trailing non-inc op"
            if e.last_tok is not None:
                toks.append(e.last_tok)
        for lanes in self.lanes.values():
            for ln in lanes:
                if ln["sem"] is not None and ln["val"] > 0:
                    toks.append((ln["sem"], ln["val"]))
        return toks

    def barrier(self, engines=None):
        toks = self.all_tokens()
        for name, e in self.eng.items():
            if engines is not None and name not in engines:
                continue
            for t in toks:
                if e.sem is not None and t[0] == e.sem:
                    continue
                self._wait(e, t)

    def run(self, stack, build):
        self._stack = stack
        build(self)
        self.barrier()
        block = stack.enter_context(self.nc.Block())
        e = self.eng

        @block.sync
        def _(h):
            for f in e["sp"].ops:
                f(h)

        @block.tensor
        def _(h):
            for f in e["pe"].ops:
                f(h)

        @block.scalar
        def _(h):
            for f in e["act"].ops:
                f(h)

        @block.vector
        def _(h):
            for f in e["dve"].ops:
                f(h)

        @block.gpsimd
        def _(h):
            for f in e["pool"].ops:
                f(h)


class Arena:
    def __init__(self, handle, words):
        self.h = handle
        self.words = words
        self.top = 0

    def alloc(self, shape, dt, name="t"):
        free = int(np.prod(shape[1:]))
        nbytes = free * (2 if dt == BF16 else 4)
        w = (nbytes + 3) // 4
        w = (w + 7) // 8 * 8
        assert self.top + w <= self.words, f"arena overflow: {name} {shape} top={self.top} w={w}"
        ap = self.h[:, self.top:self.top + w]
        self.top += w
        if dt == BF16:
            ap = ap.bitcast(BF16)[:, 0:free]
        else:
            ap = ap[:, 0:free]
        if shape[0] < 128:
            ap = ap[0:shape[0], :]
        if len(shape) > 2:
            names = " ".join(f"d{i}" for i in range(len(shape) - 1))
            kw = {f"d{i}": shape[i + 1] for i in range(len(shape) - 1)}
            ap = ap.rearrange(f"p ({names}) -> p {names}", **kw)
        return ap, Buf(name)


def fft_consts():
    n1 = np.arange(128)[:, None]
    k1 = np.arange(64)[None, :]
    th = 2 * np.pi * n1 * (k1 + 0.5) / 128
    FA = np.concatenate([np.cos(th), -np.sin(th)], axis=1)
    n2 = np.arange(64)[:, None, None]
    k1b = np.arange(64)[None, :, None]
    k2 = np.arange(64)[None, None, :]
    ph = 2 * np.pi * (n2 * (k1b + 0.5) / 8192 + n2 * k2 / 64)
    wr = np.cos(ph)
    wi = -np.sin(ph)

    def mk(a0, a1, b0, b1):
        M = np.zeros((64, 64, 2, 128))
        M[:, :, 0, :64] = a0
        M[:, :, 0, 64:] = a1
        M[:, :, 1, :64] = b0
        M[:, :, 1, 64:] = b1
        M = M.reshape(64, -1)
        return np.concatenate([M, M], axis=0)

    FB = mk(wr, wi, -wi, wr)
    FBG1 = mk(wr, wr, -wi, -wi)
    FBG2 = mk(wi, wi, wr, wr)
    k2c = np.arange(64)[:, None]
    n2c = np.arange(64)[None, :]
    t = 2 * np.pi * n2c * k2c / 64
    c, s = np.cos(t), np.sin(t)
    FinvB1 = np.block([[c, s], [-s, c]])
    FinvB2 = np.block([[-s, c], [-c, -s]])
    k1a = np.arange(64)[:, None, None]
    n2a = np.arange(64)[None, :, None]
    n1a = np.arange(64)[None, None, :]
    phi = 2 * np.pi * (k1a + 0.5) * (64 * n1a + n2a) / 8192
    FinvA = np.zeros((64, 64, 2, 64))
    FinvA[:, :, 0, :] = (2.0 / NFFT) * np.cos(phi)
    FinvA[:, :, 1, :] = -(2.0 / NFFT) * np.sin(phi)
    FinvA = FinvA.reshape(64, -1)
    FinvA = np.concatenate([FinvA, FinvA], axis=0)
    bf = lambda a: np.ascontiguousarray(a.astype(np.float32).astype(NPBF))
    return dict(FA=bf(FA), FB=bf(FB), FBG1=bf(FBG1), FBG2=bf(FBG2), FinvB1=bf(FinvB1),
                FinvB2=bf(FinvB2), FinvA=bf(FinvA))


def filter_slot_tables(L, d):
    m = np.arange(NFFT)
    lam = np.where(m < T, d * T + m, d * T - (NFFT - m)).astype(np.int64)
    sign = np.where(m < T, 1.0, -1.0)
    sign[T] = 0.0
    tpos = np.abs(lam)
    valid = tpos < L
    sign = sign * valid
    tpos = np.minimum(tpos, L - 1)
    fwd = lam >= 0
    f32 = np.float32
    tl = np.linspace(0.0, 1.0, L, dtype=f32)
    bands = 16
    w = (f32(2.0 * math.pi) * np.arange(L, dtype=f32) / f32(L)).astype(f32)
    fr = np.linspace(1e-4, bands - 1, bands, dtype=f32)
    ang = (fr[None, :] * w[:, None]).astype(f32)
    z = np.concatenate([tl[:, None], np.cos(ang), -np.sin(ang)], axis=1).astype(f32)
    zT = np.ascontiguousarray(z[tpos].T.astype(f32))
    sel = np.zeros((128, NFFT), np.float32)
    sel[:64] = (sign * fwd)[None, :]
    sel[64:] = (sign * (~fwd))[None, :]
    ntl = (-tl[tpos]).astype(f32).reshape(128, 64)
    flag = 1.0 if d == 0 else 0.0
    return zT, sel.astype(NPBF), np.ascontiguousarray(ntl), flag


def decay_deltas():
    f32 = np.float32
    max_decay = math.log(1e-2) / 0.3
    min_decay = math.log(1e-2) / 1.5
    return np.abs(np.linspace(min_decay, max_decay, NCH, dtype=f32)).astype(f32)


def attn_tables(left_ok, right_ok):
    slopes = np.array([2.0 ** (-8.0 * (i + 1) / 8) for i in range(8)], np.float64)
    p = np.arange(128)[:, None]
    col = np.arange(256)[None, :]
    rel = np.abs(col - 64 - p)
    E = np.zeros((128, 3, 8, 256), np.float32)
    EL = np.zeros((64, 3, 8, 64), np.float32)
    ER = np.zeros((64, 3, 8, 64), np.float32)
    pk = np.arange(64)[:, None]
    q = np.arange(64)[None, :]
    relL = q + 64 - pk
    relR = pk + 64 - q
    for pi, (_, r) in enumerate(PATTERNS):
        for h in range(8):
            E[:, pi, h, :] = np.where(rel <= 64, np.exp(-slopes[h] * r * rel), 0.0)
            if left_ok:
                EL[:, pi, h, :] = np.where(relL <= 64, np.exp(-slopes[h] * r * relL), 0.0)
            if right_ok:
                ER[:, pi, h, :] = np.where(relR <= 64, np.exp(-slopes[h] * r * relR), 0.0)
    ELR = np.zeros((128, 3, 8, 64), np.float32)
    bf = lambda a: np.ascontiguousarray(a.astype(NPBF))
    return bf(E.reshape(128, -1)), bf(EL.reshape(64, -1)), bf(ER.reshape(64, -1))


class Prog:
    def __init__(self, nc, st, phases, ext_in=(), ext_out=()):
        self.ext_in = set(ext_in)
        self.ext_out = set(ext_out)
        self.nc = nc
        self.st = st
        self.phases = phases
        self.d = {}
        self.ps_i = 0
        self.ev_i = 0
        self.fresh = {}
        self.pools = {"acc": [0, 1], "tmp": [2, 3, 4, 5, 6, 7]}
        self.pool_i = {}

    def inp(self, name, shape, dt=F32):
        self.d[name] = self.nc.dram_tensor(name, list(shape), dt, kind="ExternalInput").ap()
        return self.d[name]

    def outp(self, name, shape, dt=F32):
        self.d[name] = self.nc.dram_tensor(name, list(shape), dt, kind="ExternalOutput").ap()
        return self.d[name]

    def scr(self, name, shape, dt=BF16):
        kind = "Internal"
        if name in self.ext_in:
            kind = "ExternalInput"
        if name in self.ext_out:
            kind = "ExternalOutput"
        self.d[name] = self.nc.dram_tensor(name, list(shape), dt, kind=kind).ap()
        return self.d[name]

    def ps(self, pool=None):
        if pool is None:
            i = self.ps_i % 8
            self.ps_i += 1
        else:
            lst = self.pools[pool]
            k = self.pool_i.get(pool, 0)
            self.pool_i[pool] = k + 1
            i = lst[k % len(lst)]
        self.fresh[self.bankb[i]] = True
        return self.bank[i], self.bankb[i]

    def evac(self, out, in_, reads, writes, eng=None, disjoint=True):
        kb = self.kb
        if eng is None:
            eng = ("act", "dve")[self.ev_i % 2]
            self.ev_i += 1
        w, dw = ((), writes) if disjoint else (writes, ())
        if eng == "act":
            kb.op("act", lambda e: e.activation(out=out, in_=in_, func=AF.Copy), reads, w, dwrites=dw)
        else:
            kb.op(eng, lambda e: e.tensor_copy(out=out, in_=in_), reads, w, dwrites=dw)

    def mm(self, out, lhsT, rhs, start, stop, reads, writes, inc):
        bb = writes[0]
        start = self.fresh.get(bb, False)
        self.fresh[bb] = False
        self.kb.op("pe", lambda e: e.matmul(out, lhsT=lhsT, rhs=rhs, start=start, stop=stop, skip_group_check=True),
                   reads, writes, inc=inc)

    def tr(self, out, in_, ident, reads, writes, inc):
        self.kb.op("pe", lambda e: e.transpose(out, in_, ident), reads, writes, inc=inc)

    def reset(self):
        self.kb.barrier()
        self.hiwater = max(getattr(self, "hiwater", 0), self.A.top)
        self.A.top = self.A_base

    def declare(self):
        inp, scr = self.inp, self.scr
        inp("xp", [SEQ, D_MODEL]); inp("xsf", [DEC_SEQ, D_MODEL]); inp("xso", [EXT, D_MODEL])
        inp("w_in", [D_MODEL, 4096]); inp("w_out", [D_MODEL, D_MODEL])
        inp("conv_w", [3, 1536]); inp("conv_b", [1536])
        inp("filt_w1", [33, 64]); inp("filt_b1", [64]); inp("filt_w2", [64, 64]); inp("filt_b2", [64])
        inp("filt_w3", [64, 64]); inp("filt_b3", [64]); inp("filt_freq", [3, 64]); inp("filt_w4", [64, 2048])
        inp("hyena_d", [2, NCH]); inp("attn_norm_g", [512]); inp("hyena_norm_g", [512])
        inp("ln_g", [D_MODEL]); inp("ln_b", [D_MODEL])
        inp("ident", [128, 128], BF16); inp("ones", [128, 128], BF16); inp("onesz", [128, 256], BF16)
        inp("FA", [128, 128], BF16); inp("FB", [128, 64 * 256], BF16)
        inp("FBG1", [128, 64 * 256], BF16); inp("FBG2", [128, 64 * 256], BF16)
        inp("FinvB1", [128, 128], BF16); inp("FinvB2", [128, 128], BF16); inp("FinvA", [128, 64 * 128], BF16)
        inp("f_zT", [12, 33, NFFT]); inp("f_sel", [12, 128, NFFT], BF16); inp("f_ntl", [12, 128, 64])
        inp("f_flag", [1, 12]); inp("delta", [NCH])
        inp("E_P", [128, 3 * 8 * 256], BF16)
        inp("EL_S", [64, 3 * 8 * 64], BF16); inp("ER_S", [64, 3 * 8 * 64], BF16)
        self.outp("yp", [SEQ, D_MODEL]); self.outp("ys", [T, D_MODEL])
        scr("xT_P", [8, 128, SEQ]); scr("xT_SF", [8, 128, DEC_SEQ]); scr("xT_SO", [8, 128, EXT])
        for nm, nb in (("P", 1), ("S", 4)):
            scr(f"UT0_{nm}", [nb, 64, NCH * 64]); scr(f"X1T_{nm}", [nb, 64, NCH * 64])
            scr(f"X2T_{nm}", [64, NCH * 64]); scr(f"Z1T_{nm}", [nb, 64, NCH * 64]); scr(f"Z2T_{nm}", [64, NCH * 64])
            scr(f"XS_{nm}", [nb, 128, NCH * 64])
            scr(f"MA_{nm}", [128, 4 * T]); scr(f"MH_{nm}", [128, 4 * T])
        scr("HAUG", [12, 128, NFFT])
        scr("G_P", [2, 2, 128, NCH * 64])
        scr("G_S1", [7, 2, 128, NCH * 64])
        scr("G_S2", [4, 2, 128, NCH * 64])

    def setup(self):
        A, kb, d = self.A, self.kb, self.d
        self.ident, self.identb = A.alloc([128, 128], BF16, "ident")
        kb.dma("sp", self.ident, d["ident"], writes=[self.identb])
        self.ones, self.onesb = A.alloc([128, 128], BF16, "ones")
        kb.dma("sp", self.ones, d["ones"], writes=[self.onesb])
        self.A_base = A.top

    def phase_xt(self, xname, xTname, ntok):
        A, kb, d = self.A, self.kb, self.d
        x, xT = d[xname], d[xTname]
        x32 = [A.alloc([128, 1024], F32, f"x32_{i}") for i in range(4)]
        xb = [A.alloc([128, 1024], BF16, f"xb_{i}") for i in range(4)]
        xT4 = [A.alloc([128, 8, 512], BF16, f"xT4_{i}") for i in range(2)]
        for t in range(ntok // 128):
            a32, b32 = x32[t % 4]
            ab, bb = xb[t % 4]
            kb.dma("sp", a32, x[t * 128:(t + 1) * 128, :], writes=[b32])
            kb.op("pool" if t % 4 == 3 else "dve", lambda e, o=ab, i=a32: e.tensor_copy(out=o, in_=i), [b32], [bb])
            bank, bankb = self.ps()
            bbf = bank.bitcast(BF16)
            for kc in range(8):
                self.tr(bbf[:, kc * 128:(kc + 1) * 128], ab[:, kc * 128:(kc + 1) * 128], self.ident,
                        [bb, self.identb], [bankb], inc=(kc == 7))
            g = t // 4
            q = t % 4
            a4, b4 = xT4[g % 2]
            self.evac(a4[:, :, q * 128:(q + 1) * 128], bbf.rearrange("p (k t) -> p k t", k=8), [bankb], [b4], eng="act")
            if q == 3:
                kb.dma("act", xT[:, :, g * 512:(g + 1) * 512].rearrange("k p t -> p k t"), a4, reads=[b4])

    def phase_h1(self, xTname, tok0, zero_l, zero_r, groups):
        A, kb, d = self.A, self.kb, self.d
        xT = d[xTname]
        xblk, xblkb = A.alloc([128, 8, T + 2], BF16, "xblk")
        src = xT[:, :, tok0:tok0 + T].rearrange("k p t -> p k t")
        kb.dma("sp", xblk[:, :, 1:T + 1], src, writes=[xblkb])
        if zero_l:
            kb.op("pool", lambda e: e.memset(xblk[:, :, 0:1], 0.0), [], [xblkb])
        else:
            kb.dma("sp", xblk[:, :, 0:1], xT[:, :, tok0 - 1:tok0].rearrange("k p t -> p k t"), writes=[xblkb], slow=True)
        if zero_r:
            kb.op("pool", lambda e: e.memset(xblk[:, :, T + 1:T + 2], 0.0), [], [xblkb])
        else:
            kb.dma("sp", xblk[:, :, T + 1:T + 2], xT[:, :, tok0 + T:tok0 + T + 1].rearrange("k p t -> p k t"),
                   writes=[xblkb], slow=True)
        w32 = [A.alloc([128, 8, 128], F32, f"w32_{i}") for i in range(3)]
        wb = [A.alloc([128, 8, 128], BF16, f"wb_{i}") for i in range(3)]
        cw = [A.alloc([128, 4], F32, f"cw_{i}") for i in range(3)]
        pfs = [A.alloc([128, T + 2], BF16, f"pf{i}") for i in range(2)]
        ufs = [A.alloc([128, T], F32, f"uf{i}") for i in range(2)]
        ubs = [A.alloc([128, T], BF16, f"ub{i}") for i in range(2)]
        utg = [A.alloc([64, 128, 64], BF16, f"utg_{i}") for i in range(2)]
        def load_w(gi):
            if gi >= len(groups):
                return
            co, cc, dst, dg = groups[gi]
            a32, b32 = w32[gi % 3]
            awb, bwb = wb[gi % 3]
            acw, bcw = cw[gi % 3]
            kb.dma("sp", a32, d["w_in"][:, co:co + 128].rearrange("(k p) c -> p k c", p=128), writes=[b32])
            kb.op("pool", lambda e, o=awb, i=a32: e.tensor_copy(out=o, in_=i), [b32], [bwb])
            kb.dma("sp", acw[:, 0:3], d["conv_w"][:, cc:cc + 128].rearrange("j c -> c j"), writes=[bcw], slow=True)
            kb.dma("sp", acw[:, 3:4], d["conv_b"][cc:cc + 128].rearrange("(c o) -> c o", o=1), writes=[bcw], slow=True)

        def stage1(gi):
            co, cc, dst, dg = groups[gi]
            pf, pfb = pfs[gi % 2]
            uf, ufb = ufs[gi % 2]
            ub, ubb = ubs[gi % 2]
            awb, bwb = wb[gi % 3]
            acw, bcw = cw[gi % 3]
            load_w(gi + 1)
            nchunk = (T + 2 + 511) // 512
            for ch in range(nchunk):
                c0 = ch * 512
                n = min(512, T + 2 - c0)
                bank, bankb = self.ps()
                for kc in range(8):
                    self.mm(bank[:, 0:n], awb[:, kc, :], xblk[:, kc, c0:c0 + n], kc == 0, kc == 7,
                            [bwb, xblkb], [bankb], inc=(kc == 7))
                self.evac(pf[:, c0:c0 + n], bank[:, 0:n], [bankb], [pfb])
            kb.op("act", lambda e: e.activation(out=uf, in_=pf[:, 1:T + 1], func=AF.Identity,
                                                bias=acw[:, 3:4], scale=acw[:, 1:2]), [pfb, bcw], [ufb])
            kb.op("dve", lambda e: e.scalar_tensor_tensor(out=uf, in0=pf[:, 0:T], scalar=acw[:, 0:1], in1=uf,
                                                          op0=ALU.mult, op1=ALU.add), [pfb, bcw, ufb], [ufb])
            kb.op("dve", lambda e: e.scalar_tensor_tensor(out=ub, in0=pf[:, 2:T + 2], scalar=acw[:, 2:3], in1=uf,
                                                          op0=ALU.mult, op1=ALU.add), [pfb, bcw, ufb], [ubb])

        def stage2(gi):
            co, cc, dst, dg = groups[gi]
            ub, ubb = ubs[gi % 2]
            ubv = ub.rearrange("p (a b) -> p a b", b=64)
            autg, butg = utg[gi % 2]
            for ng in range(8):
                bank, bankb = self.ps()
                bbf = bank.bitcast(BF16)
                for nn in range(8):
                    n2 = ng * 8 + nn
                    self.tr(bbf[0:64, nn * 128:(nn + 1) * 128], ubv[:, :, n2], self.ident,
                            [ubb, self.identb], [bankb], inc=(nn == 7))
                self.evac(autg[:, :, ng * 8:(ng + 1) * 8], bbf[0:64, :].rearrange("p (n c) -> p c n", n=8),
                          [bankb], [butg])
            kb.dma("act", dst[:, dg * 8192:(dg + 1) * 8192], autg.rearrange("p c n -> p (c n)"), reads=[butg])

        ng_ = len(groups)
        load_w(0)
        for gi in range(ng_ + 1):
            if gi < ng_:
                stage1(gi)
            if gi >= 1:
                stage2(gi - 1)

    def fwd_cg(self, ut, utb, K, FA, FAb, variants, ast, astb, emit):
        astv = ast.rearrange("p (c r k) -> p c r k", r=2, k=64)
        for g in range(16):
            bank, bankb = self.ps()
            for q in range(4):
                cp = g * 4 + q
                self.mm(bank[:, q * 128:(q + 1) * 128], ut[:, cp * 128:(cp + 1) * 128], FA[0:K, :], True, True,
                        [utb, FAb], [bankb], inc=(q == 3))
            self.evac(ast[:, g * 512:(g + 1) * 512], bank, [bankb], [astb])
        for v, (FBt, FBb) in enumerate(variants):
            def fill(Xv, Xb, FBt=FBt, FBb=FBb):
                for kg in range(8):
                    b0, b0b = self.ps()
                    b1, b1b = self.ps()
                    for kk in range(8):
                        k1 = kg * 8 + kk
                        for par, (bk, bkb) in enumerate(((b0, b0b), (b1, b1b))):
                            for ri in range(2):
                                self.mm(bk[:, kk * 64:(kk + 1) * 64], FBt[par * 64:(par + 1) * 64, k1, ri, :],
                                        astv[par * 64:(par + 1) * 64, :, ri, k1], ri == 0, ri == 1,
                                        [FBb, astb], [bkb], inc=(kk == 7 and ri == 1))
                    for par, (bk, bkb) in enumerate(((b0, b0b), (b1, b1b))):
                        self.evac(Xv[:, :, par, kg * 8:(kg + 1) * 8], bk.rearrange("p (k c) -> p c k", k=8), [bkb], [Xb])
            emit(v, fill)

    def phase_hx(self, srcs, dsts):
        A, kb, d = self.A, self.kb, self.d
        FA, FAb = A.alloc([128, 128], BF16, "FA")
        kb.dma("sp", FA, d["FA"], writes=[FAb])
        FB, FBb = A.alloc([128, 64, 2, 128], BF16, "FB")
        kb.dma("sp", FB.rearrange("p a b c -> p (a b c)"), d["FB"], writes=[FBb])
        uts = [A.alloc([64, 8192], BF16, f"ut_{i}") for i in range(2)]
        asts = [A.alloc([128, 8192], BF16, f"ast_{i}") for i in range(2)]
        xts = [A.alloc([128, 64, 2, 64], BF16, f"xt_{i}") for i in range(2)]
        it = 0
        for src, dst in zip(srcs, dsts):
            for cg in range(4):
                ut, utb = uts[it % 2]
                ast, astb = asts[it % 2]
                xt, xtb = xts[it % 2]
                it += 1
                kb.dma("sp", ut, src[:, cg * 8192:(cg + 1) * 8192], writes=[utb])

                def emit(v, fill, xt=xt, xtb=xtb, dst=dst, cg=cg):
                    fill(xt, xtb)
                    kb.dma("act", dst[:, cg * 8192:(cg + 1) * 8192], xt.rearrange("p a b c -> p (a b c)"), reads=[xtb])
                self.fwd_cg(ut, utb, 64, FA, FAb, [(FB, FBb)], ast, astb, emit)

    def phase_hmlp(self, slot_ids):
        A, kb, d = self.A, self.kb, self.d
        inv2pi = 1.0 / TWO_PI
        fq, fqb = A.alloc([128, 3, 64], F32, "fq")
        kb.dma("sp", fq.rearrange("p a b -> p (a b)"), d["filt_freq"].rearrange("a b -> (a b)").partition_broadcast(128),
               writes=[fqb])
        fqc, fqcb = A.alloc([128, 3], F32, "fqc")
        for half in range(2):
            kb.dma("sp", fqc[half * 64:(half + 1) * 64, :], d["filt_freq"].rearrange("a b -> b a"), writes=[fqcb], slow=True)
        w1p, w1pb = A.alloc([33, 64], F32, "w1p")
        kb.dma("sp", w1p, d["filt_w1"], writes=[w1pb])
        kb.op("dve", lambda e: e.scalar_tensor_tensor(out=w1p, in0=w1p, scalar=inv2pi, in1=fq[0:33, 0, :],
                                                       op0=ALU.mult, op1=ALU.mult), [w1pb, fqb], [w1pb])
        w2f, w2fb = A.alloc([64, 64], F32, "w2f")
        kb.dma("sp", w2f, d["filt_w2"], writes=[w2fb])
        w2p, w2pb = A.alloc([64, 64], BF16, "w2p")
        kb.op("dve", lambda e: e.scalar_tensor_tensor(out=w2p, in0=w2f, scalar=inv2pi, in1=fq[0:64, 1, :],
                                                       op0=ALU.mult, op1=ALU.mult), [w2fb, fqb], [w2pb])
        w3f, w3fb = A.alloc([64, 64], F32, "w3f")
        kb.dma("sp", w3f, d["filt_w3"], writes=[w3fb])
        w3p, w3pb = A.alloc([64, 2, 64], BF16, "w3p")
        for dup in range(2):
            kb.op("dve", lambda e, dup=dup: e.scalar_tensor_tensor(out=w3p[:, dup, :], in0=w3f, scalar=inv2pi,
                                                                    in1=fq[0:64, 2, :], op0=ALU.mult, op1=ALU.mult),
                  [w3fb, fqb], [w3pb])
        bp, bpb = A.alloc([128, 3], F32, "bp")
        for l, nm in enumerate(("filt_b1", "filt_b2", "filt_b3")):
            for half in range(2):
                kb.dma("sp", bp[half * 64:(half + 1) * 64, l:l + 1], d[nm].rearrange("(c o) -> c o", o=1), writes=[bpb], slow=True)
        kb.op("dve", lambda e: e.scalar_tensor_tensor(out=bp, in0=bp, scalar=inv2pi, in1=fqc, op0=ALU.mult, op1=ALU.mult),
              [bpb, fqcb], [bpb])
        haugs = [A.alloc([128, NFFT], BF16, f"haug{i}") for i in range(2)]
        GC = 8
        zts = [A.alloc([33, GC * 512], F32, f"zt_{i}") for i in range(2)]
        sels = [A.alloc([128, GC * 512], BF16, f"sel_{i}") for i in range(2)]
        uu = [A.alloc([128, 512], F32, f"uu_{i}") for i in range(GC)]
        vv = [A.alloc([128, 512], F32, f"vv_{i}") for i in range(GC)]
        hh = [[A.alloc([128, 512], BF16, f"hh_{l}_{i}") for i in range(GC)] for l in range(3)]
        gi = 0
        for si, slot in enumerate(slot_ids):
            haug, haugb = haugs[si % 2]
            for cgp in range(16 // GC):
                zt, ztb = zts[gi % 2]
                sl, slb = sels[gi % 2]
                gi += 1
                kb.dma("sp", zt, d["f_zT"][slot][:, cgp * GC * 512:(cgp + 1) * GC * 512], writes=[ztb])
                kb.dma("sp", sl, d["f_sel"][slot][:, cgp * GC * 512:(cgp + 1) * GC * 512], writes=[slb])
                prev = [(zt[:, c * 512:(c + 1) * 512], ztb) for c in range(GC)]
                for l in range(3):
                    P_ = 128 if l == 2 else 64
                    lhs, lhsb = ((w1p, w1pb), (w2p, w2pb), (w3p.rearrange("p a b -> p (a b)"), w3pb))[l]
                    new = []
                    for c in range(GC):
                        pin, pinb = prev[c]
                        bank, bankb = self.ps()
                        self.mm(bank[0:P_, :], lhs, pin, True, True, [lhsb, pinb], [bankb], inc=True)
                        u, ub_ = uu[c]
                        kb.op("act", lambda e, u=u, bank=bank, P_=P_, l=l: e.activation(
                            out=u[0:P_, :], in_=bank[0:P_, :], func=AF.Identity, bias=bp[0:P_, l:l + 1], scale=1.0),
                            [bankb, bpb], [ub_])
                    for c in range(GC):
                        u, ub_ = uu[c]
                        v_, vb_ = vv[c]
                        kb.op("dve", lambda e, u=u, v_=v_, P_=P_: e.scalar_tensor_tensor(
                            out=v_[0:P_, :], in0=u[0:P_, :], scalar=0.5, in1=u[0:P_, :], op0=ALU.is_gt, op1=ALU.subtract),
                            [ub_], [vb_])
                    for c in range(GC):
                        u, ub_ = uu[c]
                        v_, vb_ = vv[c]
                        kb.op("dve", lambda e, u=u, v_=v_, P_=P_: e.scalar_tensor_tensor(
                            out=v_[0:P_, :], in0=u[0:P_, :], scalar=-0.5, in1=v_[0:P_, :], op0=ALU.is_lt, op1=ALU.subtract),
                            [ub_, vb_], [vb_])
                    for c in range(GC):
                        v_, vb_ = vv[c]
                        h_, hb_ = hh[l][c]
                        kb.op("act", lambda e, h_=h_, v_=v_, P_=P_: e.activation(
                            out=h_[0:P_, :], in_=v_[0:P_, :], func=AF.Sin, scale=TWO_PI), [vb_], [hb_])
                        new.append((h_[0:P_, :], hb_))
                    prev = new
                for c in range(GC):
                    pin, pinb = prev[c]
                    ch = cgp * GC + c
                    kb.op("pool", lambda e, pin=pin, sl=sl, c=c, ch=ch, haug=haug: e.tensor_tensor(
                        out=haug[:, ch * 512:(ch + 1) * 512], in0=pin, in1=sl[:, c * 512:(c + 1) * 512], op=ALU.mult),
                        [pinb, slb], dwrites=[haugb])
            kb.dma("act", d["HAUG"][slot], haug, reads=[haugb])

    def phase_hf(self, slots):
        A, kb, d = self.A, self.kb, self.d
        FA, FAb = A.alloc([128, 128], BF16, "FA")
        kb.dma("sp", FA, d["FA"], writes=[FAb])
        FB1, FB1b = A.alloc([128, 64, 2, 128], BF16, "FBG1")
        kb.dma("sp", FB1.rearrange("p a b c -> p (a b c)"), d["FBG1"], writes=[FB1b])
        FB2, FB2b = A.alloc([128, 64, 2, 128], BF16, "FBG2")
        kb.dma("sp", FB2.rearrange("p a b c -> p (a b c)"), d["FBG2"], writes=[FB2b])
        FBs = ((FB1, FB1b), (FB2, FB2b))
        w4f, w4fb = A.alloc([128, 2, 512], F32, "w4f")
        w4v = d["filt_w4"].rearrange("h (o dd c) -> h o dd c", o=2, dd=2)
        for dirn in range(2):
            kb.dma("sp", w4f[dirn * 64:(dirn + 1) * 64, :, :], w4v[:, :, dirn, :], writes=[w4fb])
        w4t, w4tb = A.alloc([128, 2, 512], BF16, "w4t")
        kb.op("dve", lambda e: e.tensor_copy(out=w4t, in_=w4f), [w4fb], [w4tb])
        dl, dlb = A.alloc([128, NCH], F32, "dl")
        kb.dma("sp", dl, d["delta"].partition_broadcast(128), writes=[dlb])
        drow, drowb = A.alloc([1, 2, NCH], F32, "drow")
        kb.dma("sp", drow.rearrange("p a b -> p (a b)"), d["hyena_d"].rearrange("(x a) b -> x (a b)", x=1), writes=[drowb])
        flg, flgb = A.alloc([1, 12], F32, "flg")
        kb.dma("sp", flg, d["f_flag"], writes=[flgb])
        haugs = [A.alloc([128, NFFT], BF16, f"haug{i}") for i in range(2)]
        ntls = [A.alloc([128, 64], F32, f"ntl{i}") for i in range(2)]
        dec4 = [A.alloc([128, 4, 128], F32, f"dec_{i}") for i in range(2)]
        gts = [A.alloc([128, 128, 64], BF16, f"gt{i}") for i in range(2)]
        ast, astb = A.alloc([128, 8192], BF16, "ast")
        astv = ast.rearrange("p (c r k) -> p c r k", r=2, k=64)
        xts = [A.alloc([128, 64, 2, 64], BF16, f"xt_{i}") for i in range(2)]
        units = []
        for si, (slot, outs) in enumerate(slots):
            for (o, g1dst, g2dst) in outs:
                for cg in range(4):
                    units.append(dict(si=si, slot=slot, o=o, cg=cg, dst=(g1dst, g2dst)))
        loaded = set()

        def load_slot(si):
            if si in loaded or si >= len(slots):
                return
            loaded.add(si)
            slot = slots[si][0]
            kb.dma("sp", haugs[si % 2][0], d["HAUG"][slot], writes=[haugs[si % 2][1]])
            kb.dma("sp", ntls[si % 2][0], d["f_ntl"][slot], writes=[ntls[si % 2][1]])

        dci = [0]

        def fg_step(ui, ng):
            u = units[ui]
            if ng == 0:
                load_slot(u["si"])
                load_slot(u["si"] + 1)
            haug, haugb = haugs[u["si"] % 2]
            ntl, ntlb = ntls[u["si"] % 2]
            haugv = haug.rearrange("p (a b) -> p a b", b=64)
            gt, gtb = gts[ui % 2]
            o, cg = u["o"], u["cg"]
            bank, bankb = self.ps()
            dc, dcb = dec4[dci[0] % 2]
            dci[0] += 1
            for nn in range(4):
                n2 = ng * 4 + nn
                kb.op("act", lambda e, nn=nn, n2=n2: e.activation(
                    out=dc[:, nn, :], in_=dl[:, cg * 128:(cg + 1) * 128], func=AF.Exp,
                    scale=ntl[:, n2:n2 + 1]), [dlb, ntlb], [dcb])
                self.mm(bank[:, nn * 128:(nn + 1) * 128], haugv[:, :, n2], w4t[:, o, cg * 128:(cg + 1) * 128],
                        True, True, [haugb, w4tb], [bankb], inc=(nn == 3))
            kb.op("dve", lambda e: e.tensor_tensor(
                out=gt[:, :, ng * 4:(ng + 1) * 4], in0=bank.rearrange("p (n c) -> p c n", n=4),
                in1=dc.rearrange("p n c -> p c n"), op=ALU.mult), [bankb, dcb], dwrites=[gtb])
            if ng == 15:
                slot = u["slot"]
                kb.op("dve", lambda e: e.scalar_tensor_tensor(
                    out=gt[0:1, :, 0], in0=drow[0:1, o, cg * 128:(cg + 1) * 128], scalar=flg[0:1, slot:slot + 1],
                    in1=gt[0:1, :, 0], op0=ALU.mult, op1=ALU.add), [drowb, flgb, gtb], [gtb])

        def da(ui):
            gt, gtb = gts[ui % 2]
            ut = gt.rearrange("p c n -> p (c n)")
            for g in range(16):
                bank, bankb = self.ps()
                for q in range(4):
                    cp = g * 4 + q
                    self.mm(bank[:, q * 128:(q + 1) * 128], ut[:, cp * 128:(cp + 1) * 128], FA, True, True,
                            [gtb, FAb], [bankb], inc=(q == 3))
                self.evac(ast[:, g * 512:(g + 1) * 512], bank, [bankb], [astb])

        def mb_step(ui, step):
            u = units[ui]
            v, kg = step // 8, step % 8
            FBt, FBb = FBs[v]
            Xv, Xb = xts[(ui * 2 + v) % 2]
            b0, b0b = self.ps()
            b1, b1b = self.ps()
            for kk in range(8):
                k1 = kg * 8 + kk
                for par, (bk, bkb) in enumerate(((b0, b0b), (b1, b1b))):
                    for ri in range(2):
                        self.mm(bk[:, kk * 64:(kk + 1) * 64], FBt[par * 64:(par + 1) * 64, k1, ri, :],
                                astv[par * 64:(par + 1) * 64, :, ri, k1], ri == 0, ri == 1,
                                [FBb, astb], [bkb], inc=(kk == 7 and ri == 1))
            for par, (bk, bkb) in enumerate(((b0, b0b), (b1, b1b))):
                self.evac(Xv[:, :, par, kg * 8:(kg + 1) * 8], bk.rearrange("p (k c) -> p c k", k=8), [bkb], [Xb])
            if kg == 7:
                cg = u["cg"]
                kb.dma("act", u["dst"][v][:, cg * 8192:(cg + 1) * 8192], Xv.rearrange("p a b c -> p (a b c)"), reads=[Xb])

        n = len(units)
        for ng in range(16):
            fg_step(0, ng)
        da(0)
        for ui in range(n):
            for step in range(16):
                if ui + 1 < n:
                    fg_step(ui + 1, step)
                mb_step(ui, step)
            if ui + 1 < n:
                da(ui + 1)

    def phase_hp(self, outs, xs, nin):
        A, kb, d = self.A, self.kb, self.d
        FiB1, FiB1b = A.alloc([128, 128], BF16, "FiB1")
        kb.dma("sp", FiB1, d["FinvB1"], writes=[FiB1b])
        FiB2, FiB2b = A.alloc([128, 128], BF16, "FiB2")
        kb.dma("sp", FiB2, d["FinvB2"], writes=[FiB2b])
        FiA, FiAb = A.alloc([128, 64, 2, 64], BF16, "FiA")
        kb.dma("sp", FiA.rearrange("p a b c -> p (a b c)"), d["FinvA"], writes=[FiAb])
        NB = 3
        xres = [A.alloc([128, 8192], BF16, f"xres_{j}") for j in range(nin)]
        g1 = [A.alloc([128, 2048], BF16, f"g1_{i}") for i in range(NB)]
        g2 = [A.alloc([128, 2048], BF16, f"g2_{i}") for i in range(NB)]
        t1 = [A.alloc([128, 2048], BF16, f"t1_{i}") for i in range(2)]
        t2 = [A.alloc([128, 2048], BF16, f"t2_{i}") for i in range(2)]
        bsts = [A.alloc([128, 64, 2, 64], BF16, f"bst_{i}") for i in range(2)]
        xgs = [A.alloc([64, 64, 2, 64], BF16, f"xg_{i}") for i in range(1)]
        zts = [A.alloc([64, 64, 2, 64], BF16, f"zt_{i}") for i in range(2)]
        li = 0
        ti = 0
        ci = 0
        for cg in range(4):
            for j in range(nin):
                kb.dma("sp", xres[j][0], xs[j][:, cg * 8192:(cg + 1) * 8192], writes=[xres[j][1]])
            for (gl, gate, dst) in outs:
                bst, bstb = bsts[ci % 2]
                xg, xgb = xgs[0]
                zt, ztb = zts[ci % 2]
                ci += 1
                kb.dma("sp", xg.rearrange("p a b c -> p (a b c)"), gate[:, cg * 8192:(cg + 1) * 8192], writes=[xgb])
                bstf = bst.rearrange("p a b c -> p (a b c)")
                for cq in range(4):
                    c0 = (cg * 128 + cq * 32) * 64
                    banks = [self.ps() for _ in range(4)]
                    for j in range(nin):
                        ax, bx = xres[j][0][:, cq * 2048:(cq + 1) * 2048], xres[j][1]
                        a1, b1 = g1[li % NB]
                        a2, b2 = g2[li % NB]
                        li += 1
                        kb.dma("sp", a1, gl[j][0][:, c0:c0 + 2048], writes=[b1])
                        kb.dma("sp", a2, gl[j][1][:, c0:c0 + 2048], writes=[b2])
                        at1, bt1 = t1[ti % 2]
                        at2, bt2 = t2[ti % 2]
                        ti += 1
                        kb.op("dve", lambda e, o=at1, a=a1, b=ax: e.tensor_tensor(out=o, in0=a, in1=b, op=ALU.mult),
                              [b1, bx], [bt1])
                        kb.op("dve", lambda e, o=at2, a=a2, b=ax: e.tensor_tensor(
                            out=o[:, 0:1536], in0=a[:, 0:1536], in1=b[:, 0:1536], op=ALU.mult), [b2, bx], dwrites=[bt2])
                        kb.op("pool", lambda e, o=at2, a=a2, b=ax: e.tensor_tensor(
                            out=o[:, 1536:2048], in0=a[:, 1536:2048], in1=b[:, 1536:2048], op=ALU.mult), [b2, bx], dwrites=[bt2])
                        for q in range(16):
                            bk, bkb = banks[q // 4]
                            sl = (q % 4) * 128
                            self.mm(bk[:, sl:sl + 128], at1[:, q * 128:(q + 1) * 128], FiB1, j == 0, False,
                                    [bt1, FiB1b], [bkb], inc=False)
                            self.mm(bk[:, sl:sl + 128], at2[:, q * 128:(q + 1) * 128], FiB2, False, j == nin - 1,
                                    [bt2, FiB2b], [bkb], inc=(q % 4 == 3 or q == 15))
                    for b4 in range(4):
                        bk, bkb = banks[b4]
                        off = (cq * 16 + b4 * 4) * 128
                        self.evac(bstf[:, off:off + 512], bk, [bkb], [bstb], eng="act")
                for ng in range(8):
                    b0, b0b = self.ps()
                    b1_, b1b = self.ps()
                    for nn in range(8):
                        n2 = ng * 8 + nn
                        for par, (bk, bkb) in enumerate(((b0, b0b), (b1_, b1b))):
                            for ri in range(2):
                                self.mm(bk[0:64, nn * 64:(nn + 1) * 64], FiA[par * 64:(par + 1) * 64, n2, ri, :],
                                        bst[par * 64:(par + 1) * 64, :, ri, n2], ri == 0, ri == 1,
                                        [FiAb, bstb], [bkb], inc=(nn == 7 and ri == 1))
                    for par, (bk, bkb) in enumerate(((b0, b0b), (b1_, b1b))):
                        kb.op("dve", lambda e, bk=bk, par=par, ng=ng, zt=zt, xg=xg: e.tensor_tensor(
                            out=zt[:, :, par, ng * 8:(ng + 1) * 8], in0=bk[0:64, :].rearrange("p (n c) -> p c n", n=8),
                            in1=xg[:, :, par, ng * 8:(ng + 1) * 8], op=ALU.mult), [bkb, xgb], dwrites=[ztb])
                kb.dma("act", dst[:, cg * 8192:(cg + 1) * 8192], zt.rearrange("p a b c -> p (a b c)"), reads=[ztb])

    def inproj_fm(self, xTname, tok0, ntok, co, wst, xts, consume):
        A, kb, d = self.A, self.kb, self.d
        (a32, b32), (awb, bwb) = wst
        kb.dma("sp", a32, d["w_in"][:, co:co + 128].rearrange("(k p) c -> p k c", p=128), writes=[b32])
        kb.op("pool", lambda e: e.tensor_copy(out=awb, in_=a32), [b32], [bwb])
        xT = d[xTname]
        for ch in range(ntok // 512):
            ax, bx = xts[self.xi % len(xts)]
            self.xi += 1
            kb.dma("sp", ax, xT[:, :, tok0 + ch * 512: tok0 + (ch + 1) * 512].rearrange("k p t -> p k t"), writes=[bx])
            bank, bankb = self.ps()
            for kc in range(8):
                self.mm(bank, awb[:, kc, :], ax[:, kc, :], kc == 0, kc == 7, [bwb, bx], [bankb], inc=(kc == 7))
            consume(ch * 512, 512, bank, bankb)

    def rms_gate(self, val, valb, gate, gateb, gcol, gcolb):
        A, kb = self.A, self.kb
        sq = [A.alloc([128, 4, 512], BF16, f"sq_{i}") for i in range(2)]
        rs = [A.alloc([128, 512], F32, f"rs_{i}") for i in range(2)]
        tm = [A.alloc([128, 512], F32, f"tm_{i}") for i in range(4)]
        epsc, epsb = A.alloc([128, 1], F32, "epsc")
        kb.op("pool", lambda e: e.memset(epsc, RMS_EPS), [], [epsb])
        outb = []
        ti = 0
        for ch in range(T // 512):
            asq, bsq = sq[ch % 2]
            ars, brs = rs[ch % 2]
            cb_ = Buf(f"rmsch{ch}")
            outb.append(cb_)
            sl = slice(ch * 512, (ch + 1) * 512)
            kb.op("pool", lambda e, asq=asq, sl=sl: e.tensor_tensor(out=asq, in0=val[:, :, sl], in1=val[:, :, sl], op=ALU.mult),
                  [valb], [bsq])
            bank, bankb = self.ps()
            for hp in range(4):
                self.mm(bank, self.ones, asq[:, hp, :], hp == 0, hp == 3, [self.onesb, bsq], [bankb], inc=(hp == 3))
            kb.op("act", lambda e, ars=ars, bank=bank: e.activation(out=ars, in_=bank, func=AF.Sqrt, bias=epsc, scale=1.0 / 512.0),
                  [bankb, epsb], [brs])
            kb.op("dve", lambda e, ars=ars: e.reciprocal(out=ars, in_=ars), [brs], [brs])
            for hp in range(4):
                atm, btm = tm[ti % 4]
                ti += 1
                kb.op("dve", lambda e, atm=atm, hp=hp, sl=sl, ars=ars: e.scalar_tensor_tensor(
                    out=atm, in0=val[:, hp, sl], scalar=gcol[:, hp:hp + 1], in1=ars, op0=ALU.mult, op1=ALU.mult),
                    [valb, gcolb, brs], [btm])
                kb.op("pool", lambda e, atm=atm, hp=hp, sl=sl: e.tensor_tensor(
                    out=val[:, hp, sl], in0=atm, in1=gate[:, hp, sl], op=ALU.mult), [btm, gateb, bsq], [cb_])
        return outb

    def _att_job(self, g, r_):
        kb = self.kb
        st = {}

        def A_():
            kind, J, qlo, qhi, rho, pidx, hp = g["kind"], g["J"], g["qlo"], g["qhi"], g["rho"], g["pidx"], g["hp"]
            koff, n_r = g["koff"], g["n_r"]
            wq = qhi - qlo
            if kind == "main":
                nk, k0 = 128, koff + 128 * J
                col0 = qlo - (128 * J - 64)
                Et = r_["E"][:, pidx, 2 * hp:2 * hp + 2, col0:col0 + wq]
                Etb = r_["Eb"]
            elif kind == "left":
                nk, k0 = 64, koff - 64
                Et = r_["EL"][:, pidx, 2 * hp:2 * hp + 2, 0:wq]
                Etb = r_["ELb"]
            else:
                nk, k0 = 64, koff + n_r
                Et = r_["ER"][:, pidx, 2 * hp:2 * hp + 2, 0:wq]
                Etb = r_["ERb"]
            QTv, KTv, VTv = r_["QTv"], r_["KTv"], r_["VTv"]
            s0, s0b = self.ps("tmp")
            s1, s1b = self.ps("tmp")
            self.mm(s0[0:nk, 0:wq], KTv[0:64, k0:k0 + nk, rho], QTv[0:64, qlo:qhi, rho], True, True,
                    [r_["KTb"], r_["QTb"]], [s0b], inc=True)
            self.mm(s1[0:nk, 0:wq], KTv[64:128, k0:k0 + nk, rho], QTv[64:128, qlo:qhi, rho], True, True,
                    [r_["KTb"], r_["QTb"]], [s1b], inc=True)
            i = self.att_i
            self.att_i += 1
            ape, bpe = r_["pes"][i % 6]
            apm, bpm = r_["pms"][i % 6]
            avz, bvz = r_["vzs"][i % 6]
            kb.op("act", lambda e: e.activation(out=ape[0:nk, 0, 0:wq], in_=s0[0:nk, 0:wq], func=AF.Exp, scale=0.125),
                  [s0b], [bpe])
            kb.op("act", lambda e: e.activation(out=ape[0:nk, 1, 0:wq], in_=s1[0:nk, 0:wq], func=AF.Exp, scale=0.125),
                  [s1b], [bpe])
            kb.op("pool", lambda e: e.tensor_tensor(out=apm[0:nk, :, 0:wq], in0=ape[0:nk, :, 0:wq], in1=Et, op=ALU.mult),
                  [bpe, Etb], [bpm])
            tb, tbb = self.ps("tmp")
            tbf = tb.bitcast(BF16)
            self.tr(tbf[0:nk, 0:128], VTv[:, k0:k0 + nk, rho], self.ident, [r_["VTb"], self.identb], [tbb], inc=True)
            kb.op("dve", lambda e: e.tensor_copy(out=avz[0:nk, 0, 0:64], in_=tbf[0:nk, 0:64]), [tbb], [bvz])
            kb.op("dve", lambda e: e.tensor_copy(out=avz[0:nk, 1, 64:128], in_=tbf[0:nk, 64:128]), [tbb], [bvz])
            st.update(nk=nk, wq=wq, apm=apm, bpm=bpm, avz=avz, bvz=bvz)

        def B_():
            ch = g["chunk"]
            c0, nq, qlo, qhi, rho = g["c0"], g["nq"], g["qlo"], g["qhi"], g["rho"]
            if g["first"]:
                ch["num"] = self.ps("acc")
                ch["den"] = self.ps("acc")
            numb, numbb = ch["num"]
            denb, denbb = ch["den"]
            nk, wq, apm, bpm, avz, bvz = st["nk"], st["wq"], st["apm"], st["bpm"], st["avz"], st["bvz"]
            onz, onzb = r_["onz"], r_["onzb"]
            for hs in range(2):
                self.mm(numb[:, qlo - c0:qhi - c0], avz[0:nk, hs, :], apm[0:nk, hs, 0:wq], False, False,
                        [bvz, bpm], [numbb], inc=False)
                self.mm(denb[:, qlo - c0:qhi - c0], onz[0:nk, hs, :], apm[0:nk, hs, 0:wq], False, False,
                        [onzb, bpm], [denbb], inc=(hs == 1))
            if g["last"]:
                NUMv, DENv, NUMb, DENb = r_["NUMv"], r_["DENv"], r_["NUMb"], r_["DENb"]
                if g["first_pat"]:
                    kb.op("act", lambda e: e.activation(out=NUMv[:, c0:c0 + nq, rho], in_=numb[:, 0:nq], func=AF.Copy),
                          [numbb], [NUMb])
                    kb.op("dve", lambda e: e.tensor_copy(out=DENv[:, c0:c0 + nq, rho], in_=denb[:, 0:nq]),
                          [denbb], [DENb])
                else:
                    kb.op("dve", lambda e: e.tensor_tensor(out=NUMv[:, c0:c0 + nq, rho], in0=numb[:, 0:nq],
                                                            in1=NUMv[:, c0:c0 + nq, rho], op=ALU.add), [numbb, NUMb], [NUMb])
                    kb.op("dve", lambda e: e.tensor_tensor(out=DENv[:, c0:c0 + nq, rho], in0=denb[:, 0:nq],
                                                            in1=DENv[:, c0:c0 + nq, rho], op=ALU.add), [denbb, DENb], [DENb])
        return (A_, B_)

    def phase_att(self, xTname, off, next_, halo, Ename, ELname, ERname, MAname):
        A, kb, d = self.A, self.kb, self.d
        self.xi = 0
        E, Eb = A.alloc([128, 3, 8, 256], BF16, "E")
        kb.dma("sp", E.rearrange("p a b c -> p (a b c)"), d[Ename], writes=[Eb])
        if halo:
            EL, ELb = A.alloc([64, 3, 8, 64], BF16, "EL")
            kb.dma("sp", EL.rearrange("p a b c -> p (a b c)"), d[ELname], writes=[ELb])
            ER, ERb = A.alloc([64, 3, 8, 64], BF16, "ER")
            kb.dma("sp", ER.rearrange("p a b c -> p (a b c)"), d[ERname], writes=[ERb])
        onz, onzb = A.alloc([128, 2, 128], BF16, "onz")
        kb.dma("sp", onz.rearrange("p a b -> p (a b)"), d["onesz"], writes=[onzb])
        gcol, gcolb = A.alloc([128, 4], F32, "gcol")
        kb.dma("sp", gcol, d["attn_norm_g"].rearrange("(h p) -> p h", p=128), writes=[gcolb], slow=True)
        ATT, ATTb = A.alloc([128, 4, T], BF16, "ATT")
        GA, GAb = A.alloc([128, 4, T], BF16, "GA")
        QT, QTb = A.alloc([128, T], BF16, "QT")
        KT, KTb = A.alloc([128, next_], BF16, "KT")
        VT, VTb = A.alloc([128, next_], BF16, "VT")
        NUM, NUMb = A.alloc([128, T], F32, "NUM")
        DEN, DENb = A.alloc([128, T], F32, "DEN")
        w32 = A.alloc([128, 8, 128], F32, "w32")
        wbs = A.alloc([128, 8, 128], BF16, "wbs")
        xts = [A.alloc([128, 8, 512], BF16, f"xt_{i}") for i in range(2)]
        pes = [A.alloc([128, 2, 256], BF16, f"pe_{i}") for i in range(6)]
        pms = [A.alloc([128, 2, 256], BF16, f"pm_{i}") for i in range(6)]
        vzs = [A.alloc([128, 2, 128], BF16, f"vz_{i}") for i in range(6)]
        self.att_i = 0
        for (avz, bvz) in vzs:
            kb.op("pool", lambda e, avz=avz: e.memset(avz, 0.0), [], [bvz])
        for hp in range(4):
            def to_tile(dstt, dstb, base, fn=None):
                def consume(c0, n, bank, bankb):
                    if fn is None:
                        self.evac(dstt[:, base + c0: base + c0 + n], bank, [bankb], [dstb])
                    else:
                        kb.op("act", lambda e: e.activation(out=dstt[:, base + c0: base + c0 + n], in_=bank, func=fn),
                              [bankb], dwrites=[dstb])
                return consume
            self.inproj_fm(xTname, off, T, 0 + hp * 128, (w32, wbs), xts, to_tile(QT, QTb, 0))
            self.inproj_fm(xTname, 0, next_, 512 + hp * 128, (w32, wbs), xts, to_tile(KT, KTb, 0))
            self.inproj_fm(xTname, 0, next_, 1024 + hp * 128, (w32, wbs), xts, to_tile(VT, VTb, 0))
            self.inproj_fm(xTname, off, T, 1536 + hp * 128, (w32, wbs), xts, to_tile(GA[:, hp, :], GAb, 0, AF.Silu))
            jobs = []
            for pidx, (_, r) in enumerate(PATTERNS):
                n_r = T // r
                ntile = n_r // 128
                QTv = QT.rearrange("p (i r) -> p i r", r=r)
                KTv = KT.rearrange("p (i r) -> p i r", r=r)
                VTv = VT.rearrange("p (i r) -> p i r", r=r)
                NUMv = NUM.rearrange("p (i r) -> p i r", r=r)
                DENv = DEN.rearrange("p (i r) -> p i r", r=r)
                koff = off // r
                for rho in range(r):
                    for c0 in range(0, n_r, 512):
                        nq = min(512, n_r - c0)
                        segs = []
                        for J in range(c0 // 128 - 1, (c0 + nq) // 128 + 1):
                            if 0 <= J < ntile:
                                kind = "main"
                            elif halo and J == -1:
                                kind = "left"
                            elif halo and J == ntile:
                                kind = "right"
                            else:
                                continue
                            wlo, whi = 128 * J - 64, 128 * J + 192
                            if kind == "left":
                                wlo, whi = 0, 64
                            if kind == "right":
                                wlo, whi = n_r - 64, n_r
                            qlo, qhi = max(wlo, c0), min(whi, c0 + nq)
                            if qhi > qlo:
                                segs.append((kind, J, qlo, qhi))
                        cover = [0] * (nq // 64)
                        for (_, _, qlo, qhi) in segs:
                            for b_ in range((qlo - c0) // 64, (qhi - c0) // 64):
                                cover[b_] += 1
                        assert all(c_ > 0 for c_ in cover), cover
                        chunk = {}
                        for si, (kind, J, qlo, qhi) in enumerate(segs):
                            jobs.append(self._att_job(
                                dict(kind=kind, J=J, qlo=qlo, qhi=qhi, c0=c0, nq=nq, rho=rho, pidx=pidx, hp=hp,
                                     koff=koff, n_r=n_r, first=(si == 0), last=(si == len(segs) - 1),
                                     first_pat=(pidx == 0), chunk=chunk),
                                dict(QTv=QTv, KTv=KTv, VTv=VTv, NUMv=NUMv, DENv=DENv, QTb=QTb, KTb=KTb, VTb=VTb,
                                     NUMb=NUMb, DENb=DENb, E=E, Eb=Eb, EL=EL if halo else None, ELb=ELb if halo else None,
                                     ER=ER if halo else None, ERb=ERb if halo else None, onz=onz, onzb=onzb,
                                     pes=pes, pms=pms, vzs=vzs)))
            SK = 4
            for i in range(len(jobs) + SK):
                if i < len(jobs):
                    jobs[i][0]()
                if i - SK >= 0:
                    jobs[i - SK][1]()
            kb.op("dve", lambda e: e.reciprocal(out=DEN, in_=DEN), [DENb], [DENb])
            kb.op("dve", lambda e, hp=hp: e.tensor_tensor(out=ATT[:, hp, :], in0=NUM, in1=DEN, op=ALU.mult),
                  [NUMb, DENb], dwrites=[ATTb])
        ob = self.rms_gate(ATT, ATTb, GA, GAb, gcol, gcolb)
        kb.dma("act", d[MAname], ATT.rearrange("p a b -> p (a b)"), reads=[ATTb] + ob)

    def phase_mh(self, xTname, off, Z2name, MHname):
        A, kb, d = self.A, self.kb, self.d
        self.xi = 0
        ZF, ZFb = A.alloc([128, 4, T], BF16, "ZF")
        GH, GHb = A.alloc([128, 4, T], BF16, "GH")
        gcol, gcolb = A.alloc([128, 4], F32, "gcolh")
        kb.dma("sp", gcol, d["hyena_norm_g"].rearrange("(h p) -> p h", p=128), writes=[gcolb], slow=True)
        w32 = A.alloc([128, 8, 128], F32, "w32")
        wbs = A.alloc([128, 8, 128], BF16, "wbs")
        xts = [A.alloc([128, 8, 512], BF16, f"xt_{i}") for i in range(2)]
        zts = [A.alloc([64, 128, 64], BF16, f"ztm_{i}") for i in range(2)]
        ZFv = ZF.rearrange("p a (n m) -> p a n m", m=64)
        for cb in range(4):
            zt, ztb = zts[cb % 2]
            kb.dma("sp", zt.rearrange("p a b -> p (a b)"), d[Z2name][:, cb * 8192:(cb + 1) * 8192], writes=[ztb])
            for ng in range(4):
                bank, bankb = self.ps()
                bbf = bank.bitcast(BF16)
                for nn in range(16):
                    n2 = ng * 16 + nn
                    self.tr(bbf[:, nn * 64:(nn + 1) * 64], zt[:, :, n2], self.ident[0:64, 0:64], [ztb, self.identb], [bankb],
                            inc=(nn == 15))
                self.evac(ZFv[:, cb, :, ng * 16:(ng + 1) * 16], bbf.rearrange("p (n m) -> p m n", n=16), [bankb], [ZFb])

            def consume(c0, n, bank, bankb, cb=cb):
                kb.op("act", lambda e: e.activation(out=GH[:, cb, c0:c0 + n], in_=bank, func=AF.Silu), [bankb], dwrites=[GHb])
            self.inproj_fm(xTname, off, T, 3584 + cb * 128, (w32, wbs), xts, consume)
        ob = self.rms_gate(ZF, ZFb, GH, GHb, gcol, gcolb)
        kb.dma("act", d[MHname], ZF.rearrange("p a b -> p (a b)"), reads=[ZFb] + ob)

    def phase_out(self, xname, xrow0, MAname, MHname, yname):
        A, kb, d = self.A, self.kb, self.d
        MA, MAb = A.alloc([128, 4, T], BF16, "MA")
        kb.dma("sp", MA.rearrange("p a b -> p (a b)"), d[MAname], writes=[MAb])
        ZF, ZFb = A.alloc([128, 4, T], BF16, "MH")
        kb.dma("sp", ZF.rearrange("p a b -> p (a b)"), d[MHname], writes=[ZFb])
        wo, wob = A.alloc([128, 8, 1024], BF16, "wo")
        wst = [A.alloc([128, 1024], F32, f"wst_{i}") for i in range(2)]
        for kc in range(8):
            a, b = wst[kc % 2]
            kb.dma("sp", a, d["w_out"][kc * 128:(kc + 1) * 128, :], writes=[b])
            kb.op("pool", lambda e, a=a, kc=kc: e.tensor_copy(out=wo[:, kc, :], in_=a), [b], [wob])
        lg, lgb = A.alloc([128, 1024], F32, "lg")
        kb.dma("sp", lg, d["ln_g"].partition_broadcast(128), writes=[lgb])
        lb, lbb = A.alloc([128, 1024], F32, "lb")
        kb.dma("sp", lb, d["ln_b"].partition_broadcast(128), writes=[lbb])
        epsc, epsb = A.alloc([128, 1], F32, "epsl")
        kb.op("pool", lambda e: e.memset(epsc, LN_EPS), [], [epsb])
        xin = [A.alloc([128, 1024], F32, f"xin_{i}") for i in range(4)]
        hs = [A.alloc([128, 1024], F32, f"h_{i}") for i in range(2)]
        ys = [A.alloc([128, 1024], F32, f"y_{i}") for i in range(2)]
        sts = [A.alloc([128, 2, 6], F32, f"st_{i}") for i in range(2)]
        mvs = [A.alloc([128, 2], F32, f"mv_{i}") for i in range(2)]
        x, y = d[xname], d[yname]

        def ldx(t):
            if t < T // 128:
                kb.dma("sp", xin[t % 4][0], x[xrow0 + t * 128: xrow0 + (t + 1) * 128, :], writes=[xin[t % 4][1]])
        ldx(0)
        ldx(1)
        for t in range(T // 128):
            ax, bx = xin[t % 4]
            ah, bh = hs[t % 2]
            ay, by = ys[t % 2]
            ast_, bst_ = sts[t % 2]
            amv, bmv = mvs[t % 2]
            ldx(t + 2)
            for half in range(2):
                bank, bankb = self.ps()
                for kc in range(8):
                    src, srcb = (MA, MAb) if kc < 4 else (ZF, ZFb)
                    self.mm(bank, src[:, kc % 4, t * 128:(t + 1) * 128], wo[:, kc, half * 512:(half + 1) * 512],
                            kc == 0, kc == 7, [srcb, wob], [bankb], inc=(kc == 7))
                kb.op("dve", lambda e, ah=ah, ax=ax, bank=bank, half=half: e.scalar_tensor_tensor(
                    out=ah[:, half * 512:(half + 1) * 512], in0=ax[:, half * 512:(half + 1) * 512], scalar=ALPHA, in1=bank,
                    op0=ALU.mult, op1=ALU.add), [bx, bankb], [bh])
                kb.op("dve", lambda e, ast_=ast_, ah=ah, half=half: e.bn_stats(out=ast_[:, half, :], in_=ah[:, half * 512:(half + 1) * 512]),
                      [bh], [bst_])
            kb.op("dve", lambda e, amv=amv, ast_=ast_: e.bn_aggr(out=amv, in_=ast_.rearrange("p a b -> p (a b)")), [bst_], [bmv])
            kb.op("act", lambda e, amv=amv: e.activation(out=amv[:, 1:2], in_=amv[:, 1:2], func=AF.Sqrt, bias=epsc, scale=1.0),
                  [bmv, epsb], [bmv])
            kb.op("dve", lambda e, amv=amv: e.reciprocal(out=amv[:, 1:2], in_=amv[:, 1:2]), [bmv], [bmv])
            kb.op("dve", lambda e, ay=ay, ah=ah, amv=amv: e.tensor_scalar(out=ay, in0=ah, scalar1=amv[:, 0:1], scalar2=amv[:, 1:2],
                                                                        op0=ALU.subtract, op1=ALU.mult), [bh, bmv], [by])
            kb.op("pool", lambda e, ay=ay: e.tensor_tensor(out=ay, in0=ay, in1=lg, op=ALU.mult), [by, lgb], [by])
            kb.op("pool", lambda e, ay=ay: e.tensor_tensor(out=ay, in0=ay, in1=lb, op=ALU.add), [by, lbb], [by])
            kb.dma("act", y[t * 128:(t + 1) * 128, :], ay, reads=[by])

    def build(self, kb):
        self.kb = kb
        d = self.d
        ph = self.phases
        self.setup()
        if "xtp" in ph:
            self.phase_xt("xp", "xT_P", SEQ); self.reset()
        if "xts" in ph:
            self.phase_xt("xsf", "xT_SF", DEC_SEQ); self.reset()
            self.phase_xt("xso", "xT_SO", EXT); self.reset()
        if "h1p" in ph:
            gP = []
            for k in range(4):
                gP.append((2048 + k * 128, k * 128, d["UT0_P"][0], k))
            for k in range(4):
                gP.append((2560 + k * 128, 512 + k * 128, d["X1T_P"][0], k))
            for k in range(4):
                gP.append((3072 + k * 128, 1024 + k * 128, d["X2T_P"], k))
            self.phase_h1("xT_P", 0, True, True, gP); self.reset()
        if "h1s" in ph:
            for b in range(4):
                g = []
                for k in range(4):
                    g.append((2048 + k * 128, k * 128, d["UT0_S"][b], k))
                for k in range(4):
                    g.append((2560 + k * 128, 512 + k * 128, d["X1T_S"][b], k))
                self.phase_h1("xT_SF", b * T, b == 0, b == 3, g); self.reset()
            g = [(3072 + k * 128, 1024 + k * 128, d["X2T_S"], k) for k in range(4)]
            self.phase_h1("xT_SO", HALO, False, False, g); self.reset()
        slots = []
        if "hfp" in ph:
            slots.append((0, [(0, d["G_P"][0][0], d["G_P"][0][1]), (1, d["G_P"][1][0], d["G_P"][1][1])]))
        if "hfs" in ph:
            for i in range(7):
                slots.append((1 + i, [(0, d["G_S1"][i][0], d["G_S1"][i][1])]))
            for j in range(4):
                slots.append((8 + j, [(1, d["G_S2"][j][0], d["G_S2"][j][1])]))
        if slots:
            self.phase_hmlp([sl_[0] for sl_ in slots]); self.reset()
            self.phase_hf(slots); self.reset()
        if "hyp" in ph:
            self.phase_hx([d["UT0_P"][0]], [d["XS_P"][0]]); self.reset()
            self.phase_hp([([(d["G_P"][0][0], d["G_P"][0][1])], d["X1T_P"][0], d["Z1T_P"][0])], [d["XS_P"][0]], 1); self.reset()
            self.phase_hx([d["Z1T_P"][0]], [d["XS_P"][0]]); self.reset()
            self.phase_hp([([(d["G_P"][1][0], d["G_P"][1][1])], d["X2T_P"], d["Z2T_P"])], [d["XS_P"][0]], 1); self.reset()
        if "hys" in ph:
            self.phase_hx([d["UT0_S"][j] for j in range(4)], [d["XS_S"][j] for j in range(4)]); self.reset()
            outs = []
            for i in range(4):
                gl = [(d["G_S1"][i - j + 3][0], d["G_S1"][i - j + 3][1]) for j in range(4)]
                outs.append((gl, d["X1T_S"][i], d["Z1T_S"][i]))
            self.phase_hp(outs, [d["XS_S"][j] for j in range(4)], 4); self.reset()
            self.phase_hx([d["Z1T_S"][j] for j in range(4)], [d["XS_S"][j] for j in range(4)]); self.reset()
            gl = [(d["G_S2"][j][0], d["G_S2"][j][1]) for j in range(4)]
            self.phase_hp([(gl, d["X2T_S"], d["Z2T_S"])], [d["XS_S"][j] for j in range(4)], 4); self.reset()
        if "attp" in ph:
            self.phase_att("xT_P", 0, SEQ, False, "E_P", None, None, "MA_P"); self.reset()
        if "atts" in ph:
            self.phase_att("xT_SO", HALO, EXT, True, "E_P", "EL_S", "ER_S", "MA_S"); self.reset()
        if "outp" in ph:
            self.phase_mh("xT_P", 0, "Z2T_P", "MH_P"); self.reset()
            self.phase_out("xp", 0, "MA_P", "MH_P", "yp"); self.reset()
        if "outs" in ph:
            self.phase_mh("xT_SO", HALO, "Z2T_S", "MH_S"); self.reset()
            self.phase_out("xso", HALO, "MA_S", "MH_S", "ys"); self.reset()


ALL_PHASES = ("xtp", "xts", "h1p", "h1s", "hfp", "hfs", "hyp", "hys", "attp", "atts", "outp", "outs")


def build_program(phases=ALL_PHASES, extra=None, ext_in=(), ext_out=()):
    nc = bass.Bass("TRN2", target_bir_lowering=False)
    st = ExitStack()
    P = Prog(nc, st, phases, ext_in, ext_out)
    P.declare()
    if extra is not None:
        extra(P)
    words = 53000
    ar = st.enter_context(nc.sbuf_tensor("arena", [128, words], F32))
    P.A = Arena(ar, words)
    P.bank = []
    P.bankb = []
    for i in range(8):
        P.bank.append(st.enter_context(nc.psum_tensor(f"pb{i}", [128, 512], F32))[:, :])
        P.bankb.append(Buf(f"pb{i}"))
    kb = KB(nc)
    kb.run(st, P.build)
    st.close()
    return nc, P


_CONST_CACHE = {}


def const_inputs():
    if "c" in _CONST_CACHE:
        return _CONST_CACHE["c"]
    c = dict(fft_consts())
    c["ident"] = np.eye(128, dtype=np.float32).astype(NPBF)
    c["ones"] = np.ones((128, 128), np.float32).astype(NPBF)
    oz = np.zeros((128, 2, 128), np.float32)
    oz[:, 0, :64] = 1.0
    oz[:, 1, 64:] = 1.0
    c["onesz"] = oz.reshape(128, 256).astype(NPBF)
    c["delta"] = decay_deltas()
    _CONST_CACHE["c"] = c
    return c


def core_inputs(core, inputs):
    c = dict(const_inputs())
    sb, blk = core // 4, core % 4
    f32 = lambda a: np.ascontiguousarray(np.asarray(a, dtype=np.float32))
    c["xp"] = f32(inputs["x_prompt"][core])
    xs = np.asarray(inputs["x_sample"][sb], dtype=np.float32)
    c["xsf"] = np.ascontiguousarray(xs)
    ext = np.zeros((EXT, D_MODEL), np.float32)
    lo, hi = blk * T - HALO, (blk + 1) * T + HALO
    slo, shi = max(lo, 0), min(hi, DEC_SEQ)
    ext[slo - lo: shi - lo] = xs[slo:shi]
    c["xso"] = ext
    for k in ("w_in", "w_out", "conv_w", "conv_b", "filt_w1", "filt_b1", "filt_w2", "filt_b2", "filt_w3",
              "filt_b3", "filt_freq", "filt_w4", "hyena_d", "attn_norm_g", "hyena_norm_g", "ln_g", "ln_b"):
        c[k] = f32(inputs[k][0])
    lags = [(SEQ, 0)] + [(DEC_SEQ, dd) for dd in range(-3, 4)] + [(DEC_SEQ, blk - j) for j in range(4)]
    key = ("slots", blk)
    if key not in _CONST_CACHE:
        zT = np.zeros((12, 33, NFFT), np.float32)
        sel = np.zeros((12, 128, NFFT), NPBF)
        ntl = np.zeros((12, 128, 64), np.float32)
        flag = np.zeros((1, 12), np.float32)
        for s, (L, dd) in enumerate(lags):
            k2 = ("slot", L, dd)
            if k2 not in _CONST_CACHE:
                _CONST_CACHE[k2] = filter_slot_tables(L, dd)
            zT[s], sel[s], ntl[s], flag[0, s] = _CONST_CACHE[k2]
        _CONST_CACHE[key] = (zT, sel, ntl, flag)
    c["f_zT"], c["f_sel"], c["f_ntl"], c["f_flag"] = _CONST_CACHE[key]
    ek = ("E", blk)
    if ek not in _CONST_CACHE:
        _CONST_CACHE[ek] = attn_tables(blk > 0, blk < 3)
    c["E_P"], c["EL_S"], c["ER_S"] = _CONST_CACHE[ek]
    return c


_PROG = {}


def kernel(**inputs):
    if "nc" not in _PROG:
        _PROG["nc"] = build_program()[0]
    nc = _PROG["nc"]
    in_maps = [core_inputs(core, inputs) for core in range(8)]
    res = run_bass_kernel_spmd(nc, in_maps, core_ids=list(range(8)))
    yp = np.stack([np.asarray(res.results[c]["yp"], dtype=np.float32) for c in range(8)], axis=0)
    ys = np.zeros((2, DEC_SEQ, D_MODEL), np.float32)
    for c in range(8):
        ys[c // 4, (c % 4) * T:(c % 4 + 1) * T] = np.asarray(res.results[c]["ys"], dtype=np.float32)
    return (yp, ys)
```

```python
import math
from contextlib import ExitStack

import numpy as np
import ml_dtypes

import concourse.bass as bass
import concourse.mybir as mybir
from concourse.bass_utils import run_bass_kernel_spmd

F32 = mybir.dt.float32
BF16 = mybir.dt.bfloat16
AF = mybir.ActivationFunctionType
ALU = mybir.AluOpType
NPBF = ml_dtypes.bfloat16

SEM_EPOCH = 30000
TWO_PI = 2.0 * math.pi

D_MODEL = 1024
SEQ = 4096
DEC_SEQ = 16384
NCH = 512
T = 4096
NFFT = 8192
HALO = 1024
EXT = T + 2 * HALO
PATTERNS = ((128, 1), (512, 4), (2048, 16))
LN_EPS = 1e-5
RMS_EPS = 1e-6
ALPHA = 2.0 ** 0.25


class Buf:
    __slots__ = ("name", "w", "r")

    def __init__(self, name):
        self.name = name
        self.w = {}
        self.r = {}


class Eng:
    def __init__(self, kb, name, handle):
        self.kb = kb
        self.name = name
        self.h = handle
        self.ops = []
        self.sem = None
        self.cnt = 0
        self.waited = {}
        self.last_tok = None
        self.pending = False

    def new_epoch(self):
        self.sem = self.kb.new_sem(self.name)
        self.cnt = 0


class KB:
    def __init__(self, nc, n_lanes=8):
        self.nc = nc
        self.sems = []
        self._stack = None
        self.eng = {}
        for name, h in (("pe", nc.tensor), ("act", nc.scalar), ("dve", nc.vector),
                        ("pool", nc.gpsimd), ("sp", nc.sync)):
            self.eng[name] = Eng(self, name, h)
        self.lanes = {}
        self.n_lanes = n_lanes
        self.lane_rr = {}

    def new_sem(self, name):
        cm = self.nc.semaphore(f"s{len(self.sems)}_{name}")
        s = self._stack.enter_context(cm)
        self.sems.append(s)
        return len(self.sems) - 1

    def _wait(self, e, tok):
        if tok is None:
            return
        sid, val = tok
        if e.waited.get(sid, 0) >= val:
            return
        e.waited[sid] = val
        sem = self.sems[sid]
        e.ops.append(lambda h, sem=sem, val=val: h.wait_ge(sem, val))

    def _deps(self, e, reads, writes, dwrites=()):
        toks = []
        for b in reads:
            toks.extend(b.w.items())
        for b in writes:
            toks.extend(b.w.items())
            toks.extend(b.r.items())
        for b in dwrites:
            toks.extend(b.r.items())
        for t in toks:
            if e.name == "pe" and e.sem is not None and t[0] == e.sem:
                continue
            self._wait(e, t)

    def _mark(self, tok, reads, writes, dwrites=()):
        sid, val = tok
        for b in list(writes) + list(dwrites):
            if b.w.get(sid, 0) < val:
                b.w[sid] = val
        for b in reads:
            if b.r.get(sid, 0) < val:
                b.r[sid] = val

    def op(self, eng, fn, reads=(), writes=(), inc=True, dwrites=()):
        e = self.eng[eng]
        if e.sem is None or e.cnt >= SEM_EPOCH:
            e.new_epoch()
        self._deps(e, reads, writes, dwrites)
        if inc:
            e.cnt += 1
            tok = (e.sem, e.cnt)
            sem = self.sems[e.sem]
            e.ops.append(lambda h, fn=fn, sem=sem: fn(h).then_inc(sem, 1))
            e.last_tok = tok
            e.pending = False
        else:
            tok = (e.sem, e.cnt + 1)
            e.ops.append(lambda h, fn=fn: fn(h))
            e.pending = True
        self._mark(tok, reads, writes, dwrites)
        return tok

    def dma(self, q, out, in_, reads=(), writes=(), slow=False, dwrites=()):
        e = self.eng[q]
        if q not in self.lanes:
            self.lanes[q] = [{"sem": None, "val": 0} for _ in range(self.n_lanes)]
            self.lane_rr[q] = 0
        ln = self.lanes[q][self.lane_rr[q] % self.n_lanes]
        self.lane_rr[q] += 1
        if ln["sem"] is None or ln["val"] >= 60000:
            ln["sem"] = self.new_sem(f"dma_{q}")
            ln["val"] = 0
        else:
            self._wait(e, (ln["sem"], ln["val"]))
        self._deps(e, reads, writes, dwrites)
        ln["val"] += 16
        tok = (ln["sem"], ln["val"])
        sem = self.sems[ln["sem"]]
        if slow:
            e.ops.append(lambda h, out=out, in_=in_, sem=sem:
                         h.dma_start(out=out, in_=in_, allow_slow_non_contiguous=True).then_inc(sem, 16))
        else:
            e.ops.append(lambda h, out=out, in_=in_, sem=sem: h.dma_start(out=out, in_=in_).then_inc(sem, 16))
        self._mark(tok, reads, writes, dwrites)
        return tok

    def all_tokens(self):
        toks = []
        for e in self.eng.values():
            assert not e.pending, f"engine {e.name} has a trailing non-inc op"
            if e.last_tok is not None:
                toks.append(e.last_tok)
        for lanes in self.lanes.values():
            for ln in lanes:
                if ln["sem"] is not None and ln["val"] > 0:
                    toks.append((ln["sem"], ln["val"]))
        return toks

    def barrier(self, engines=None):
        toks = self.all_tokens()
        for name, e in self.eng.items():
            if engines is not None and name not in engines:
                continue
            for t in toks:
                if e.sem is not None and t[0] == e.sem:
                    continue
                self._wait(e, t)

    def run(self, stack, build):
        self._stack = stack
        build(self)
        self.barrier()
        block = stack.enter_context(self.nc.Block())
        e = self.eng

        @block.sync
        def _(h):
            for f in e["sp"].ops:
                f(h)

        @block.tensor
        def _(h):
            for f in e["pe"].ops:
                f(h)

        @block.scalar
        def _(h):
            for f in e["act"].ops:
                f(h)

        @block.vector
        def _(h):
            for f in e["dve"].ops:
                f(h)

        @block.gpsimd
        def _(h):
            for f in e["pool"].ops:
                f(h)


class Arena:
    def __init__(self, handle, words):
        self.h = handle
        self.words = words
        self.top = 0

    def alloc(self, shape, dt, name="t"):
        free = int(np.prod(shape[1:]))
        nbytes = free * (2 if dt == BF16 else 4)
        w = (nbytes + 3) // 4
        w = (w + 7) // 8 * 8
        assert self.top + w <= self.words, f"arena overflow: {name} {shape} top={self.top} w={w}"
        ap = self.h[:, self.top:self.top + w]
        self.top += w
        if dt == BF16:
            ap = ap.bitcast(BF16)[:, 0:free]
        else:
            ap = ap[:, 0:free]
        if shape[0] < 128:
            ap = ap[0:shape[0], :]
        if len(shape) > 2:
            names = " ".join(f"d{i}" for i in range(len(shape) - 1))
            kw = {f"d{i}": shape[i + 1] for i in range(len(shape) - 1)}
            ap = ap.rearrange(f"p ({names}) -> p {names}", **kw)
        return ap, Buf(name)


def fft_consts():
    n1 = np.arange(128)[:, None]
    k1 = np.arange(64)[None, :]
    th = 2 * np.pi * n1 * (k1 + 0.5) / 128
    FA = np.concatenate([np.cos(th), -np.sin(th)], axis=1)
    n2 = np.arange(64)[:, None, None]
    k1b = np.arange(64)[None, :, None]
    k2 = np.arange(64)[None, None, :]
    ph = 2 * np.pi * (n2 * (k1b + 0.5) / 8192 + n2 * k2 / 64)
    wr = np.cos(ph)
    wi = -np.sin(ph)

    def mk(a0, a1, b0, b1):
        M = np.zeros((64, 64, 2, 128))
        M[:, :, 0, :64] = a0
        M[:, :, 0, 64:] = a1
        M[:, :, 1, :64] = b0
        M[:, :, 1, 64:] = b1
        M = M.reshape(64, -1)
        return np.concatenate([M, M], axis=0)

    FB = mk(wr, wi, -wi, wr)
    FBG1 = mk(wr, wr, -wi, -wi)
    FBG2 = mk(wi, wi, wr, wr)
    k2c = np.arange(64)[:, None]
    n2c = np.arange(64)[None, :]
    t = 2 * np.pi * n2c * k2c / 64
    c, s = np.cos(t), np.sin(t)
    FinvB1 = np.block([[c, s], [-s, c]])
    FinvB2 = np.block([[-s, c], [-c, -s]])
    k1a = np.arange(64)[:, None, None]
    n2a = np.arange(64)[None, :, None]
    n1a = np.arange(64)[None, None, :]
    phi = 2 * np.pi * (k1a + 0.5) * (64 * n1a + n2a) / 8192
    FinvA = np.zeros((64, 64, 2, 64))
    FinvA[:, :, 0, :] = (2.0 / NFFT) * np.cos(phi)
    FinvA[:, :, 1, :] = -(2.0 / NFFT) * np.sin(phi)
    FinvA = FinvA.reshape(64, -1)
    FinvA = np.concatenate([FinvA, FinvA], axis=0)
    bf = lambda a: np.ascontiguousarray(a.astype(np.float32).astype(NPBF))
    return dict(FA=bf(FA), FB=bf(FB), FBG1=bf(FBG1), FBG2=bf(FBG2), FinvB1=bf(FinvB1),
                FinvB2=bf(FinvB2), FinvA=bf(FinvA))


def filter_slot_tables(L, d):
    m = np.arange(NFFT)
    lam = np.where(m < T, d * T + m, d * T - (NFFT - m)).astype(np.int64)
    sign = np.where(m < T, 1.0, -1.0)
    sign[T] = 0.0
    tpos = np.abs(lam)
    valid = tpos < L
    sign = sign * valid
    tpos = np.minimum(tpos, L - 1)
    fwd = lam >= 0
    f32 = np.float32
    tl = np.linspace(0.0, 1.0, L, dtype=f32)
    bands = 16
    w = (f32(2.0 * math.pi) * np.arange(L, dtype=f32) / f32(L)).astype(f32)
    fr = np.linspace(1e-4, bands - 1, bands, dtype=f32)
    ang = (fr[None, :] * w[:, None]).astype(f32)
    z = np.concatenate([tl[:, None], np.cos(ang), -np.sin(ang)], axis=1).astype(f32)
    zT = np.ascontiguousarray(z[tpos].T.astype(f32))
    sel = np.zeros((128, NFFT), np.float32)
    sel[:64] = (sign * fwd)[None, :]
    sel[64:] = (sign * (~fwd))[None, :]
    ntl = (-tl[tpos]).astype(f32).reshape(128, 64)
    flag = 1.0 if d == 0 else 0.0
    return zT, sel.astype(NPBF), np.ascontiguousarray(ntl), flag


def decay_deltas():
    f32 = np.float32
    max_decay = math.log(1e-2) / 0.3
    min_decay = math.log(1e-2) / 1.5
    return np.abs(np.linspace(min_decay, max_decay, NCH, dtype=f32)).astype(f32)


def attn_tables(left_ok, right_ok):
    slopes = np.array([2.0 ** (-8.0 * (i + 1) / 8) for i in range(8)], np.float64)
    p = np.arange(128)[:, None]
    col = np.arange(256)[None, :]
    rel = np.abs(col - 64 - p)
    E = np.zeros((128, 3, 8, 256), np.float32)
    EL = np.zeros((64, 3, 8, 64), np.float32)
    ER = np.zeros((64, 3, 8, 64), np.float32)
    pk = np.arange(64)[:, None]
    q = np.arange(64)[None, :]
    relL = q + 64 - pk
    relR = pk + 64 - q
    for pi, (_, r) in enumerate(PATTERNS):
        for h in range(8):
            E[:, pi, h, :] = np.where(rel <= 64, np.exp(-slopes[h] * r * rel), 0.0)
            if left_ok:
                EL[:, pi, h, :] = np.where(relL <= 64, np.exp(-slopes[h] * r * relL), 0.0)
            if right_ok:
                ER[:, pi, h, :] = np.where(relR <= 64, np.exp(-slopes[h] * r * relR), 0.0)
    ELR = np.zeros((128, 3, 8, 64), np.float32)
    bf = lambda a: np.ascontiguousarray(a.astype(NPBF))
    return bf(E.reshape(128, -1)), bf(EL.reshape(64, -1)), bf(ER.reshape(64, -1))


class Prog:
    def __init__(self, nc, st, phases, ext_in=(), ext_out=()):
        self.ext_in = set(ext_in)
        self.ext_out = set(ext_out)
        self.nc = nc
        self.st = st
        self.phases = phases
        self.d = {}
        self.ps_i = 0
        self.ev_i = 0
        self.fresh = {}
        self.pools = {"acc": [0, 1], "tmp": [2, 3, 4, 5, 6, 7]}
        self.pool_i = {}

    def inp(self, name, shape, dt=F32):
        self.d[name] = self.nc.dram_tensor(name, list(shape), dt, kind="ExternalInput").ap()
        return self.d[name]

    def outp(self, name, shape, dt=F32):
        self.d[name] = self.nc.dram_tensor(name, list(shape), dt, kind="ExternalOutput").ap()
        return self.d[name]

    def scr(self, name, shape, dt=BF16):
        kind = "Internal"
        if name in self.ext_in:
            kind = "ExternalInput"
        if name in self.ext_out:
            kind = "ExternalOutput"
        self.d[name] = self.nc.dram_tensor(name, list(shape), dt, kind=kind).ap()
        return self.d[name]

    def ps(self, pool=None):
        if pool is None:
            i = self.ps_i % 8
            self.ps_i += 1
        else:
            lst = self.pools[pool]
            k = self.pool_i.get(pool, 0)
            self.pool_i[pool] = k + 1
            i = lst[k % len(lst)]
        self.fresh[self.bankb[i]] = True
        return self.bank[i], self.bankb[i]

    def evac(self, out, in_, reads, writes, eng=None, disjoint=True):
        kb = self.kb
        if eng is None:
            eng = ("act", "dve")[self.ev_i % 2]
            self.ev_i += 1
        w, dw = ((), writes) if disjoint else (writes, ())
        if eng == "act":
            kb.op("act", lambda e: e.activation(out=out, in_=in_, func=AF.Copy), reads, w, dwrites=dw)
        else:
            kb.op(eng, lambda e: e.tensor_copy(out=out, in_=in_), reads, w, dwrites=dw)

    def mm(self, out, lhsT, rhs, start, stop, reads, writes, inc):
        bb = writes[0]
        start = self.fresh.get(bb, False)
        self.fresh[bb] = False
        self.kb.op("pe", lambda e: e.matmul(out, lhsT=lhsT, rhs=rhs, start=start, stop=stop, skip_group_check=True),
                   reads, writes, inc=inc)

    def tr(self, out, in_, ident, reads, writes, inc):
        self.kb.op("pe", lambda e: e.transpose(out, in_, ident), reads, writes, inc=inc)

    def reset(self):
        self.kb.barrier()
        self.hiwater = max(getattr(self, "hiwater", 0), self.A.top)
        self.A.top = self.A_base

    def declare(self):
        inp, scr = self.inp, self.scr
        inp("xp", [SEQ, D_MODEL]); inp("xsf", [DEC_SEQ, D_MODEL]); inp("xso", [EXT, D_MODEL])
        inp("w_in", [D_MODEL, 4096]); inp("w_out", [D_MODEL, D_MODEL])
        inp("conv_w", [3, 1536]); inp("conv_b", [1536])
        inp("filt_w1", [33, 64]); inp("filt_b1", [64]); inp("filt_w2", [64, 64]); inp("filt_b2", [64])
        inp("filt_w3", [64, 64]); inp("filt_b3", [64]); inp("filt_freq", [3, 64]); inp("filt_w4", [64, 2048])
        inp("hyena_d", [2, NCH]); inp("attn_norm_g", [512]); inp("hyena_norm_g", [512])
        inp("ln_g", [D_MODEL]); inp("ln_b", [D_MODEL])
        inp("ident", [128, 128], BF16); inp("ones", [128, 128], BF16); inp("onesz", [128, 256], BF16)
        inp("FA", [128, 128], BF16); inp("FB", [128, 64 * 256], BF16)
        inp("FBG1", [128, 64 * 256], BF16); inp("FBG2", [128, 64 * 256], BF16)
        inp("FinvB1", [128, 128], BF16); inp("FinvB2", [128, 128], BF16); inp("FinvA", [128, 64 * 128], BF16)
        inp("f_zT", [12, 33, NFFT]); inp("f_sel", [12, 128, NFFT], BF16); inp("f_ntl", [12, 128, 64])
        inp("f_flag", [1, 12]); inp("delta", [NCH])
        inp("E_P", [128, 3 * 8 * 256], BF16)
        inp("EL_S", [64, 3 * 8 * 64], BF16); inp("ER_S", [64, 3 * 8 * 64], BF16)
        self.outp("yp", [SEQ, D_MODEL]); self.outp("ys", [T, D_MODEL])
        scr("xT_P", [8, 128, SEQ]); scr("xT_SF", [8, 128, DEC_SEQ]); scr("xT_SO", [8, 128, EXT])
        for nm, nb in (("P", 1), ("S", 4)):
            scr(f"UT0_{nm}", [nb, 64, NCH * 64]); scr(f"X1T_{nm}", [nb, 64, NCH * 64])
            scr(f"X2T_{nm}", [64, NCH * 64]); scr(f"Z1T_{nm}", [nb, 64, NCH * 64]); scr(f"Z2T_{nm}", [64, NCH * 64])
            scr(f"XS_{nm}", [nb, 128, NCH * 64])
            scr(f"MA_{nm}", [128, 4 * T]); scr(f"MH_{nm}", [128, 4 * T])
        scr("HAUG", [12, 128, NFFT])
        scr("G_P", [2, 2, 128, NCH * 64])
        scr("G_S1", [7, 2, 128, NCH * 64])
        scr("G_S2", [4, 2, 128, NCH * 64])

    def setup(self):
        A, kb, d = self.A, self.kb, self.d
        self.ident, self.identb = A.alloc([128, 128], BF16, "ident")
        kb.dma("sp", self.ident, d["ident"], writes=[self.identb])
        self.ones, self.onesb = A.alloc([128, 128], BF16, "ones")
        kb.dma("sp", self.ones, d["ones"], writes=[self.onesb])
        self.A_base = A.top

    def phase_xt(self, xname, xTname, ntok):
        A, kb, d = self.A, self.kb, self.d
        x, xT = d[xname], d[xTname]
        x32 = [A.alloc([128, 1024], F32, f"x32_{i}") for i in range(4)]
        xb = [A.alloc([128, 1024], BF16, f"xb_{i}") for i in range(4)]
        xT4 = [A.alloc([128, 8, 512], BF16, f"xT4_{i}") for i in range(2)]
        for t in range(ntok // 128):
            a32, b32 = x32[t % 4]
            ab, bb = xb[t % 4]
            kb.dma("sp", a32, x[t * 128:(t + 1) * 128, :], writes=[b32])
            kb.op("pool" if t % 4 == 3 else "dve", lambda e, o=ab, i=a32: e.tensor_copy(out=o, in_=i), [b32], [bb])
            bank, bankb = self.ps()
            bbf = bank.bitcast(BF16)
            for kc in range(8):
                self.tr(bbf[:, kc * 128:(kc + 1) * 128], ab[:, kc * 128:(kc + 1) * 128], self.ident,
                        [bb, self.identb], [bankb], inc=(kc == 7))
            g = t // 4
            q = t % 4
            a4, b4 = xT4[g % 2]
            self.evac(a4[:, :, q * 128:(q + 1) * 128], bbf.rearrange("p (k t) -> p k t", k=8), [bankb], [b4], eng="act")
            if q == 3:
                kb.dma("act", xT[:, :, g * 512:(g + 1) * 512].rearrange("k p t -> p k t"), a4, reads=[b4])

    def phase_h1(self, xTname, tok0, zero_l, zero_r, groups):
        A, kb, d = self.A, self.kb, self.d
        xT = d[xTname]
        xblk, xblkb = A.alloc([128, 8, T + 2], BF16, "xblk")
        src = xT[:, :, tok0:tok0 + T].rearrange("k p t -> p k t")
        kb.dma("sp", xblk[:, :, 1:T + 1], src, writes=[xblkb])
        if zero_l:
            kb.op("pool", lambda e: e.memset(xblk[:, :, 0:1], 0.0), [], [xblkb])
        else:
            kb.dma("sp", xblk[:, :, 0:1], xT[:, :, tok0 - 1:tok0].rearrange("k p t -> p k t"), writes=[xblkb], slow=True)
        if zero_r:
            kb.op("pool", lambda e: e.memset(xblk[:, :, T + 1:T + 2], 0.0), [], [xblkb])
        else:
            kb.dma("sp", xblk[:, :, T + 1:T + 2], xT[:, :, tok0 + T:tok0 + T + 1].rearrange("k p t -> p k t"),
                   writes=[xblkb], slow=True)
        w32 = [A.alloc([128, 8, 128], F32, f"w32_{i}") for i in range(3)]
        wb = [A.alloc([128, 8, 128], BF16, f"wb_{i}") for i in range(3)]
        cw = [A.alloc([128, 4], F32, f"cw_{i}") for i in range(3)]
        pfs = [A.alloc([128, T + 2], BF16, f"pf{i}") for i in range(2)]
        ufs = [A.alloc([128, T], F32, f"uf{i}") for i in range(2)]
        ubs = [A.alloc([128, T], BF16, f"ub{i}") for i in range(2)]
        utg = [A.alloc([64, 128, 64], BF16, f"utg_{i}") for i in range(2)]
        def load_w(gi):
            if gi >= len(groups):
                return
            co, cc, dst, dg = groups[gi]
            a32, b32 = w32[gi % 3]
            awb, bwb = wb[gi % 3]
            acw, bcw = cw[gi % 3]
            kb.dma("sp", a32, d["w_in"][:, co:co + 128].rearrange("(k p) c -> p k c", p=128), writes=[b32])
            kb.op("pool", lambda e, o=awb, i=a32: e.tensor_copy(out=o, in_=i), [b32], [bwb])
            kb.dma("sp", acw[:, 0:3], d["conv_w"][:, cc:cc + 128].rearrange("j c -> c j"), writes=[bcw], slow=True)
            kb.dma("sp", acw[:, 3:4], d["conv_b"][cc:cc + 128].rearrange("(c o) -> c o", o=1), writes=[bcw], slow=True)

        def stage1(gi):
            co, cc, dst, dg = groups[gi]
            pf, pfb = pfs[gi % 2]
            uf, ufb = ufs[gi % 2]
            ub, ubb = ubs[gi % 2]
            awb, bwb = wb[gi % 3]
            acw, bcw = cw[gi % 3]
            load_w(gi + 1)
            nchunk = (T + 2 + 511) // 512
            for ch in range(nchunk):
                c0 = ch * 512
                n = min(512, T + 2 - c0)
                bank, bankb = self.ps()
                for kc in range(8):
                    self.mm(bank[:, 0:n], awb[:, kc, :], xblk[:, kc, c0:c0 + n], kc == 0, kc == 7,
                            [bwb, xblkb], [bankb], inc=(kc == 7))
                self.evac(pf[:, c0:c0 + n], bank[:, 0:n], [bankb], [pfb], eng=("act", "act", "dve")[ch % 3])
            kb.op("act", lambda e: e.activation(out=uf, in_=pf[:, 1:T + 1], func=AF.Identity,
                                                bias=acw[:, 3:4], scale=acw[:, 1:2]), [pfb, bcw], [ufb])
            kb.op("dve", lambda e: e.scalar_tensor_tensor(out=uf, in0=pf[:, 0:T], scalar=acw[:, 0:1], in1=uf,
                                                          op0=ALU.mult, op1=ALU.add), [pfb, bcw, ufb], [ufb])
            kb.op("dve", lambda e: e.scalar_tensor_tensor(out=ub, in0=pf[:, 2:T + 2], scalar=acw[:, 2:3], in1=uf,
                                                          op0=ALU.mult, op1=ALU.add), [pfb, bcw, ufb], [ubb])

        def stage2(gi):
            co, cc, dst, dg = groups[gi]
            ub, ubb = ubs[gi % 2]
            ubv = ub.rearrange("p (a b) -> p a b", b=64)
            autg, butg = utg[gi % 2]
            for ng in range(8):
                bank, bankb = self.ps()
                bbf = bank.bitcast(BF16)
                for nn in range(8):
                    n2 = ng * 8 + nn
                    self.tr(bbf[0:64, nn * 128:(nn + 1) * 128], ubv[:, :, n2], self.ident,
                            [ubb, self.identb], [bankb], inc=(nn == 7))
                self.evac(autg[:, :, ng * 8:(ng + 1) * 8], bbf[0:64, :].rearrange("p (n c) -> p c n", n=8),
                          [bankb], [butg], eng=("act", "act", "dve")[ng % 3])
            kb.dma("act", dst[:, dg * 8192:(dg + 1) * 8192], autg.rearrange("p c n -> p (c n)"), reads=[butg])

        ng_ = len(groups)
        load_w(0)
        for gi in range(ng_ + 1):
            if gi < ng_:
                stage1(gi)
            if gi >= 1:
                stage2(gi - 1)

    def fwd_cg(self, ut, utb, K, FA, FAb, variants, ast, astb, emit):
        astv = ast.rearrange("p (c r k) -> p c r k", r=2, k=64)
        for g in range(16):
            bank, bankb = self.ps()
            for q in range(4):
                cp = g * 4 + q
                self.mm(bank[:, q * 128:(q + 1) * 128], ut[:, cp * 128:(cp + 1) * 128], FA[0:K, :], True, True,
                        [utb, FAb], [bankb], inc=(q == 3))
            self.evac(ast[:, g * 512:(g + 1) * 512], bank, [bankb], [astb])
        for v, (FBt, FBb) in enumerate(variants):
            def fill(Xv, Xb, FBt=FBt, FBb=FBb):
                for kg in range(8):
                    b0, b0b = self.ps()
                    b1, b1b = self.ps()
                    for kk in range(8):
                        k1 = kg * 8 + kk
                        for par, (bk, bkb) in enumerate(((b0, b0b), (b1, b1b))):
                            for ri in range(2):
                                self.mm(bk[:, kk * 64:(kk + 1) * 64], FBt[par * 64:(par + 1) * 64, k1, ri, :],
                                        astv[par * 64:(par + 1) * 64, :, ri, k1], ri == 0, ri == 1,
                                        [FBb, astb], [bkb], inc=(kk == 7 and ri == 1))
                    for par, (bk, bkb) in enumerate(((b0, b0b), (b1, b1b))):
                        self.evac(Xv[:, :, par, kg * 8:(kg + 1) * 8], bk.rearrange("p (k c) -> p c k", k=8), [bkb], [Xb])
            emit(v, fill)

    def phase_hx(self, srcs, dsts):
        A, kb, d = self.A, self.kb, self.d
        FA, FAb = A.alloc([128, 128], BF16, "FA")
        kb.dma("sp", FA, d["FA"], writes=[FAb])
        FB, FBb = A.alloc([128, 64, 2, 128], BF16, "FB")
        kb.dma("sp", FB.rearrange("p a b c -> p (a b c)"), d["FB"], writes=[FBb])
        uts = [A.alloc([64, 8192], BF16, f"ut_{i}") for i in range(2)]
        asts = [A.alloc([128, 8192], BF16, f"ast_{i}") for i in range(2)]
        xts = [A.alloc([128, 64, 2, 64], BF16, f"xt_{i}") for i in range(2)]
        it = 0
        for src, dst in zip(srcs, dsts):
            for cg in range(4):
                ut, utb = uts[it % 2]
                ast, astb = asts[it % 2]
                xt, xtb = xts[it % 2]
                it += 1
                kb.dma("sp", ut, src[:, cg * 8192:(cg + 1) * 8192], writes=[utb])

                def emit(v, fill, xt=xt, xtb=xtb, dst=dst, cg=cg):
                    fill(xt, xtb)
                    kb.dma("act", dst[:, cg * 8192:(cg + 1) * 8192], xt.rearrange("p a b c -> p (a b c)"), reads=[xtb])
                self.fwd_cg(ut, utb, 64, FA, FAb, [(FB, FBb)], ast, astb, emit)

    def phase_hmlp(self, slot_ids):
        A, kb, d = self.A, self.kb, self.d
        inv2pi = 1.0 / TWO_PI
        fq, fqb = A.alloc([128, 3, 64], F32, "fq")
        kb.dma("sp", fq.rearrange("p a b -> p (a b)"), d["filt_freq"].rearrange("a b -> (a b)").partition_broadcast(128),
               writes=[fqb])
        fqc, fqcb = A.alloc([128, 3], F32, "fqc")
        for half in range(2):
            kb.dma("sp", fqc[half * 64:(half + 1) * 64, :], d["filt_freq"].rearrange("a b -> b a"), writes=[fqcb], slow=True)
        w1p, w1pb = A.alloc([33, 64], F32, "w1p")
        kb.dma("sp", w1p, d["filt_w1"], writes=[w1pb])
        kb.op("dve", lambda e: e.scalar_tensor_tensor(out=w1p, in0=w1p, scalar=inv2pi, in1=fq[0:33, 0, :],
                                                       op0=ALU.mult, op1=ALU.mult), [w1pb, fqb], [w1pb])
        w2f, w2fb = A.alloc([64, 64], F32, "w2f")
        kb.dma("sp", w2f, d["filt_w2"], writes=[w2fb])
        w2p, w2pb = A.alloc([64, 64], BF16, "w2p")
        kb.op("dve", lambda e: e.scalar_tensor_tensor(out=w2p, in0=w2f, scalar=inv2pi, in1=fq[0:64, 1, :],
                                                       op0=ALU.mult, op1=ALU.mult), [w2fb, fqb], [w2pb])
        w3f, w3fb = A.alloc([64, 64], F32, "w3f")
        kb.dma("sp", w3f, d["filt_w3"], writes=[w3fb])
        w3p, w3pb = A.alloc([64, 2, 64], BF16, "w3p")
        for dup in range(2):
            kb.op("dve", lambda e, dup=dup: e.scalar_tensor_tensor(out=w3p[:, dup, :], in0=w3f, scalar=inv2pi,
                                                                    in1=fq[0:64, 2, :], op0=ALU.mult, op1=ALU.mult),
                  [w3fb, fqb], [w3pb])
        bp, bpb = A.alloc([128, 3], F32, "bp")
        for l, nm in enumerate(("filt_b1", "filt_b2", "filt_b3")):
            for half in range(2):
                kb.dma("sp", bp[half * 64:(half + 1) * 64, l:l + 1], d[nm].rearrange("(c o) -> c o", o=1), writes=[bpb], slow=True)
        kb.op("dve", lambda e: e.scalar_tensor_tensor(out=bp, in0=bp, scalar=inv2pi, in1=fqc, op0=ALU.mult, op1=ALU.mult),
              [bpb, fqcb], [bpb])
        haugs = [A.alloc([128, NFFT], BF16, f"haug{i}") for i in range(2)]
        GC = 8
        zts = [A.alloc([33, GC * 512], F32, f"zt_{i}") for i in range(2)]
        sels = [A.alloc([128, GC * 512], BF16, f"sel_{i}") for i in range(2)]
        uu = [A.alloc([128, 512], F32, f"uu_{i}") for i in range(GC)]
        vv = [A.alloc([128, 512], F32, f"vv_{i}") for i in range(GC)]
        hh = [[A.alloc([128, 512], BF16, f"hh_{l}_{i}") for i in range(GC)] for l in range(3)]
        gi = 0
        for si, slot in enumerate(slot_ids):
            haug, haugb = haugs[si % 2]
            for cgp in range(16 // GC):
                zt, ztb = zts[gi % 2]
                sl, slb = sels[gi % 2]
                gi += 1
                kb.dma("sp", zt, d["f_zT"][slot][:, cgp * GC * 512:(cgp + 1) * GC * 512], writes=[ztb])
                kb.dma("sp", sl, d["f_sel"][slot][:, cgp * GC * 512:(cgp + 1) * GC * 512], writes=[slb])
                prev = [(zt[:, c * 512:(c + 1) * 512], ztb) for c in range(GC)]
                for l in range(3):
                    P_ = 128 if l == 2 else 64
                    lhs, lhsb = ((w1p, w1pb), (w2p, w2pb), (w3p.rearrange("p a b -> p (a b)"), w3pb))[l]
                    new = []
                    for c in range(GC):
                        pin, pinb = prev[c]
                        bank, bankb = self.ps()
                        self.mm(bank[0:P_, :], lhs, pin, True, True, [lhsb, pinb], [bankb], inc=True)
                        u, ub_ = uu[c]
                        kb.op("act", lambda e, u=u, bank=bank, P_=P_, l=l: e.activation(
                            out=u[0:P_, :], in_=bank[0:P_, :], func=AF.Identity, bias=bp[0:P_, l:l + 1], scale=1.0),
                            [bankb, bpb], [ub_])
                    for c in range(GC):
                        u, ub_ = uu[c]
                        v_, vb_ = vv[c]
                        kb.op("dve", lambda e, u=u, v_=v_, P_=P_: e.scalar_tensor_tensor(
                            out=v_[0:P_, :], in0=u[0:P_, :], scalar=0.5, in1=u[0:P_, :], op0=ALU.is_gt, op1=ALU.subtract),
                            [ub_], [vb_])
                    for c in range(GC):
                        u, ub_ = uu[c]
                        v_, vb_ = vv[c]
                        kb.op("dve", lambda e, u=u, v_=v_, P_=P_: e.scalar_tensor_tensor(
                            out=v_[0:P_, :], in0=u[0:P_, :], scalar=-0.5, in1=v_[0:P_, :], op0=ALU.is_lt, op1=ALU.subtract),
                            [ub_, vb_], [vb_])
                    for c in range(GC):
                        v_, vb_ = vv[c]
                        h_, hb_ = hh[l][c]
                        kb.op("act", lambda e, h_=h_, v_=v_, P_=P_: e.activation(
                            out=h_[0:P_, :], in_=v_[0:P_, :], func=AF.Sin, scale=TWO_PI), [vb_], [hb_])
                        new.append((h_[0:P_, :], hb_))
                    prev = new
                for c in range(GC):
                    pin, pinb = prev[c]
                    ch = cgp * GC + c
                    kb.op("pool", lambda e, pin=pin, sl=sl, c=c, ch=ch, haug=haug: e.tensor_tensor(
                        out=haug.rearrange("p (b a) -> p a b", a=128)[:, ch * 8:(ch + 1) * 8, :],
                        in0=pin.rearrange("p (a b) -> p a b", b=64),
                        in1=sl[:, c * 512:(c + 1) * 512].rearrange("p (a b) -> p a b", b=64), op=ALU.mult),
                        [pinb, slb], dwrites=[haugb])
            kb.dma("act", d["HAUG"][slot], haug, reads=[haugb])

    def phase_hf(self, slots):
        A, kb, d = self.A, self.kb, self.d
        FA, FAb = A.alloc([128, 128], BF16, "FA")
        kb.dma("sp", FA, d["FA"], writes=[FAb])
        FB1, FB1b = A.alloc([128, 64, 2, 128], BF16, "FBG1")
        kb.dma("sp", FB1.rearrange("p a b c -> p (a b c)"), d["FBG1"], writes=[FB1b])
        FB2, FB2b = A.alloc([128, 64, 2, 128], BF16, "FBG2")
        kb.dma("sp", FB2.rearrange("p a b c -> p (a b c)"), d["FBG2"], writes=[FB2b])
        FBs = ((FB1, FB1b), (FB2, FB2b))
        w4f, w4fb = A.alloc([128, 2, 512], F32, "w4f")
        w4v = d["filt_w4"].rearrange("h (o dd c) -> h o dd c", o=2, dd=2)
        for dirn in range(2):
            kb.dma("sp", w4f[dirn * 64:(dirn + 1) * 64, :, :], w4v[:, :, dirn, :], writes=[w4fb])
        w4t, w4tb = A.alloc([128, 2, 512], BF16, "w4t")
        kb.op("dve", lambda e: e.tensor_copy(out=w4t, in_=w4f), [w4fb], [w4tb])
        dl, dlb = A.alloc([128, NCH], F32, "dl")
        kb.dma("sp", dl, d["delta"].partition_broadcast(128), writes=[dlb])
        drow, drowb = A.alloc([1, 2, NCH], F32, "drow")
        kb.dma("sp", drow.rearrange("p a b -> p (a b)"), d["hyena_d"].rearrange("(x a) b -> x (a b)", x=1), writes=[drowb])
        flg, flgb = A.alloc([1, 12], F32, "flg")
        kb.dma("sp", flg, d["f_flag"], writes=[flgb])
        haugs = [A.alloc([128, NFFT], BF16, f"haug{i}") for i in range(2)]
        ntls = [A.alloc([128, 64], F32, f"ntl{i}") for i in range(2)]
        dec4 = [A.alloc([128, 4, 128], F32, f"dec_{i}") for i in range(2)]
        gts = [A.alloc([128, 128, 64], BF16, f"gt{i}") for i in range(2)]
        ast, astb = A.alloc([128, 8192], BF16, "ast")
        astv = ast.rearrange("p (c r k) -> p c r k", r=2, k=64)
        xts = [A.alloc([128, 64, 2, 64], BF16, f"xt_{i}") for i in range(2)]
        units = []
        for si, (slot, outs) in enumerate(slots):
            for (o, g1dst, g2dst) in outs:
                for cg in range(4):
                    units.append(dict(si=si, slot=slot, o=o, cg=cg, dst=(g1dst, g2dst)))
        loaded = set()

        def load_slot(si):
            if si in loaded or si >= len(slots):
                return
            loaded.add(si)
            slot = slots[si][0]
            kb.dma("sp", haugs[si % 2][0], d["HAUG"][slot], writes=[haugs[si % 2][1]])
            kb.dma("sp", ntls[si % 2][0], d["f_ntl"][slot], writes=[ntls[si % 2][1]])

        dci = [0]

        def fg_step(ui, ng):
            u = units[ui]
            if ng == 0:
                load_slot(u["si"])
                load_slot(u["si"] + 1)
            haug, haugb = haugs[u["si"] % 2]
            ntl, ntlb = ntls[u["si"] % 2]
            haugv = haug.rearrange("p (b a) -> p a b", a=128)
            gt, gtb = gts[ui % 2]
            o, cg = u["o"], u["cg"]
            bank, bankb = self.ps()
            dc, dcb = dec4[dci[0] % 2]
            dci[0] += 1
            for nn in range(4):
                n2 = ng * 4 + nn
                kb.op("act", lambda e, nn=nn, n2=n2: e.activation(
                    out=dc[:, nn, :], in_=dl[:, cg * 128:(cg + 1) * 128], func=AF.Exp,
                    scale=ntl[:, n2:n2 + 1]), [dlb, ntlb], [dcb])
                self.mm(bank[:, nn * 128:(nn + 1) * 128], haugv[:, :, n2], w4t[:, o, cg * 128:(cg + 1) * 128],
                        True, True, [haugb, w4tb], [bankb], inc=(nn == 3))
            kb.op("dve", lambda e: e.tensor_tensor(
                out=gt[:, :, ng * 4:(ng + 1) * 4], in0=bank.rearrange("p (n c) -> p c n", n=4),
                in1=dc.rearrange("p n c -> p c n"), op=ALU.mult), [bankb, dcb], dwrites=[gtb])
            if ng == 15:
                slot = u["slot"]
                kb.op("dve", lambda e: e.scalar_tensor_tensor(
                    out=gt[0:1, :, 0], in0=drow[0:1, o, cg * 128:(cg + 1) * 128], scalar=flg[0:1, slot:slot + 1],
                    in1=gt[0:1, :, 0], op0=ALU.mult, op1=ALU.add), [drowb, flgb, gtb], [gtb])

        def da(ui):
            gt, gtb = gts[ui % 2]
            ut = gt.rearrange("p c n -> p (c n)")
            for g in range(16):
                bank, bankb = self.ps()
                for q in range(4):
                    cp = g * 4 + q
                    self.mm(bank[:, q * 128:(q + 1) * 128], ut[:, cp * 128:(cp + 1) * 128], FA, True, True,
                            [gtb, FAb], [bankb], inc=(q == 3))
                self.evac(ast[:, g * 512:(g + 1) * 512], bank, [bankb], [astb])

        def mb_step(ui, step):
            u = units[ui]
            v, kg = step // 8, step % 8
            FBt, FBb = FBs[v]
            Xv, Xb = xts[(ui * 2 + v) % 2]
            b0, b0b = self.ps()
            b1, b1b = self.ps()
            for kk in range(8):
                k1 = kg * 8 + kk
                for par, (bk, bkb) in enumerate(((b0, b0b), (b1, b1b))):
                    for ri in range(2):
                        self.mm(bk[:, kk * 64:(kk + 1) * 64], FBt[par * 64:(par + 1) * 64, k1, ri, :],
                                astv[par * 64:(par + 1) * 64, :, ri, k1], ri == 0, ri == 1,
                                [FBb, astb], [bkb], inc=(kk == 7 and ri == 1))
            for par, (bk, bkb) in enumerate(((b0, b0b), (b1, b1b))):
                self.evac(Xv[:, :, par, kg * 8:(kg + 1) * 8], bk.rearrange("p (k c) -> p c k", k=8), [bkb], [Xb])
            if kg == 7:
                cg = u["cg"]
                kb.dma("act", u["dst"][v][:, cg * 8192:(cg + 1) * 8192], Xv.rearrange("p a b c -> p (a b c)"), reads=[Xb])

        n = len(units)
        for ng in range(16):
            fg_step(0, ng)
        da(0)
        for ui in range(n):
            for step in range(16):
                if ui + 1 < n:
                    fg_step(ui + 1, step)
                mb_step(ui, step)
            if ui + 1 < n:
                da(ui + 1)

    def phase_hp(self, outs, xs, nin):
        A, kb, d = self.A, self.kb, self.d
        FiB1, FiB1b = A.alloc([128, 128], BF16, "FiB1")
        kb.dma("sp", FiB1, d["FinvB1"], writes=[FiB1b])
        FiB2, FiB2b = A.alloc([128, 128], BF16, "FiB2")
        kb.dma("sp", FiB2, d["FinvB2"], writes=[FiB2b])
        FiA, FiAb = A.alloc([128, 64, 2, 64], BF16, "FiA")
        kb.dma("sp", FiA.rearrange("p a b c -> p (a b c)"), d["FinvA"], writes=[FiAb])
        NB = 3
        xres = [A.alloc([128, 8192], BF16, f"xres_{j}") for j in range(nin)]
        g1 = [A.alloc([128, 2048], BF16, f"g1_{i}") for i in range(NB)]
        g2 = [A.alloc([128, 2048], BF16, f"g2_{i}") for i in range(NB)]
        t1 = [A.alloc([128, 2048], BF16, f"t1_{i}") for i in range(2)]
        t2 = [A.alloc([128, 2048], BF16, f"t2_{i}") for i in range(2)]
        bsts = [A.alloc([128, 64, 2, 64], BF16, f"bst_{i}") for i in range(2)]
        xgs = [A.alloc([64, 64, 2, 64], BF16, f"xg_{i}") for i in range(1)]
        zts = [A.alloc([64, 64, 2, 64], BF16, f"zt_{i}") for i in range(2)]
        li = 0
        ti = 0
        ci = 0
        for cg in range(4):
            for j in range(nin):
                kb.dma("sp", xres[j][0], xs[j][:, cg * 8192:(cg + 1) * 8192], writes=[xres[j][1]])
            for (gl, gate, dst) in outs:
                bst, bstb = bsts[ci % 2]
                xg, xgb = xgs[0]
                zt, ztb = zts[ci % 2]
                ci += 1
                kb.dma("sp", xg.rearrange("p a b c -> p (a b c)"), gate[:, cg * 8192:(cg + 1) * 8192], writes=[xgb])
                bstf = bst.rearrange("p a b c -> p (a b c)")
                for cq in range(4):
                    c0 = (cg * 128 + cq * 32) * 64
                    banks = [self.ps() for _ in range(4)]
                    for j in range(nin):
                        ax, bx = xres[j][0][:, cq * 2048:(cq + 1) * 2048], xres[j][1]
                        a1, b1 = g1[li % NB]
                        a2, b2 = g2[li % NB]
                        li += 1
                        kb.dma("sp", a1, gl[j][0][:, c0:c0 + 2048], writes=[b1])
                        kb.dma("sp", a2, gl[j][1][:, c0:c0 + 2048], writes=[b2])
                        at1, bt1 = t1[ti % 2]
                        at2, bt2 = t2[ti % 2]
                        ti += 1
                        kb.op("dve", lambda e, o=at1, a=a1, b=ax: e.tensor_tensor(out=o, in0=a, in1=b, op=ALU.mult),
                              [b1, bx], [bt1])
                        kb.op("dve", lambda e, o=at2, a=a2, b=ax: e.tensor_tensor(
                            out=o[:, 0:1536], in0=a[:, 0:1536], in1=b[:, 0:1536], op=ALU.mult), [b2, bx], dwrites=[bt2])
                        kb.op("pool", lambda e, o=at2, a=a2, b=ax: e.tensor_tensor(
                            out=o[:, 1536:2048], in0=a[:, 1536:2048], in1=b[:, 1536:2048], op=ALU.mult), [b2, bx], dwrites=[bt2])
                        for q in range(16):
                            bk, bkb = banks[q // 4]
                            sl = (q % 4) * 128
                            self.mm(bk[:, sl:sl + 128], at1[:, q * 128:(q + 1) * 128], FiB1, j == 0, False,
                                    [bt1, FiB1b], [bkb], inc=False)
                            self.mm(bk[:, sl:sl + 128], at2[:, q * 128:(q + 1) * 128], FiB2, False, j == nin - 1,
                                    [bt2, FiB2b], [bkb], inc=(q % 4 == 3 or q == 15))
                    for b4 in range(4):
                        bk, bkb = banks[b4]
                        off = (cq * 16 + b4 * 4) * 128
                        self.evac(bstf[:, off:off + 512], bk, [bkb], [bstb], eng="act")
                for ng in range(8):
                    b0, b0b = self.ps()
                    b1_, b1b = self.ps()
                    for nn in range(8):
                        n2 = ng * 8 + nn
                        for par, (bk, bkb) in enumerate(((b0, b0b), (b1_, b1b))):
                            for ri in range(2):
                                self.mm(bk[0:64, nn * 64:(nn + 1) * 64], FiA[par * 64:(par + 1) * 64, n2, ri, :],
                                        bst[par * 64:(par + 1) * 64, :, ri, n2], ri == 0, ri == 1,
                                        [FiAb, bstb], [bkb], inc=(nn == 7 and ri == 1))
                    for par, (bk, bkb) in enumerate(((b0, b0b), (b1_, b1b))):
                        kb.op("dve", lambda e, bk=bk, par=par, ng=ng, zt=zt, xg=xg: e.tensor_tensor(
                            out=zt[:, :, par, ng * 8:(ng + 1) * 8], in0=bk[0:64, :].rearrange("p (n c) -> p c n", n=8),
                            in1=xg[:, :, par, ng * 8:(ng + 1) * 8], op=ALU.mult), [bkb, xgb], dwrites=[ztb])
                kb.dma("act", dst[:, cg * 8192:(cg + 1) * 8192], zt.rearrange("p a b c -> p (a b c)"), reads=[ztb])

    def inproj_fm(self, xTname, tok0, ntok, co, wst, xts, consume):
        A, kb, d = self.A, self.kb, self.d
        (a32, b32), (awb, bwb) = wst
        kb.dma("sp", a32, d["w_in"][:, co:co + 128].rearrange("(k p) c -> p k c", p=128), writes=[b32])
        kb.op("pool", lambda e: e.tensor_copy(out=awb, in_=a32), [b32], [bwb])
        xT = d[xTname]
        for ch in range(ntok // 512):
            ax, bx = xts[self.xi % len(xts)]
            self.xi += 1
            kb.dma("sp", ax, xT[:, :, tok0 + ch * 512: tok0 + (ch + 1) * 512].rearrange("k p t -> p k t"), writes=[bx])
            bank, bankb = self.ps()
            for kc in range(8):
                self.mm(bank, awb[:, kc, :], ax[:, kc, :], kc == 0, kc == 7, [bwb, bx], [bankb], inc=(kc == 7))
            consume(ch * 512, 512, bank, bankb)

    def rms_gate(self, val, valb, gate, gateb, gcol, gcolb):
        A, kb = self.A, self.kb
        sq = [A.alloc([128, 4, 512], BF16, f"sq_{i}") for i in range(2)]
        rs = [A.alloc([128, 512], F32, f"rs_{i}") for i in range(2)]
        tm = [A.alloc([128, 512], F32, f"tm_{i}") for i in range(4)]
        epsc, epsb = A.alloc([128, 1], F32, "epsc")
        kb.op("pool", lambda e: e.memset(epsc, RMS_EPS), [], [epsb])
        outb = []
        ti = 0
        for ch in range(T // 512):
            asq, bsq = sq[ch % 2]
            ars, brs = rs[ch % 2]
            cb_ = Buf(f"rmsch{ch}")
            outb.append(cb_)
            sl = slice(ch * 512, (ch + 1) * 512)
            kb.op("act", lambda e, asq=asq, sl=sl: e.activation(out=asq, in_=val[:, :, sl], func=AF.Square),
                  [valb], [bsq])
            bank, bankb = self.ps()
            for hp in range(4):
                self.mm(bank, self.ones, asq[:, hp, :], hp == 0, hp == 3, [self.onesb, bsq], [bankb], inc=(hp == 3))
            kb.op("act", lambda e, ars=ars, bank=bank: e.activation(out=ars, in_=bank, func=AF.Sqrt, bias=epsc, scale=1.0 / 512.0),
                  [bankb, epsb], [brs])
            kb.op("dve", lambda e, ars=ars: e.reciprocal(out=ars, in_=ars), [brs], [brs])
            for hp in range(4):
                atm, btm = tm[ti % 4]
                ti += 1
                kb.op("dve", lambda e, atm=atm, hp=hp, sl=sl, ars=ars: e.scalar_tensor_tensor(
                    out=atm, in0=val[:, hp, sl], scalar=gcol[:, hp:hp + 1], in1=ars, op0=ALU.mult, op1=ALU.mult),
                    [valb, gcolb, brs], [btm])
                kb.op("pool", lambda e, atm=atm, hp=hp, sl=sl: e.tensor_tensor(
                    out=val[:, hp, sl], in0=atm, in1=gate[:, hp, sl], op=ALU.mult), [btm, gateb, bsq], [cb_])
        return outb

    def _att_job(self, g, r_):
        kb = self.kb
        st = {}

        def A_():
            kind, J, qlo, qhi, rho, pidx, hp = g["kind"], g["J"], g["qlo"], g["qhi"], g["rho"], g["pidx"], g["hp"]
            koff, n_r = g["koff"], g["n_r"]
            wq = qhi - qlo
            if kind == "main":
                nk, k0 = 128, koff + 128 * J
                col0 = qlo - (128 * J - 64)
                Et = r_["E"][:, pidx, 2 * hp:2 * hp + 2, col0:col0 + wq]
                Etb = r_["Eb"]
            elif kind == "left":
                nk, k0 = 64, koff - 64
                Et = r_["EL"][:, pidx, 2 * hp:2 * hp + 2, 0:wq]
                Etb = r_["ELb"]
            else:
                nk, k0 = 64, koff + n_r
                Et = r_["ER"][:, pidx, 2 * hp:2 * hp + 2, 0:wq]
                Etb = r_["ERb"]
            QTv, KTv, VTv = r_["QTv"], r_["KTv"], r_["VTv"]
            s0, s0b = self.ps("tmp")
            s1, s1b = self.ps("tmp")
            self.mm(s0[0:nk, 0:wq], KTv[0:64, k0:k0 + nk, rho], QTv[0:64, qlo:qhi, rho], True, True,
                    [r_["KTb"], r_["QTb"]], [s0b], inc=True)
            self.mm(s1[0:nk, 0:wq], KTv[64:128, k0:k0 + nk, rho], QTv[64:128, qlo:qhi, rho], True, True,
                    [r_["KTb"], r_["QTb"]], [s1b], inc=True)
            i = self.att_i
            self.att_i += 1
            ape, bpe = r_["pes"][i % 6]
            apm, bpm = r_["pms"][i % 6]
            avz, bvz = r_["vzs"][i % 6]
            kb.op("act", lambda e: e.activation(out=ape[0:nk, 0, 0:wq], in_=s0[0:nk, 0:wq], func=AF.Exp, scale=0.125),
                  [s0b], [bpe])
            kb.op("act", lambda e: e.activation(out=ape[0:nk, 1, 0:wq], in_=s1[0:nk, 0:wq], func=AF.Exp, scale=0.125),
                  [s1b], [bpe])
            kb.op("pool", lambda e: e.tensor_tensor(out=apm[0:nk, :, 0:wq], in0=ape[0:nk, :, 0:wq], in1=Et, op=ALU.mult),
                  [bpe, Etb], [bpm])
            tb, tbb = self.ps("tmp")
            tbf = tb.bitcast(BF16)
            self.tr(tbf[0:nk, 0:128], VTv[:, k0:k0 + nk, rho], self.ident, [r_["VTb"], self.identb], [tbb], inc=True)
            kb.op("dve", lambda e: e.tensor_copy(out=avz[0:nk, 0, 0:64], in_=tbf[0:nk, 0:64]), [tbb], [bvz])
            kb.op("dve", lambda e: e.tensor_copy(out=avz[0:nk, 1, 64:128], in_=tbf[0:nk, 64:128]), [tbb], [bvz])
            st.update(nk=nk, wq=wq, apm=apm, bpm=bpm, avz=avz, bvz=bvz)

        def B_():
            ch = g["chunk"]
            c0, nq, qlo, qhi, rho = g["c0"], g["nq"], g["qlo"], g["qhi"], g["rho"]
            if g["first"]:
                ch["num"] = self.ps("acc")
                ch["den"] = self.ps("acc")
            numb, numbb = ch["num"]
            denb, denbb = ch["den"]
            nk, wq, apm, bpm, avz, bvz = st["nk"], st["wq"], st["apm"], st["bpm"], st["avz"], st["bvz"]
            onz, onzb = r_["onz"], r_["onzb"]
            for hs in range(2):
                self.mm(numb[:, qlo - c0:qhi - c0], avz[0:nk, hs, :], apm[0:nk, hs, 0:wq], False, False,
                        [bvz, bpm], [numbb], inc=False)
                self.mm(denb[:, qlo - c0:qhi - c0], onz[0:nk, hs, :], apm[0:nk, hs, 0:wq], False, False,
                        [onzb, bpm], [denbb], inc=(hs == 1))
            if g["last"]:
                NUMv, DENv, NUMb, DENb = r_["NUMv"], r_["DENv"], r_["NUMb"], r_["DENb"]
                if g["first_pat"]:
                    kb.op("act", lambda e: e.activation(out=NUMv[:, c0:c0 + nq, rho], in_=numb[:, 0:nq], func=AF.Copy),
                          [numbb], [NUMb])
                    kb.op("dve", lambda e: e.tensor_copy(out=DENv[:, c0:c0 + nq, rho], in_=denb[:, 0:nq]),
                          [denbb], [DENb])
                else:
                    kb.op("dve", lambda e: e.tensor_tensor(out=NUMv[:, c0:c0 + nq, rho], in0=numb[:, 0:nq],
                                                            in1=NUMv[:, c0:c0 + nq, rho], op=ALU.add), [numbb, NUMb], [NUMb])
                    kb.op("dve", lambda e: e.tensor_tensor(out=DENv[:, c0:c0 + nq, rho], in0=denb[:, 0:nq],
                                                            in1=DENv[:, c0:c0 + nq, rho], op=ALU.add), [denbb, DENb], [DENb])
        return (A_, B_)

    def phase_att(self, xTname, off, next_, halo, Ename, ELname, ERname, MAname):
        A, kb, d = self.A, self.kb, self.d
        self.xi = 0
        E, Eb = A.alloc([128, 3, 8, 256], BF16, "E")
        kb.dma("sp", E.rearrange("p a b c -> p (a b c)"), d[Ename], writes=[Eb])
        if halo:
            EL, ELb = A.alloc([64, 3, 8, 64], BF16, "EL")
            kb.dma("sp", EL.rearrange("p a b c -> p (a b c)"), d[ELname], writes=[ELb])
            ER, ERb = A.alloc([64, 3, 8, 64], BF16, "ER")
            kb.dma("sp", ER.rearrange("p a b c -> p (a b c)"), d[ERname], writes=[ERb])
        onz, onzb = A.alloc([128, 2, 128], BF16, "onz")
        kb.dma("sp", onz.rearrange("p a b -> p (a b)"), d["onesz"], writes=[onzb])
        gcol, gcolb = A.alloc([128, 4], F32, "gcol")
        kb.dma("sp", gcol, d["attn_norm_g"].rearrange("(h p) -> p h", p=128), writes=[gcolb], slow=True)
        ATT, ATTb = A.alloc([128, 4, T], BF16, "ATT")
        GA, GAb = A.alloc([128, 4, T], BF16, "GA")
        QT, QTb = A.alloc([128, T], BF16, "QT")
        KT, KTb = A.alloc([128, next_], BF16, "KT")
        VT, VTb = A.alloc([128, next_], BF16, "VT")
        NUM, NUMb = A.alloc([128, T], F32, "NUM")
        DEN, DENb = A.alloc([128, T], F32, "DEN")
        w32 = A.alloc([128, 8, 128], F32, "w32")
        wbs = A.alloc([128, 8, 128], BF16, "wbs")
        xts = [A.alloc([128, 8, 512], BF16, f"xt_{i}") for i in range(2)]
        pes = [A.alloc([128, 2, 256], BF16, f"pe_{i}") for i in range(6)]
        pms = [A.alloc([128, 2, 256], BF16, f"pm_{i}") for i in range(6)]
        vzs = [A.alloc([128, 2, 128], BF16, f"vz_{i}") for i in range(6)]
        self.att_i = 0
        for (avz, bvz) in vzs:
            kb.op("pool", lambda e, avz=avz: e.memset(avz, 0.0), [], [bvz])
        for hp in range(4):
            def to_tile(dstt, dstb, base, fn=None):
                def consume(c0, n, bank, bankb):
                    if fn is None:
                        self.evac(dstt[:, base + c0: base + c0 + n], bank, [bankb], [dstb])
                    else:
                        kb.op("act", lambda e: e.activation(out=dstt[:, base + c0: base + c0 + n], in_=bank, func=fn),
                              [bankb], dwrites=[dstb])
                return consume
            self.inproj_fm(xTname, off, T, 0 + hp * 128, (w32, wbs), xts, to_tile(QT, QTb, 0))
            self.inproj_fm(xTname, 0, next_, 512 + hp * 128, (w32, wbs), xts, to_tile(KT, KTb, 0))
            self.inproj_fm(xTname, 0, next_, 1024 + hp * 128, (w32, wbs), xts, to_tile(VT, VTb, 0))
            self.inproj_fm(xTname, off, T, 1536 + hp * 128, (w32, wbs), xts, to_tile(GA[:, hp, :], GAb, 0, AF.Silu))
            jobs = []
            for pidx, (_, r) in enumerate(PATTERNS):
                n_r = T // r
                ntile = n_r // 128
                QTv = QT.rearrange("p (i r) -> p i r", r=r)
                KTv = KT.rearrange("p (i r) -> p i r", r=r)
                VTv = VT.rearrange("p (i r) -> p i r", r=r)
                NUMv = NUM.rearrange("p (i r) -> p i r", r=r)
                DENv = DEN.rearrange("p (i r) -> p i r", r=r)
                koff = off // r
                for rho in range(r):
                    for c0 in range(0, n_r, 512):
                        nq = min(512, n_r - c0)
                        segs = []
                        for J in range(c0 // 128 - 1, (c0 + nq) // 128 + 1):
                            if 0 <= J < ntile:
                                kind = "main"
                            elif halo and J == -1:
                                kind = "left"
                            elif halo and J == ntile:
                                kind = "right"
                            else:
                                continue
                            wlo, whi = 128 * J - 64, 128 * J + 192
                            if kind == "left":
                                wlo, whi = 0, 64
                            if kind == "right":
                                wlo, whi = n_r - 64, n_r
                            qlo, qhi = max(wlo, c0), min(whi, c0 + nq)
                            if qhi > qlo:
                                segs.append((kind, J, qlo, qhi))
                        cover = [0] * (nq // 64)
                        for (_, _, qlo, qhi) in segs:
                            for b_ in range((qlo - c0) // 64, (qhi - c0) // 64):
                                cover[b_] += 1
                        assert all(c_ > 0 for c_ in cover), cover
                        chunk = {}
                        for si, (kind, J, qlo, qhi) in enumerate(segs):
                            jobs.append(self._att_job(
                                dict(kind=kind, J=J, qlo=qlo, qhi=qhi, c0=c0, nq=nq, rho=rho, pidx=pidx, hp=hp,
                                     koff=koff, n_r=n_r, first=(si == 0), last=(si == len(segs) - 1),
                                     first_pat=(pidx == 0), chunk=chunk),
                                dict(QTv=QTv, KTv=KTv, VTv=VTv, NUMv=NUMv, DENv=DENv, QTb=QTb, KTb=KTb, VTb=VTb,
                                     NUMb=NUMb, DENb=DENb, E=E, Eb=Eb, EL=EL if halo else None, ELb=ELb if halo else None,
                                     ER=ER if halo else None, ERb=ERb if halo else None, onz=onz, onzb=onzb,
                                     pes=pes, pms=pms, vzs=vzs)))
            SK = 4
            for i in range(len(jobs) + SK):
                if i < len(jobs):
                    jobs[i][0]()
                if i - SK >= 0:
                    jobs[i - SK][1]()
            kb.op("dve", lambda e: e.reciprocal(out=DEN, in_=DEN), [DENb], [DENb])
            kb.op("dve", lambda e, hp=hp: e.tensor_tensor(out=ATT[:, hp, :], in0=NUM, in1=DEN, op=ALU.mult),
                  [NUMb, DENb], dwrites=[ATTb])
        ob = self.rms_gate(ATT, ATTb, GA, GAb, gcol, gcolb)
        kb.dma("act", d[MAname], ATT.rearrange("p a b -> p (a b)"), reads=[ATTb] + ob)

    def phase_mh(self, xTname, off, Z2name, MHname):
        A, kb, d = self.A, self.kb, self.d
        self.xi = 0
        ZF, ZFb = A.alloc([128, 4, T], BF16, "ZF")
        GH, GHb = A.alloc([128, 4, T], BF16, "GH")
        gcol, gcolb = A.alloc([128, 4], F32, "gcolh")
        kb.dma("sp", gcol, d["hyena_norm_g"].rearrange("(h p) -> p h", p=128), writes=[gcolb], slow=True)
        w32 = A.alloc([128, 8, 128], F32, "w32")
        wbs = A.alloc([128, 8, 128], BF16, "wbs")
        xts = [A.alloc([128, 8, 512], BF16, f"xt_{i}") for i in range(2)]
        zts = [A.alloc([64, 128, 64], BF16, f"ztm_{i}") for i in range(2)]
        ZFv = ZF.rearrange("p a (n m) -> p a n m", m=64)
        for cb in range(4):
            zt, ztb = zts[cb % 2]
            kb.dma("sp", zt.rearrange("p a b -> p (a b)"), d[Z2name][:, cb * 8192:(cb + 1) * 8192], writes=[ztb])
            for ng in range(4):
                bank, bankb = self.ps()
                bbf = bank.bitcast(BF16)
                for nn in range(16):
                    n2 = ng * 16 + nn
                    self.tr(bbf[:, nn * 64:(nn + 1) * 64], zt[:, :, n2], self.ident[0:64, 0:64], [ztb, self.identb], [bankb],
                            inc=(nn == 15))
                self.evac(ZFv[:, cb, :, ng * 16:(ng + 1) * 16], bbf.rearrange("p (n m) -> p m n", n=16), [bankb], [ZFb])

            def consume(c0, n, bank, bankb, cb=cb):
                kb.op("act", lambda e: e.activation(out=GH[:, cb, c0:c0 + n], in_=bank, func=AF.Silu), [bankb], dwrites=[GHb])
            self.inproj_fm(xTname, off, T, 3584 + cb * 128, (w32, wbs), xts, consume)
        ob = self.rms_gate(ZF, ZFb, GH, GHb, gcol, gcolb)
        kb.dma("act", d[MHname], ZF.rearrange("p a b -> p (a b)"), reads=[ZFb] + ob)

    def phase_out(self, xname, xrow0, MAname, MHname, yname):
        A, kb, d = self.A, self.kb, self.d
        MA, MAb = A.alloc([128, 4, T], BF16, "MA")
        kb.dma("sp", MA.rearrange("p a b -> p (a b)"), d[MAname], writes=[MAb])
        ZF, ZFb = A.alloc([128, 4, T], BF16, "MH")
        kb.dma("sp", ZF.rearrange("p a b -> p (a b)"), d[MHname], writes=[ZFb])
        wo, wob = A.alloc([128, 8, 1024], BF16, "wo")
        wst = [A.alloc([128, 1024], F32, f"wst_{i}") for i in range(2)]
        for kc in range(8):
            a, b = wst[kc % 2]
            kb.dma("sp", a, d["w_out"][kc * 128:(kc + 1) * 128, :], writes=[b])
            kb.op("pool", lambda e, a=a, kc=kc: e.tensor_copy(out=wo[:, kc, :], in_=a), [b], [wob])
        lg, lgb = A.alloc([128, 1024], F32, "lg")
        kb.dma("sp", lg, d["ln_g"].partition_broadcast(128), writes=[lgb])
        lb, lbb = A.alloc([128, 1024], F32, "lb")
        kb.dma("sp", lb, d["ln_b"].partition_broadcast(128), writes=[lbb])
        epsc, epsb = A.alloc([128, 1], F32, "epsl")
        kb.op("pool", lambda e: e.memset(epsc, LN_EPS), [], [epsb])
        xin = [A.alloc([128, 1024], F32, f"xin_{i}") for i in range(4)]
        hs = [A.alloc([128, 1024], F32, f"h_{i}") for i in range(2)]
        ys = [A.alloc([128, 1024], F32, f"y_{i}") for i in range(2)]
        sts = [A.alloc([128, 2, 6], F32, f"st_{i}") for i in range(2)]
        mvs = [A.alloc([128, 2], F32, f"mv_{i}") for i in range(2)]
        x, y = d[xname], d[yname]

        def ldx(t):
            if t < T // 128:
                kb.dma("sp", xin[t % 4][0], x[xrow0 + t * 128: xrow0 + (t + 1) * 128, :], writes=[xin[t % 4][1]])
        ldx(0)
        ldx(1)
        for t in range(T // 128):
            ax, bx = xin[t % 4]
            ah, bh = hs[t % 2]
            ay, by = ys[t % 2]
            ast_, bst_ = sts[t % 2]
            amv, bmv = mvs[t % 2]
            ldx(t + 2)
            for half in range(2):
                bank, bankb = self.ps()
                for kc in range(8):
                    src, srcb = (MA, MAb) if kc < 4 else (ZF, ZFb)
                    self.mm(bank, src[:, kc % 4, t * 128:(t + 1) * 128], wo[:, kc, half * 512:(half + 1) * 512],
                            kc == 0, kc == 7, [srcb, wob], [bankb], inc=(kc == 7))
                kb.op("dve", lambda e, ah=ah, ax=ax, bank=bank, half=half: e.scalar_tensor_tensor(
                    out=ah[:, half * 512:(half + 1) * 512], in0=ax[:, half * 512:(half + 1) * 512], scalar=ALPHA, in1=bank,
                    op0=ALU.mult, op1=ALU.add), [bx, bankb], [bh])
                kb.op("dve", lambda e, ast_=ast_, ah=ah, half=half: e.bn_stats(out=ast_[:, half, :], in_=ah[:, half * 512:(half + 1) * 512]),
                      [bh], [bst_])
            kb.op("dve", lambda e, amv=amv, ast_=ast_: e.bn_aggr(out=amv, in_=ast_.rearrange("p a b -> p (a b)")), [bst_], [bmv])
            kb.op("act", lambda e, amv=amv: e.activation(out=amv[:, 1:2], in_=amv[:, 1:2], func=AF.Sqrt, bias=epsc, scale=1.0),
                  [bmv, epsb], [bmv])
            kb.op("dve", lambda e, amv=amv: e.reciprocal(out=amv[:, 1:2], in_=amv[:, 1:2]), [bmv], [bmv])
            kb.op("dve", lambda e, ay=ay, ah=ah, amv=amv: e.tensor_scalar(out=ay, in0=ah, scalar1=amv[:, 0:1], scalar2=amv[:, 1:2],
                                                                        op0=ALU.subtract, op1=ALU.mult), [bh, bmv], [by])
            kb.op("pool", lambda e, ay=ay: e.tensor_tensor(out=ay, in0=ay, in1=lg, op=ALU.mult), [by, lgb], [by])
            kb.op("pool", lambda e, ay=ay: e.tensor_tensor(out=ay, in0=ay, in1=lb, op=ALU.add), [by, lbb], [by])
            kb.dma("act", y[t * 128:(t + 1) * 128, :], ay, reads=[by])

    def build(self, kb):
        self.kb = kb
        d = self.d
        ph = self.phases
        self.setup()
        if "xtp" in ph:
            self.phase_xt("xp", "xT_P", SEQ); self.reset()
        if "xts" in ph:
            self.phase_xt("xsf", "xT_SF", DEC_SEQ); self.reset()
            self.phase_xt("xso", "xT_SO", EXT); self.reset()
        if "h1p" in ph:
            gP = []
            for k in range(4):
                gP.append((2048 + k * 128, k * 128, d["UT0_P"][0], k))
            for k in range(4):
                gP.append((2560 + k * 128, 512 + k * 128, d["X1T_P"][0], k))
            for k in range(4):
                gP.append((3072 + k * 128, 1024 + k * 128, d["X2T_P"], k))
            self.phase_h1("xT_P", 0, True, True, gP); self.reset()
        if "h1s" in ph:
            for b in range(4):
                g = []
                for k in range(4):
                    g.append((2048 + k * 128, k * 128, d["UT0_S"][b], k))
                for k in range(4):
                    g.append((2560 + k * 128, 512 + k * 128, d["X1T_S"][b], k))
                self.phase_h1("xT_SF", b * T, b == 0, b == 3, g); self.reset()
            g = [(3072 + k * 128, 1024 + k * 128, d["X2T_S"], k) for k in range(4)]
            self.phase_h1("xT_SO", HALO, False, False, g); self.reset()
        slots = []
        if "hfp" in ph:
            slots.append((0, [(0, d["G_P"][0][0], d["G_P"][0][1]), (1, d["G_P"][1][0], d["G_P"][1][1])]))
        if "hfs" in ph:
            for i in range(7):
                slots.append((1 + i, [(0, d["G_S1"][i][0], d["G_S1"][i][1])]))
            for j in range(4):
                slots.append((8 + j, [(1, d["G_S2"][j][0], d["G_S2"][j][1])]))
        if slots:
            self.phase_hmlp([sl_[0] for sl_ in slots]); self.reset()
            self.phase_hf(slots); self.reset()
        if "hyp" in ph:
            self.phase_hx([d["UT0_P"][0]], [d["XS_P"][0]]); self.reset()
            self.phase_hp([([(d["G_P"][0][0], d["G_P"][0][1])], d["X1T_P"][0], d["Z1T_P"][0])], [d["XS_P"][0]], 1); self.reset()
            self.phase_hx([d["Z1T_P"][0]], [d["XS_P"][0]]); self.reset()
            self.phase_hp([([(d["G_P"][1][0], d["G_P"][1][1])], d["X2T_P"], d["Z2T_P"])], [d["XS_P"][0]], 1); self.reset()
        if "hys" in ph:
            self.phase_hx([d["UT0_S"][j] for j in range(4)], [d["XS_S"][j] for j in range(4)]); self.reset()
            outs = []
            for i in range(4):
                gl = [(d["G_S1"][i - j + 3][0], d["G_S1"][i - j + 3][1]) for j in range(4)]
                outs.append((gl, d["X1T_S"][i], d["Z1T_S"][i]))
            self.phase_hp(outs, [d["XS_S"][j] for j in range(4)], 4); self.reset()
            self.phase_hx([d["Z1T_S"][j] for j in range(4)], [d["XS_S"][j] for j in range(4)]); self.reset()
            gl = [(d["G_S2"][j][0], d["G_S2"][j][1]) for j in range(4)]
            self.phase_hp([(gl, d["X2T_S"], d["Z2T_S"])], [d["XS_S"][j] for j in range(4)], 4); self.reset()
        if "attp" in ph:
            self.phase_att("xT_P", 0, SEQ, False, "E_P", None, None, "MA_P"); self.reset()
        if "atts" in ph:
            self.phase_att("xT_SO", HALO, EXT, True, "E_P", "EL_S", "ER_S", "MA_S"); self.reset()
        if "outp" in ph:
            self.phase_mh("xT_P", 0, "Z2T_P", "MH_P"); self.reset()
            self.phase_out("xp", 0, "MA_P", "MH_P", "yp"); self.reset()
        if "outs" in ph:
            self.phase_mh("xT_SO", HALO, "Z2T_S", "MH_S"); self.reset()
            self.phase_out("xso", HALO, "MA_S", "MH_S", "ys"); self.reset()


ALL_PHASES = ("xtp", "xts", "h1p", "h1s", "hfp", "hfs", "hyp", "hys", "attp", "atts", "outp", "outs")


def build_program(phases=ALL_PHASES, extra=None, ext_in=(), ext_out=()):
    nc = bass.Bass("TRN2", target_bir_lowering=False)
    st = ExitStack()
    P = Prog(nc, st, phases, ext_in, ext_out)
    P.declare()
    if extra is not None:
        extra(P)
    words = 53000
    ar = st.enter_context(nc.sbuf_tensor("arena", [128, words], F32))
    P.A = Arena(ar, words)
    P.bank = []
    P.bankb = []
    for i in range(8):
        P.bank.append(st.enter_context(nc.psum_tensor(f"pb{i}", [128, 512], F32))[:, :])
        P.bankb.append(Buf(f"pb{i}"))
    kb = KB(nc)
    kb.run(st, P.build)
    st.close()
    return nc, P


_CONST_CACHE = {}


def const_inputs():
    if "c" in _CONST_CACHE:
        return _CONST_CACHE["c"]
    c = dict(fft_consts())
    c["ident"] = np.eye(128, dtype=np.float32).astype(NPBF)
    c["ones"] = np.ones((128, 128), np.float32).astype(NPBF)
    oz = np.zeros((128, 2, 128), np.float32)
    oz[:, 0, :64] = 1.0
    oz[:, 1, 64:] = 1.0
    c["onesz"] = oz.reshape(128, 256).astype(NPBF)
    c["delta"] = decay_deltas()
    _CONST_CACHE["c"] = c
    return c


def core_inputs(core, inputs):
    c = dict(const_inputs())
    sb, blk = core // 4, core % 4
    f32 = lambda a: np.ascontiguousarray(np.asarray(a, dtype=np.float32))
    c["xp"] = f32(inputs["x_prompt"][core])
    xs = np.asarray(inputs["x_sample"][sb], dtype=np.float32)
    c["xsf"] = np.ascontiguousarray(xs)
    ext = np.zeros((EXT, D_MODEL), np.float32)
    lo, hi = blk * T - HALO, (blk + 1) * T + HALO
    slo, shi = max(lo, 0), min(hi, DEC_SEQ)
    ext[slo - lo: shi - lo] = xs[slo:shi]
    c["xso"] = ext
    for k in ("w_in", "w_out", "conv_w", "conv_b", "filt_w1", "filt_b1", "filt_w2", "filt_b2", "filt_w3",
              "filt_b3", "filt_freq", "filt_w4", "hyena_d", "attn_norm_g", "hyena_norm_g", "ln_g", "ln_b"):
        c[k] = f32(inputs[k][0])
    lags = [(SEQ, 0)] + [(DEC_SEQ, dd) for dd in range(-3, 4)] + [(DEC_SEQ, blk - j) for j in range(4)]
    key = ("slots", blk)
    if key not in _CONST_CACHE:
        zT = np.zeros((12, 33, NFFT), np.float32)
        sel = np.zeros((12, 128, NFFT), NPBF)
        ntl = np.zeros((12, 128, 64), np.float32)
        flag = np.zeros((1, 12), np.float32)
        for s, (L, dd) in enumerate(lags):
            k2 = ("slot", L, dd)
            if k2 not in _CONST_CACHE:
                _CONST_CACHE[k2] = filter_slot_tables(L, dd)
            zT[s], sel[s], ntl[s], flag[0, s] = _CONST_CACHE[k2]
        _CONST_CACHE[key] = (zT, sel, ntl, flag)
    c["f_zT"], c["f_sel"], c["f_ntl"], c["f_flag"] = _CONST_CACHE[key]
    ek = ("E", blk)
    if ek not in _CONST_CACHE:
        _CONST_CACHE[ek] = attn_tables(blk > 0, blk < 3)
    c["E_P"], c["EL_S"], c["ER_S"] = _CONST_CACHE[ek]
    return c


_PROG = {}


def kernel(**inputs):
    if "nc" not in _PROG:
        _PROG["nc"] = build_program()[0]
    nc = _PROG["nc"]
    in_maps = [core_inputs(core, inputs) for core in range(8)]
    res = run_bass_kernel_spmd(nc, in_maps, core_ids=list(range(8)))
    yp = np.stack([np.asarray(res.results[c]["yp"], dtype=np.float32) for c in range(8)], axis=0)
    ys = np.zeros((2, DEC_SEQ, D_MODEL), np.float32)
    for c in range(8):
        ys[c // 4, (c % 4) * T:(c % 4 + 1) * T] = np.asarray(res.results[c]["ys"], dtype=np.float32)
    return (yp, ys)
```

```python
import math
from contextlib import ExitStack

import numpy as np
import ml_dtypes

import concourse.bass as bass
import concourse.mybir as mybir
from concourse.bass_utils import run_bass_kernel_spmd

F32 = mybir.dt.float32
BF16 = mybir.dt.bfloat16
AF = mybir.ActivationFunctionType
ALU = mybir.AluOpType
NPBF = ml_dtypes.bfloat16

SEM_EPOCH = 30000
TWO_PI = 2.0 * math.pi

D_MODEL = 1024
SEQ = 4096
DEC_SEQ = 16384
NCH = 512
T = 4096
NFFT = 8192
HALO = 1024
EXT = T + 2 * HALO
PATTERNS = ((128, 1), (512, 4), (2048, 16))
LN_EPS = 1e-5
RMS_EPS = 1e-6
ALPHA = 2.0 ** 0.25


class Buf:
    __slots__ = ("name", "w", "r")

    def __init__(self, name):
        self.name = name
        self.w = {}
        self.r = {}


class Eng:
    def __init__(self, kb, name, handle):
        self.kb = kb
        self.name = name
        self.h = handle
        self.ops = []
        self.sem = None
        self.cnt = 0
        self.waited = {}
        self.last_tok = None
        self.pending = False

    def new_epoch(self):
        self.sem = self.kb.new_sem(self.name)
        self.cnt = 0


class KB:
    def __init__(self, nc, n_lanes=8):
        self.nc = nc
        self.sems = []
        self._stack = None
        self.eng = {}
        for name, h in (("pe", nc.tensor), ("act", nc.scalar), ("dve", nc.vector),
                        ("pool", nc.gpsimd), ("sp", nc.sync)):
            self.eng[name] = Eng(self, name, h)
        self.lanes = {}
        self.n_lanes = n_lanes
        self.lane_rr = {}

    def new_sem(self, name):
        cm = self.nc.semaphore(f"s{len(self.sems)}_{name}")
        s = self._stack.enter_context(cm)
        self.sems.append(s)
        return len(self.sems) - 1

    def _wait(self, e, tok):
        if tok is None:
            return
        sid, val = tok
        if e.waited.get(sid, 0) >= val:
            return
        e.waited[sid] = val
        sem = self.sems[sid]
        e.ops.append(lambda h, sem=sem, val=val: h.wait_ge(sem, val))

    def _deps(self, e, reads, writes, dwrites=()):
        toks = []
        for b in reads:
            toks.extend(b.w.items())
        for b in writes:
            toks.extend(b.w.items())
            toks.extend(b.r.items())
        for b in dwrites:
            toks.extend(b.r.items())
        for t in toks:
            if e.name == "pe" and e.sem is not None and t[0] == e.sem:
                continue
            self._wait(e, t)

    def _mark(self, tok, reads, writes, dwrites=()):
        sid, val = tok
        for b in list(writes) + list(dwrites):
            if b.w.get(sid, 0) < val:
                b.w[sid] = val
        for b in reads:
            if b.r.get(sid, 0) < val:
                b.r[sid] = val

    def op(self, eng, fn, reads=(), writes=(), inc=True, dwrites=()):
        e = self.eng[eng]
        if e.sem is None or e.cnt >= SEM_EPOCH:
            e.new_epoch()
        self._deps(e, reads, writes, dwrites)
        if inc:
            e.cnt += 1
            tok = (e.sem, e.cnt)
            sem = self.sems[e.sem]
            e.ops.append(lambda h, fn=fn, sem=sem: fn(h).then_inc(sem, 1))
            e.last_tok = tok
            e.pending = False
        else:
            tok = (e.sem, e.cnt + 1)
            e.ops.append(lambda h, fn=fn: fn(h))
            e.pending = True
        self._mark(tok, reads, writes, dwrites)
        return tok

    def dma(self, q, out, in_, reads=(), writes=(), slow=False, dwrites=()):
        e = self.eng[q]
        if q not in self.lanes:
            self.lanes[q] = [{"sem": None, "val": 0} for _ in range(self.n_lanes)]
            self.lane_rr[q] = 0
        ln = self.lanes[q][self.lane_rr[q] % self.n_lanes]
        self.lane_rr[q] += 1
        if ln["sem"] is None or ln["val"] >= 60000:
            ln["sem"] = self.new_sem(f"dma_{q}")
            ln["val"] = 0
        else:
            self._wait(e, (ln["sem"], ln["val"]))
        self._deps(e, reads, writes, dwrites)
        ln["val"] += 16
        tok = (ln["sem"], ln["val"])
        sem = self.sems[ln["sem"]]
        if slow:
            e.ops.append(lambda h, out=out, in_=in_, sem=sem:
                         h.dma_start(out=out, in_=in_, allow_slow_non_contiguous=True).then_inc(sem, 16))
        else:
            e.ops.append(lambda h, out=out, in_=in_, sem=sem: h.dma_start(out=out, in_=in_).then_inc(sem, 16))
        self._mark(tok, reads, writes, dwrites)
        return tok

    def all_tokens(self):
        toks = []
        for e in self.eng.values():
            assert not e.pending, f"engine {e.name} has a trailing non-inc op"
            if e.last_tok is not None:
                toks.append(e.last_tok)
        for lanes in self.lanes.values():
            for ln in lanes:
                if ln["sem"] is not None and ln["val"] > 0:
                    toks.append((ln["sem"], ln["val"]))
        return toks

    def barrier(self, engines=None):
        toks = self.all_tokens()
        for name, e in self.eng.items():
            if engines is not None and name not in engines:
                continue
            for t in toks:
                if e.sem is not None and t[0] == e.sem:
                    continue
                self._wait(e, t)

    def run(self, stack, build):
        self._stack = stack
        build(self)
        self.barrier()
        block = stack.enter_context(self.nc.Block())
        e = self.eng

        @block.sync
        def _(h):
            for f in e["sp"].ops:
                f(h)

        @block.tensor
        def _(h):
            for f in e["pe"].ops:
                f(h)

        @block.scalar
        def _(h):
            for f in e["act"].ops:
                f(h)

        @block.vector
        def _(h):
            for f in e["dve"].ops:
                f(h)

        @block.gpsimd
        def _(h):
            for f in e["pool"].ops:
                f(h)


class Arena:
    def __init__(self, handle, words):
        self.h = handle
        self.words = words
        self.top = 0

    def alloc(self, shape, dt, name="t"):
        free = int(np.prod(shape[1:]))
        nbytes = free * (2 if dt == BF16 else 4)
        w = (nbytes + 3) // 4
        w = (w + 7) // 8 * 8
        assert self.top + w <= self.words, f"arena overflow: {name} {shape} top={self.top} w={w}"
        ap = self.h[:, self.top:self.top + w]
        self.top += w
        if dt == BF16:
            ap = ap.bitcast(BF16)[:, 0:free]
        else:
            ap = ap[:, 0:free]
        if shape[0] < 128:
            ap = ap[0:shape[0], :]
        if len(shape) > 2:
            names = " ".join(f"d{i}" for i in range(len(shape) - 1))
            kw = {f"d{i}": shape[i + 1] for i in range(len(shape) - 1)}
            ap = ap.rearrange(f"p ({names}) -> p {names}", **kw)
        return ap, Buf(name)


def fft_consts():
    n1 = np.arange(128)[:, None]
    k1 = np.arange(64)[None, :]
    th = 2 * np.pi * n1 * (k1 + 0.5) / 128
    FA = np.concatenate([np.cos(th), -np.sin(th)], axis=1)
    n2 = np.arange(64)[:, None, None]
    k1b = np.arange(64)[None, :, None]
    k2 = np.arange(64)[None, None, :]
    ph = 2 * np.pi * (n2 * (k1b + 0.5) / 8192 + n2 * k2 / 64)
    wr = np.cos(ph)
    wi = -np.sin(ph)

    def mk(a0, a1, b0, b1):
        M = np.zeros((64, 64, 2, 128))
        M[:, :, 0, :64] = a0
        M[:, :, 0, 64:] = a1
        M[:, :, 1, :64] = b0
        M[:, :, 1, 64:] = b1
        M = M.reshape(64, -1)
        return np.concatenate([M, M], axis=0)

    FB = mk(wr, wi, -wi, wr)
    FBG1 = mk(wr, wr, -wi, -wi)
    FBG2 = mk(wi, wi, wr, wr)
    k2c = np.arange(64)[:, None]
    n2c = np.arange(64)[None, :]
    t = 2 * np.pi * n2c * k2c / 64
    c, s = np.cos(t), np.sin(t)
    FinvB1 = np.block([[c, s], [-s, c]])
    FinvB2 = np.block([[-s, c], [-c, -s]])
    k1a = np.arange(64)[:, None, None]
    n2a = np.arange(64)[None, :, None]
    n1a = np.arange(64)[None, None, :]
    phi = 2 * np.pi * (k1a + 0.5) * (64 * n1a + n2a) / 8192
    FinvA = np.zeros((64, 64, 2, 64))
    FinvA[:, :, 0, :] = (2.0 / NFFT) * np.cos(phi)
    FinvA[:, :, 1, :] = -(2.0 / NFFT) * np.sin(phi)
    FinvA = FinvA.reshape(64, -1)
    FinvA = np.concatenate([FinvA, FinvA], axis=0)
    bf = lambda a: np.ascontiguousarray(a.astype(np.float32).astype(NPBF))
    return dict(FA=bf(FA), FB=bf(FB), FBG1=bf(FBG1), FBG2=bf(FBG2), FinvB1=bf(FinvB1),
                FinvB2=bf(FinvB2), FinvA=bf(FinvA))


def filter_slot_tables(L, d):
    m = np.arange(NFFT)
    lam = np.where(m < T, d * T + m, d * T - (NFFT - m)).astype(np.int64)
    sign = np.where(m < T, 1.0, -1.0)
    sign[T] = 0.0
    tpos = np.abs(lam)
    valid = tpos < L
    sign = sign * valid
    tpos = np.minimum(tpos, L - 1)
    fwd = lam >= 0
    f32 = np.float32
    tl = np.linspace(0.0, 1.0, L, dtype=f32)
    bands = 16
    w = (f32(2.0 * math.pi) * np.arange(L, dtype=f32) / f32(L)).astype(f32)
    fr = np.linspace(1e-4, bands - 1, bands, dtype=f32)
    ang = (fr[None, :] * w[:, None]).astype(f32)
    z = np.concatenate([tl[:, None], np.cos(ang), -np.sin(ang)], axis=1).astype(f32)
    zT = np.ascontiguousarray(z[tpos].T.astype(f32))
    sel = np.zeros((128, NFFT), np.float32)
    sel[:64] = (sign * fwd)[None, :]
    sel[64:] = (sign * (~fwd))[None, :]
    ntl = (-tl[tpos]).astype(f32).reshape(128, 64)
    flag = 1.0 if d == 0 else 0.0
    return zT, sel.astype(NPBF), np.ascontiguousarray(ntl), flag


def decay_deltas():
    f32 = np.float32
    max_decay = math.log(1e-2) / 0.3
    min_decay = math.log(1e-2) / 1.5
    return np.abs(np.linspace(min_decay, max_decay, NCH, dtype=f32)).astype(f32)


def attn_tables(left_ok, right_ok):
    slopes = np.array([2.0 ** (-8.0 * (i + 1) / 8) for i in range(8)], np.float64)
    p = np.arange(128)[:, None]
    col = np.arange(256)[None, :]
    rel = np.abs(col - 64 - p)
    E = np.zeros((128, 3, 8, 256), np.float32)
    EL = np.zeros((64, 3, 8, 64), np.float32)
    ER = np.zeros((64, 3, 8, 64), np.float32)
    pk = np.arange(64)[:, None]
    q = np.arange(64)[None, :]
    relL = q + 64 - pk
    relR = pk + 64 - q
    for pi, (_, r) in enumerate(PATTERNS):
        for h in range(8):
            E[:, pi, h, :] = np.where(rel <= 64, np.exp(-slopes[h] * r * rel), 0.0)
            if left_ok:
                EL[:, pi, h, :] = np.where(relL <= 64, np.exp(-slopes[h] * r * relL), 0.0)
            if right_ok:
                ER[:, pi, h, :] = np.where(relR <= 64, np.exp(-slopes[h] * r * relR), 0.0)
    ELR = np.zeros((128, 3, 8, 64), np.float32)
    bf = lambda a: np.ascontiguousarray(a.astype(NPBF))
    return bf(E.reshape(128, -1)), bf(EL.reshape(64, -1)), bf(ER.reshape(64, -1))


class Prog:
    def __init__(self, nc, st, phases, ext_in=(), ext_out=()):
        self.ext_in = set(ext_in)
        self.ext_out = set(ext_out)
        self.nc = nc
        self.st = st
        self.phases = phases
        self.d = {}
        self.ps_i = 0
        self.ev_i = 0
        self.fresh = {}
        self.pools = {"acc": [0, 1], "tmp": [2, 3, 4, 5, 6, 7]}
        self.pool_i = {}

    def inp(self, name, shape, dt=F32):
        self.d[name] = self.nc.dram_tensor(name, list(shape), dt, kind="ExternalInput").ap()
        return self.d[name]

    def outp(self, name, shape, dt=F32):
        self.d[name] = self.nc.dram_tensor(name, list(shape), dt, kind="ExternalOutput").ap()
        return self.d[name]

    def scr(self, name, shape, dt=BF16):
        kind = "Internal"
        if name in self.ext_in:
            kind = "ExternalInput"
        if name in self.ext_out:
            kind = "ExternalOutput"
        self.d[name] = self.nc.dram_tensor(name, list(shape), dt, kind=kind).ap()
        return self.d[name]

    def ps(self, pool=None):
        if pool is None:
            i = self.ps_i % 8
            self.ps_i += 1
        else:
            lst = self.pools[pool]
            k = self.pool_i.get(pool, 0)
            self.pool_i[pool] = k + 1
            i = lst[k % len(lst)]
        self.fresh[self.bankb[i]] = True
        return self.bank[i], self.bankb[i]

    def evac(self, out, in_, reads, writes, eng=None, disjoint=True):
        kb = self.kb
        if eng is None:
            eng = ("act", "dve")[self.ev_i % 2]
            self.ev_i += 1
        w, dw = ((), writes) if disjoint else (writes, ())
        if eng == "act":
            kb.op("act", lambda e: e.activation(out=out, in_=in_, func=AF.Copy), reads, w, dwrites=dw)
        else:
            kb.op(eng, lambda e: e.tensor_copy(out=out, in_=in_), reads, w, dwrites=dw)

    def mm(self, out, lhsT, rhs, start, stop, reads, writes, inc):
        bb = writes[0]
        start = self.fresh.get(bb, False)
        self.fresh[bb] = False
        self.kb.op("pe", lambda e: e.matmul(out, lhsT=lhsT, rhs=rhs, start=start, stop=stop, skip_group_check=True),
                   reads, writes, inc=inc)

    def tr(self, out, in_, ident, reads, writes, inc):
        self.kb.op("pe", lambda e: e.transpose(out, in_, ident), reads, writes, inc=inc)

    def reset(self):
        self.kb.barrier()
        self.hiwater = max(getattr(self, "hiwater", 0), self.A.top)
        self.A.top = self.A_base

    def declare(self):
        inp, scr = self.inp, self.scr
        inp("xp", [SEQ, D_MODEL]); inp("xsf", [DEC_SEQ, D_MODEL]); inp("xso", [EXT, D_MODEL])
        inp("w_in", [D_MODEL, 4096]); inp("w_out", [D_MODEL, D_MODEL])
        inp("conv_w", [3, 1536]); inp("conv_b", [1536])
        inp("filt_w1", [33, 64]); inp("filt_b1", [64]); inp("filt_w2", [64, 64]); inp("filt_b2", [64])
        inp("filt_w3", [64, 64]); inp("filt_b3", [64]); inp("filt_freq", [3, 64]); inp("filt_w4", [64, 2048])
        inp("hyena_d", [2, NCH]); inp("attn_norm_g", [512]); inp("hyena_norm_g", [512])
        inp("ln_g", [D_MODEL]); inp("ln_b", [D_MODEL])
        inp("ident", [128, 128], BF16); inp("ones", [128, 128], BF16); inp("onesz", [128, 256], BF16)
        inp("FA", [128, 128], BF16); inp("FB", [128, 64 * 256], BF16)
        inp("FBG1", [128, 64 * 256], BF16); inp("FBG2", [128, 64 * 256], BF16)
        inp("FinvB1", [128, 128], BF16); inp("FinvB2", [128, 128], BF16); inp("FinvA", [128, 64 * 128], BF16)
        inp("f_zT", [12, 33, NFFT]); inp("f_sel", [12, 128, NFFT], BF16); inp("f_ntl", [12, 128, 64])
        inp("f_flag", [1, 12]); inp("delta", [NCH])
        inp("E_P", [128, 3 * 8 * 256], BF16)
        inp("EL_S", [64, 3 * 8 * 64], BF16); inp("ER_S", [64, 3 * 8 * 64], BF16)
        self.outp("yp", [SEQ, D_MODEL]); self.outp("ys", [T, D_MODEL])
        scr("xT_P", [8, 128, SEQ]); scr("xT_SF", [8, 128, DEC_SEQ]); scr("xT_SO", [8, 128, EXT])
        for nm, nb in (("P", 1), ("S", 4)):
            scr(f"UT0_{nm}", [nb, 64, NCH * 64]); scr(f"X1T_{nm}", [nb, 64, NCH * 64])
            scr(f"X2T_{nm}", [64, NCH * 64]); scr(f"Z1T_{nm}", [nb, 64, NCH * 64]); scr(f"Z2T_{nm}", [64, NCH * 64])
            scr(f"XS_{nm}", [nb, 128, NCH * 64])
            scr(f"MA_{nm}", [128, 4 * T]); scr(f"MH_{nm}", [128, 4 * T])
        scr("HAUG", [12, 128, NFFT])
        scr("G_P", [2, 2, 128, NCH * 64])
        scr("G_S1", [7, 2, 128, NCH * 64])
        scr("G_S2", [4, 2, 128, NCH * 64])

    def setup(self):
        A, kb, d = self.A, self.kb, self.d
        self.ident, self.identb = A.alloc([128, 128], BF16, "ident")
        kb.dma("sp", self.ident, d["ident"], writes=[self.identb])
        self.ones, self.onesb = A.alloc([128, 128], BF16, "ones")
        kb.dma("sp", self.ones, d["ones"], writes=[self.onesb])
        self.A_base = A.top

    def phase_xt(self, xname, xTname, ntok):
        A, kb, d = self.A, self.kb, self.d
        x, xT = d[xname], d[xTname]
        x32 = [A.alloc([128, 1024], F32, f"x32_{i}") for i in range(4)]
        xb = [A.alloc([128, 1024], BF16, f"xb_{i}") for i in range(4)]
        xT4 = [A.alloc([128, 8, 512], BF16, f"xT4_{i}") for i in range(2)]
        for t in range(ntok // 128):
            a32, b32 = x32[t % 4]
            ab, bb = xb[t % 4]
            kb.dma("sp", a32, x[t * 128:(t + 1) * 128, :], writes=[b32])
            kb.op("pool" if t % 4 == 3 else "dve", lambda e, o=ab, i=a32: e.tensor_copy(out=o, in_=i), [b32], [bb])
            bank, bankb = self.ps()
            bbf = bank.bitcast(BF16)
            for kc in range(8):
                self.tr(bbf[:, kc * 128:(kc + 1) * 128], ab[:, kc * 128:(kc + 1) * 128], self.ident,
                        [bb, self.identb], [bankb], inc=(kc == 7))
            g = t // 4
            q = t % 4
            a4, b4 = xT4[g % 2]
            self.evac(a4[:, :, q * 128:(q + 1) * 128], bbf.rearrange("p (k t) -> p k t", k=8), [bankb], [b4], eng="act")
            if q == 3:
                kb.dma("act", xT[:, :, g * 512:(g + 1) * 512].rearrange("k p t -> p k t"), a4, reads=[b4])

    def phase_h1(self, xTname, tok0, zero_l, zero_r, groups):
        A, kb, d = self.A, self.kb, self.d
        xT = d[xTname]
        xblk, xblkb = A.alloc([128, 8, T + 2], BF16, "xblk")
        src = xT[:, :, tok0:tok0 + T].rearrange("k p t -> p k t")
        kb.dma("sp", xblk[:, :, 1:T + 1], src, writes=[xblkb])
        if zero_l:
            kb.op("pool", lambda e: e.memset(xblk[:, :, 0:1], 0.0), [], [xblkb])
        else:
            kb.dma("sp", xblk[:, :, 0:1], xT[:, :, tok0 - 1:tok0].rearrange("k p t -> p k t"), writes=[xblkb], slow=True)
        if zero_r:
            kb.op("pool", lambda e: e.memset(xblk[:, :, T + 1:T + 2], 0.0), [], [xblkb])
        else:
            kb.dma("sp", xblk[:, :, T + 1:T + 2], xT[:, :, tok0 + T:tok0 + T + 1].rearrange("k p t -> p k t"),
                   writes=[xblkb], slow=True)
        w32 = [A.alloc([128, 8, 128], F32, f"w32_{i}") for i in range(3)]
        wb = [A.alloc([128, 8, 128], BF16, f"wb_{i}") for i in range(3)]
        cw = [A.alloc([128, 4], F32, f"cw_{i}") for i in range(3)]
        pfs = [A.alloc([128, T + 2], BF16, f"pf{i}") for i in range(2)]
        ufs = [A.alloc([128, T], F32, f"uf{i}") for i in range(2)]
        ubs = [A.alloc([128, T], BF16, f"ub{i}") for i in range(2)]
        utg = [A.alloc([64, 128, 64], BF16, f"utg_{i}") for i in range(2)]
        def load_w(gi):
            if gi >= len(groups):
                return
            co, cc, dst, dg = groups[gi]
            a32, b32 = w32[gi % 3]
            awb, bwb = wb[gi % 3]
            acw, bcw = cw[gi % 3]
            kb.dma("sp", a32, d["w_in"][:, co:co + 128].rearrange("(k p) c -> p k c", p=128), writes=[b32])
            kb.op("pool", lambda e, o=awb, i=a32: e.tensor_copy(out=o, in_=i), [b32], [bwb])
            kb.dma("sp", acw[:, 0:3], d["conv_w"][:, cc:cc + 128].rearrange("j c -> c j"), writes=[bcw], slow=True)
            kb.dma("sp", acw[:, 3:4], d["conv_b"][cc:cc + 128].rearrange("(c o) -> c o", o=1), writes=[bcw], slow=True)

        def stage1(gi):
            co, cc, dst, dg = groups[gi]
            pf, pfb = pfs[gi % 2]
            uf, ufb = ufs[gi % 2]
            ub, ubb = ubs[gi % 2]
            awb, bwb = wb[gi % 3]
            acw, bcw = cw[gi % 3]
            load_w(gi + 1)
            nchunk = (T + 2 + 511) // 512
            for ch in range(nchunk):
                c0 = ch * 512
                n = min(512, T + 2 - c0)
                bank, bankb = self.ps()
                for kc in range(8):
                    self.mm(bank[:, 0:n], awb[:, kc, :], xblk[:, kc, c0:c0 + n], kc == 0, kc == 7,
                            [bwb, xblkb], [bankb], inc=(kc == 7))
                self.evac(pf[:, c0:c0 + n], bank[:, 0:n], [bankb], [pfb], eng=("act", "act", "dve")[ch % 3])
            kb.op("act", lambda e: e.activation(out=uf, in_=pf[:, 1:T + 1], func=AF.Identity,
                                                bias=acw[:, 3:4], scale=acw[:, 1:2]), [pfb, bcw], [ufb])
            kb.op("dve", lambda e: e.scalar_tensor_tensor(out=uf, in0=pf[:, 0:T], scalar=acw[:, 0:1], in1=uf,
                                                          op0=ALU.mult, op1=ALU.add), [pfb, bcw, ufb], [ufb])
            kb.op("dve", lambda e: e.scalar_tensor_tensor(out=ub, in0=pf[:, 2:T + 2], scalar=acw[:, 2:3], in1=uf,
                                                          op0=ALU.mult, op1=ALU.add), [pfb, bcw, ufb], [ubb])

        def stage2(gi):
            co, cc, dst, dg = groups[gi]
            ub, ubb = ubs[gi % 2]
            ubv = ub.rearrange("p (a b) -> p a b", b=64)
            autg, butg = utg[gi % 2]
            for ng in range(8):
                bank, bankb = self.ps()
                bbf = bank.bitcast(BF16)
                for nn in range(8):
                    n2 = ng * 8 + nn
                    self.tr(bbf[0:64, nn * 128:(nn + 1) * 128], ubv[:, :, n2], self.ident,
                            [ubb, self.identb], [bankb], inc=(nn == 7))
                self.evac(autg[:, :, ng * 8:(ng + 1) * 8], bbf[0:64, :].rearrange("p (n c) -> p c n", n=8),
                          [bankb], [butg], eng=("act", "act", "dve")[ng % 3])
            kb.dma("act", dst[:, dg * 8192:(dg + 1) * 8192], autg.rearrange("p c n -> p (c n)"), reads=[butg])

        ng_ = len(groups)
        load_w(0)
        for gi in range(ng_ + 1):
            if gi < ng_:
                stage1(gi)
            if gi >= 1:
                stage2(gi - 1)

    def fwd_cg(self, ut, utb, K, FA, FAb, variants, ast, astb, emit):
        astv = ast.rearrange("p (c r k) -> p c r k", r=2, k=64)
        for g in range(16):
            bank, bankb = self.ps()
            for q in range(4):
                cp = g * 4 + q
                self.mm(bank[:, q * 128:(q + 1) * 128], ut[:, cp * 128:(cp + 1) * 128], FA[0:K, :], True, True,
                        [utb, FAb], [bankb], inc=(q == 3))
            self.evac(ast[:, g * 512:(g + 1) * 512], bank, [bankb], [astb])
        for v, (FBt, FBb) in enumerate(variants):
            def fill(Xv, Xb, FBt=FBt, FBb=FBb):
                for kg in range(8):
                    b0, b0b = self.ps()
                    b1, b1b = self.ps()
                    for kk in range(8):
                        k1 = kg * 8 + kk
                        for ri in range(2):
                            for par, (bk, bkb) in enumerate(((b0, b0b), (b1, b1b))):
                                self.mm(bk[:, kk * 64:(kk + 1) * 64], FBt[par * 64:(par + 1) * 64, k1, ri, :],
                                        astv[par * 64:(par + 1) * 64, :, ri, k1], ri == 0, ri == 1,
                                        [FBb, astb], [bkb], inc=(kk == 7 and ri == 1))
                    for par, (bk, bkb) in enumerate(((b0, b0b), (b1, b1b))):
                        self.evac(Xv[:, :, par, kg * 8:(kg + 1) * 8], bk.rearrange("p (k c) -> p c k", k=8), [bkb], [Xb])
            emit(v, fill)

    def phase_hx(self, srcs, dsts):
        A, kb, d = self.A, self.kb, self.d
        FA, FAb = A.alloc([128, 128], BF16, "FA")
        kb.dma("sp", FA, d["FA"], writes=[FAb])
        FB, FBb = A.alloc([128, 64, 2, 128], BF16, "FB")
        kb.dma("sp", FB.rearrange("p a b c -> p (a b c)"), d["FB"], writes=[FBb])
        uts = [A.alloc([64, 8192], BF16, f"ut_{i}") for i in range(2)]
        asts = [A.alloc([128, 8192], BF16, f"ast_{i}") for i in range(2)]
        xts = [A.alloc([128, 64, 2, 64], BF16, f"xt_{i}") for i in range(2)]
        it = 0
        for src, dst in zip(srcs, dsts):
            for cg in range(4):
                ut, utb = uts[it % 2]
                ast, astb = asts[it % 2]
                xt, xtb = xts[it % 2]
                it += 1
                kb.dma("sp", ut, src[:, cg * 8192:(cg + 1) * 8192], writes=[utb])

                def emit(v, fill, xt=xt, xtb=xtb, dst=dst, cg=cg):
                    fill(xt, xtb)
                    kb.dma("act", dst[:, cg * 8192:(cg + 1) * 8192], xt.rearrange("p a b c -> p (a b c)"), reads=[xtb])
                self.fwd_cg(ut, utb, 64, FA, FAb, [(FB, FBb)], ast, astb, emit)

    def phase_hmlp(self, slot_ids):
        A, kb, d = self.A, self.kb, self.d
        inv2pi = 1.0 / TWO_PI
        fq, fqb = A.alloc([128, 3, 64], F32, "fq")
        kb.dma("sp", fq.rearrange("p a b -> p (a b)"), d["filt_freq"].rearrange("a b -> (a b)").partition_broadcast(128),
               writes=[fqb])
        fqc, fqcb = A.alloc([128, 3], F32, "fqc")
        for half in range(2):
            kb.dma("sp", fqc[half * 64:(half + 1) * 64, :], d["filt_freq"].rearrange("a b -> b a"), writes=[fqcb], slow=True)
        w1p, w1pb = A.alloc([33, 64], F32, "w1p")
        kb.dma("sp", w1p, d["filt_w1"], writes=[w1pb])
        kb.op("dve", lambda e: e.scalar_tensor_tensor(out=w1p, in0=w1p, scalar=inv2pi, in1=fq[0:33, 0, :],
                                                       op0=ALU.mult, op1=ALU.mult), [w1pb, fqb], [w1pb])
        w2f, w2fb = A.alloc([64, 64], F32, "w2f")
        kb.dma("sp", w2f, d["filt_w2"], writes=[w2fb])
        w2p, w2pb = A.alloc([64, 64], BF16, "w2p")
        kb.op("dve", lambda e: e.scalar_tensor_tensor(out=w2p, in0=w2f, scalar=inv2pi, in1=fq[0:64, 1, :],
                                                       op0=ALU.mult, op1=ALU.mult), [w2fb, fqb], [w2pb])
        w3f, w3fb = A.alloc([64, 64], F32, "w3f")
        kb.dma("sp", w3f, d["filt_w3"], writes=[w3fb])
        w3p, w3pb = A.alloc([64, 2, 64], BF16, "w3p")
        for dup in range(2):
            kb.op("dve", lambda e, dup=dup: e.scalar_tensor_tensor(out=w3p[:, dup, :], in0=w3f, scalar=inv2pi,
                                                                    in1=fq[0:64, 2, :], op0=ALU.mult, op1=ALU.mult),
                  [w3fb, fqb], [w3pb])
        bp, bpb = A.alloc([128, 3], F32, "bp")
        for l, nm in enumerate(("filt_b1", "filt_b2", "filt_b3")):
            for half in range(2):
                kb.dma("sp", bp[half * 64:(half + 1) * 64, l:l + 1], d[nm].rearrange("(c o) -> c o", o=1), writes=[bpb], slow=True)
        kb.op("dve", lambda e: e.scalar_tensor_tensor(out=bp, in0=bp, scalar=inv2pi, in1=fqc, op0=ALU.mult, op1=ALU.mult),
              [bpb, fqcb], [bpb])
        haugs = [A.alloc([128, NFFT], BF16, f"haug{i}") for i in range(2)]
        GC = 8
        zts = [A.alloc([33, GC * 512], F32, f"zt_{i}") for i in range(2)]
        sels = [A.alloc([128, GC * 512], BF16, f"sel_{i}") for i in range(2)]
        uu = [A.alloc([128, 512], F32, f"uu_{i}") for i in range(GC)]
        vv = [A.alloc([128, 512], F32, f"vv_{i}") for i in range(GC)]
        hh = [[A.alloc([128, 512], BF16, f"hh_{l}_{i}") for i in range(GC)] for l in range(3)]
        gi = 0
        for si, slot in enumerate(slot_ids):
            haug, haugb = haugs[si % 2]
            for cgp in range(16 // GC):
                zt, ztb = zts[gi % 2]
                sl, slb = sels[gi % 2]
                gi += 1
                kb.dma("sp", zt, d["f_zT"][slot][:, cgp * GC * 512:(cgp + 1) * GC * 512], writes=[ztb])
                kb.dma("sp", sl, d["f_sel"][slot][:, cgp * GC * 512:(cgp + 1) * GC * 512], writes=[slb])
                prev = [(zt[:, c * 512:(c + 1) * 512], ztb) for c in range(GC)]
                for l in range(3):
                    P_ = 128 if l == 2 else 64
                    lhs, lhsb = ((w1p, w1pb), (w2p, w2pb), (w3p.rearrange("p a b -> p (a b)"), w3pb))[l]
                    new = []
                    for c in range(GC):
                        pin, pinb = prev[c]
                        bank, bankb = self.ps()
                        self.mm(bank[0:P_, :], lhs, pin, True, True, [lhsb, pinb], [bankb], inc=True)
                        u, ub_ = uu[c]
                        kb.op("act", lambda e, u=u, bank=bank, P_=P_, l=l: e.activation(
                            out=u[0:P_, :], in_=bank[0:P_, :], func=AF.Identity, bias=bp[0:P_, l:l + 1], scale=1.0),
                            [bankb, bpb], [ub_])
                    for c in range(GC):
                        u, ub_ = uu[c]
                        v_, vb_ = vv[c]
                        kb.op("dve", lambda e, u=u, v_=v_, P_=P_: e.scalar_tensor_tensor(
                            out=v_[0:P_, :], in0=u[0:P_, :], scalar=0.5, in1=u[0:P_, :], op0=ALU.is_gt, op1=ALU.subtract),
                            [ub_], [vb_])
                    for c in range(GC):
                        u, ub_ = uu[c]
                        v_, vb_ = vv[c]
                        kb.op("dve", lambda e, u=u, v_=v_, P_=P_: e.scalar_tensor_tensor(
                            out=v_[0:P_, :], in0=u[0:P_, :], scalar=-0.5, in1=v_[0:P_, :], op0=ALU.is_lt, op1=ALU.subtract),
                            [ub_, vb_], [vb_])
                    for c in range(GC):
                        v_, vb_ = vv[c]
                        h_, hb_ = hh[l][c]
                        kb.op("act", lambda e, h_=h_, v_=v_, P_=P_: e.activation(
                            out=h_[0:P_, :], in_=v_[0:P_, :], func=AF.Sin, scale=TWO_PI), [vb_], [hb_])
                        new.append((h_[0:P_, :], hb_))
                    prev = new
                for c in range(GC):
                    pin, pinb = prev[c]
                    ch = cgp * GC + c
                    kb.op("pool", lambda e, pin=pin, sl=sl, c=c, ch=ch, haug=haug: e.tensor_tensor(
                        out=haug.rearrange("p (b a) -> p a b", a=128)[:, ch * 8:(ch + 1) * 8, :],
                        in0=pin.rearrange("p (a b) -> p a b", b=64),
                        in1=sl[:, c * 512:(c + 1) * 512].rearrange("p (a b) -> p a b", b=64), op=ALU.mult),
                        [pinb, slb], dwrites=[haugb])
            kb.dma("act", d["HAUG"][slot], haug, reads=[haugb])

    def phase_hf(self, slots):
        A, kb, d = self.A, self.kb, self.d
        FA, FAb = A.alloc([128, 128], BF16, "FA")
        kb.dma("sp", FA, d["FA"], writes=[FAb])
        FB1, FB1b = A.alloc([128, 64, 2, 128], BF16, "FBG1")
        kb.dma("sp", FB1.rearrange("p a b c -> p (a b c)"), d["FBG1"], writes=[FB1b])
        FB2, FB2b = A.alloc([128, 64, 2, 128], BF16, "FBG2")
        kb.dma("sp", FB2.rearrange("p a b c -> p (a b c)"), d["FBG2"], writes=[FB2b])
        FBs = ((FB1, FB1b), (FB2, FB2b))
        w4f, w4fb = A.alloc([128, 2, 512], F32, "w4f")
        w4v = d["filt_w4"].rearrange("h (o dd c) -> h o dd c", o=2, dd=2)
        for dirn in range(2):
            kb.dma("sp", w4f[dirn * 64:(dirn + 1) * 64, :, :], w4v[:, :, dirn, :], writes=[w4fb])
        w4t, w4tb = A.alloc([128, 2, 512], BF16, "w4t")
        kb.op("dve", lambda e: e.tensor_copy(out=w4t, in_=w4f), [w4fb], [w4tb])
        dl, dlb = A.alloc([128, NCH], F32, "dl")
        kb.dma("sp", dl, d["delta"].partition_broadcast(128), writes=[dlb])
        drow, drowb = A.alloc([1, 2, NCH], F32, "drow")
        kb.dma("sp", drow.rearrange("p a b -> p (a b)"), d["hyena_d"].rearrange("(x a) b -> x (a b)", x=1), writes=[drowb])
        flg, flgb = A.alloc([1, 12], F32, "flg")
        kb.dma("sp", flg, d["f_flag"], writes=[flgb])
        haugs = [A.alloc([128, NFFT], BF16, f"haug{i}") for i in range(2)]
        ntls = [A.alloc([128, 64], F32, f"ntl{i}") for i in range(2)]
        dec4 = [A.alloc([128, 4, 128], F32, f"dec_{i}") for i in range(2)]
        gts = [A.alloc([128, 128, 64], BF16, f"gt{i}") for i in range(2)]
        ast, astb = A.alloc([128, 8192], BF16, "ast")
        astv = ast.rearrange("p (c r k) -> p c r k", r=2, k=64)
        xts = [A.alloc([128, 64, 2, 64], BF16, f"xt_{i}") for i in range(2)]
        units = []
        for si, (slot, outs) in enumerate(slots):
            for (o, g1dst, g2dst) in outs:
                for cg in range(4):
                    units.append(dict(si=si, slot=slot, o=o, cg=cg, dst=(g1dst, g2dst)))
        loaded = set()

        def load_slot(si):
            if si in loaded or si >= len(slots):
                return
            loaded.add(si)
            slot = slots[si][0]
            kb.dma("sp", haugs[si % 2][0], d["HAUG"][slot], writes=[haugs[si % 2][1]])
            kb.dma("sp", ntls[si % 2][0], d["f_ntl"][slot], writes=[ntls[si % 2][1]])

        dci = [0]

        def fg_step(ui, ng):
            u = units[ui]
            if ng == 0:
                load_slot(u["si"])
                load_slot(u["si"] + 1)
            haug, haugb = haugs[u["si"] % 2]
            ntl, ntlb = ntls[u["si"] % 2]
            haugv = haug.rearrange("p (b a) -> p a b", a=128)
            gt, gtb = gts[ui % 2]
            o, cg = u["o"], u["cg"]
            bank, bankb = self.ps()
            dc, dcb = dec4[dci[0] % 2]
            dci[0] += 1
            for nn in range(4):
                n2 = ng * 4 + nn
                kb.op("act", lambda e, nn=nn, n2=n2: e.activation(
                    out=dc[:, nn, :], in_=dl[:, cg * 128:(cg + 1) * 128], func=AF.Exp,
                    scale=ntl[:, n2:n2 + 1]), [dlb, ntlb], [dcb])
                self.mm(bank[:, nn * 128:(nn + 1) * 128], haugv[:, :, n2], w4t[:, o, cg * 128:(cg + 1) * 128],
                        True, True, [haugb, w4tb], [bankb], inc=(nn == 3))
            kb.op("dve", lambda e: e.tensor_tensor(
                out=gt[:, :, ng * 4:(ng + 1) * 4], in0=bank.rearrange("p (n c) -> p c n", n=4),
                in1=dc.rearrange("p n c -> p c n"), op=ALU.mult), [bankb, dcb], dwrites=[gtb])
            if ng == 15:
                slot = u["slot"]
                kb.op("dve", lambda e: e.scalar_tensor_tensor(
                    out=gt[0:1, :, 0], in0=drow[0:1, o, cg * 128:(cg + 1) * 128], scalar=flg[0:1, slot:slot + 1],
                    in1=gt[0:1, :, 0], op0=ALU.mult, op1=ALU.add), [drowb, flgb, gtb], [gtb])

        def da(ui):
            gt, gtb = gts[ui % 2]
            ut = gt.rearrange("p c n -> p (c n)")
            for g in range(16):
                bank, bankb = self.ps()
                for q in range(4):
                    cp = g * 4 + q
                    self.mm(bank[:, q * 128:(q + 1) * 128], ut[:, cp * 128:(cp + 1) * 128], FA, True, True,
                            [gtb, FAb], [bankb], inc=(q == 3))
                self.evac(ast[:, g * 512:(g + 1) * 512], bank, [bankb], [astb])

        def mb_step(ui, step):
            u = units[ui]
            v, kg = step // 8, step % 8
            FBt, FBb = FBs[v]
            Xv, Xb = xts[(ui * 2 + v) % 2]
            b0, b0b = self.ps()
            b1, b1b = self.ps()
            for kk in range(8):
                k1 = kg * 8 + kk
                for ri in range(2):
                    for par, (bk, bkb) in enumerate(((b0, b0b), (b1, b1b))):
                        self.mm(bk[:, kk * 64:(kk + 1) * 64], FBt[par * 64:(par + 1) * 64, k1, ri, :],
                                astv[par * 64:(par + 1) * 64, :, ri, k1], ri == 0, ri == 1,
                                [FBb, astb], [bkb], inc=(kk == 7 and ri == 1))
            for par, (bk, bkb) in enumerate(((b0, b0b), (b1, b1b))):
                self.evac(Xv[:, :, par, kg * 8:(kg + 1) * 8], bk.rearrange("p (k c) -> p c k", k=8), [bkb], [Xb])
            if kg == 7:
                cg = u["cg"]
                kb.dma("act", u["dst"][v][:, cg * 8192:(cg + 1) * 8192], Xv.rearrange("p a b c -> p (a b c)"), reads=[Xb])

        n = len(units)
        for ng in range(16):
            fg_step(0, ng)
        da(0)
        for ui in range(n):
            for step in range(16):
                if ui + 1 < n:
                    fg_step(ui + 1, step)
                mb_step(ui, step)
            if ui + 1 < n:
                da(ui + 1)

    def phase_hp(self, outs, xs, nin):
        A, kb, d = self.A, self.kb, self.d
        FiB1, FiB1b = A.alloc([128, 128], BF16, "FiB1")
        kb.dma("sp", FiB1, d["FinvB1"], writes=[FiB1b])
        FiB2, FiB2b = A.alloc([128, 128], BF16, "FiB2")
        kb.dma("sp", FiB2, d["FinvB2"], writes=[FiB2b])
        FiA, FiAb = A.alloc([128, 64, 2, 64], BF16, "FiA")
        kb.dma("sp", FiA.rearrange("p a b c -> p (a b c)"), d["FinvA"], writes=[FiAb])
        NB = 3
        xres = [A.alloc([128, 8192], BF16, f"xres_{j}") for j in range(nin)]
        g1 = [A.alloc([128, 2048], BF16, f"g1_{i}") for i in range(NB)]
        g2 = [A.alloc([128, 2048], BF16, f"g2_{i}") for i in range(NB)]
        t1 = [A.alloc([128, 2048], BF16, f"t1_{i}") for i in range(2)]
        t2 = [A.alloc([128, 2048], BF16, f"t2_{i}") for i in range(2)]
        bsts = [A.alloc([128, 64, 2, 64], BF16, f"bst_{i}") for i in range(2)]
        xgs = [A.alloc([64, 64, 2, 64], BF16, f"xg_{i}") for i in range(1)]
        zts = [A.alloc([64, 64, 2, 64], BF16, f"zt_{i}") for i in range(2)]
        li = 0
        ti = 0
        ci = 0
        for cg in range(4):
            for j in range(nin):
                kb.dma("sp", xres[j][0], xs[j][:, cg * 8192:(cg + 1) * 8192], writes=[xres[j][1]])
            for (gl, gate, dst) in outs:
                bst, bstb = bsts[ci % 2]
                xg, xgb = xgs[0]
                zt, ztb = zts[ci % 2]
                ci += 1
                kb.dma("sp", xg.rearrange("p a b c -> p (a b c)"), gate[:, cg * 8192:(cg + 1) * 8192], writes=[xgb])
                bstf = bst.rearrange("p a b c -> p (a b c)")
                for cq in range(4):
                    c0 = (cg * 128 + cq * 32) * 64
                    banks = [self.ps() for _ in range(4)]
                    for j in range(nin):
                        ax, bx = xres[j][0][:, cq * 2048:(cq + 1) * 2048], xres[j][1]
                        a1, b1 = g1[li % NB]
                        a2, b2 = g2[li % NB]
                        li += 1
                        kb.dma("sp", a1, gl[j][0][:, c0:c0 + 2048], writes=[b1])
                        kb.dma("sp", a2, gl[j][1][:, c0:c0 + 2048], writes=[b2])
                        at1, bt1 = t1[ti % 2]
                        at2, bt2 = t2[ti % 2]
                        ti += 1
                        kb.op("dve", lambda e, o=at1, a=a1, b=ax: e.tensor_tensor(out=o, in0=a, in1=b, op=ALU.mult),
                              [b1, bx], [bt1])
                        kb.op("dve", lambda e, o=at2, a=a2, b=ax: e.tensor_tensor(
                            out=o[:, 0:1536], in0=a[:, 0:1536], in1=b[:, 0:1536], op=ALU.mult), [b2, bx], dwrites=[bt2])
                        kb.op("pool", lambda e, o=at2, a=a2, b=ax: e.tensor_tensor(
                            out=o[:, 1536:2048], in0=a[:, 1536:2048], in1=b[:, 1536:2048], op=ALU.mult), [b2, bx], dwrites=[bt2])
                        for q in range(16):
                            bk, bkb = banks[q // 4]
                            sl = (q % 4) * 128
                            self.mm(bk[:, sl:sl + 128], at1[:, q * 128:(q + 1) * 128], FiB1, j == 0, False,
                                    [bt1, FiB1b], [bkb], inc=False)
                            self.mm(bk[:, sl:sl + 128], at2[:, q * 128:(q + 1) * 128], FiB2, False, j == nin - 1,
                                    [bt2, FiB2b], [bkb], inc=(q % 4 == 3 or q == 15))
                    for b4 in range(4):
                        bk, bkb = banks[b4]
                        off = (cq * 16 + b4 * 4) * 128
                        self.evac(bstf[:, off:off + 512], bk, [bkb], [bstb], eng="act")
                for ng in range(8):
                    b0, b0b = self.ps()
                    b1_, b1b = self.ps()
                    for nn in range(8):
                        n2 = ng * 8 + nn
                        for ri in range(2):
                            for par, (bk, bkb) in enumerate(((b0, b0b), (b1_, b1b))):
                                self.mm(bk[0:64, nn * 64:(nn + 1) * 64], FiA[par * 64:(par + 1) * 64, n2, ri, :],
                                        bst[par * 64:(par + 1) * 64, :, ri, n2], ri == 0, ri == 1,
                                        [FiAb, bstb], [bkb], inc=(nn == 7 and ri == 1))
                    for par, (bk, bkb) in enumerate(((b0, b0b), (b1_, b1b))):
                        kb.op("dve", lambda e, bk=bk, par=par, ng=ng, zt=zt, xg=xg: e.tensor_tensor(
                            out=zt[:, :, par, ng * 8:(ng + 1) * 8], in0=bk[0:64, :].rearrange("p (n c) -> p c n", n=8),
                            in1=xg[:, :, par, ng * 8:(ng + 1) * 8], op=ALU.mult), [bkb, xgb], dwrites=[ztb])
                kb.dma("act", dst[:, cg * 8192:(cg + 1) * 8192], zt.rearrange("p a b c -> p (a b c)"), reads=[ztb])

    def inproj_fm(self, xTname, tok0, ntok, co, wst, xts, consume):
        A, kb, d = self.A, self.kb, self.d
        (a32, b32), (awb, bwb) = wst
        kb.dma("sp", a32, d["w_in"][:, co:co + 128].rearrange("(k p) c -> p k c", p=128), writes=[b32])
        kb.op("pool", lambda e: e.tensor_copy(out=awb, in_=a32), [b32], [bwb])
        xT = d[xTname]
        for ch in range(ntok // 512):
            ax, bx = xts[self.xi % len(xts)]
            self.xi += 1
            kb.dma("sp", ax, xT[:, :, tok0 + ch * 512: tok0 + (ch + 1) * 512].rearrange("k p t -> p k t"), writes=[bx])
            bank, bankb = self.ps()
            for kc in range(8):
                self.mm(bank, awb[:, kc, :], ax[:, kc, :], kc == 0, kc == 7, [bwb, bx], [bankb], inc=(kc == 7))
            consume(ch * 512, 512, bank, bankb)

    def rms_gate(self, val, valb, gate, gateb, gcol, gcolb):
        A, kb = self.A, self.kb
        sq = [A.alloc([128, 4, 512], BF16, f"sq_{i}") for i in range(2)]
        rs = [A.alloc([128, 512], F32, f"rs_{i}") for i in range(2)]
        tm = [A.alloc([128, 512], F32, f"tm_{i}") for i in range(4)]
        epsc, epsb = A.alloc([128, 1], F32, "epsc")
        kb.op("pool", lambda e: e.memset(epsc, RMS_EPS), [], [epsb])
        outb = []
        ti = 0
        for ch in range(T // 512):
            asq, bsq = sq[ch % 2]
            ars, brs = rs[ch % 2]
            cb_ = Buf(f"rmsch{ch}")
            outb.append(cb_)
            sl = slice(ch * 512, (ch + 1) * 512)
            kb.op("act", lambda e, asq=asq, sl=sl: e.activation(out=asq, in_=val[:, :, sl], func=AF.Square),
                  [valb], [bsq])
            bank, bankb = self.ps()
            for hp in range(4):
                self.mm(bank, self.ones, asq[:, hp, :], hp == 0, hp == 3, [self.onesb, bsq], [bankb], inc=(hp == 3))
            kb.op("act", lambda e, ars=ars, bank=bank: e.activation(out=ars, in_=bank, func=AF.Sqrt, bias=epsc, scale=1.0 / 512.0),
                  [bankb, epsb], [brs])
            kb.op("dve", lambda e, ars=ars: e.reciprocal(out=ars, in_=ars), [brs], [brs])
            for hp in range(4):
                atm, btm = tm[ti % 4]
                ti += 1
                kb.op("dve", lambda e, atm=atm, hp=hp, sl=sl, ars=ars: e.scalar_tensor_tensor(
                    out=atm, in0=val[:, hp, sl], scalar=gcol[:, hp:hp + 1], in1=ars, op0=ALU.mult, op1=ALU.mult),
                    [valb, gcolb, brs], [btm])
                kb.op("pool", lambda e, atm=atm, hp=hp, sl=sl: e.tensor_tensor(
                    out=val[:, hp, sl], in0=atm, in1=gate[:, hp, sl], op=ALU.mult), [btm, gateb, bsq], [cb_])
        return outb

    def _att_job(self, g, r_):
        kb = self.kb
        st = {}

        def A_():
            kind, J, qlo, qhi, rho, pidx, hp = g["kind"], g["J"], g["qlo"], g["qhi"], g["rho"], g["pidx"], g["hp"]
            koff, n_r = g["koff"], g["n_r"]
            wq = qhi - qlo
            if kind == "main":
                nk, k0 = 128, koff + 128 * J
                col0 = qlo - (128 * J - 64)
                Et = r_["E"][:, pidx, 2 * hp:2 * hp + 2, col0:col0 + wq]
                Etb = r_["Eb"]
            elif kind == "left":
                nk, k0 = 64, koff - 64
                Et = r_["EL"][:, pidx, 2 * hp:2 * hp + 2, 0:wq]
                Etb = r_["ELb"]
            else:
                nk, k0 = 64, koff + n_r
                Et = r_["ER"][:, pidx, 2 * hp:2 * hp + 2, 0:wq]
                Etb = r_["ERb"]
            QTv, KTv, VTv = r_["QTv"], r_["KTv"], r_["VTv"]
            s0, s0b = self.ps("tmp")
            s1, s1b = self.ps("tmp")
            self.mm(s0[0:nk, 0:wq], KTv[0:64, k0:k0 + nk, rho], QTv[0:64, qlo:qhi, rho], True, True,
                    [r_["KTb"], r_["QTb"]], [s0b], inc=True)
            self.mm(s1[0:nk, 0:wq], KTv[64:128, k0:k0 + nk, rho], QTv[64:128, qlo:qhi, rho], True, True,
                    [r_["KTb"], r_["QTb"]], [s1b], inc=True)
            i = self.att_i
            self.att_i += 1
            ape, bpe = r_["pes"][i % 6]
            apm, bpm = r_["pms"][i % 6]
            avz, bvz = r_["vzs"][i % 6]
            kb.op("act", lambda e: e.activation(out=ape[0:nk, 0, 0:wq], in_=s0[0:nk, 0:wq], func=AF.Exp, scale=0.125),
                  [s0b], [bpe])
            kb.op("act", lambda e: e.activation(out=ape[0:nk, 1, 0:wq], in_=s1[0:nk, 0:wq], func=AF.Exp, scale=0.125),
                  [s1b], [bpe])
            kb.op("pool", lambda e: e.tensor_tensor(out=apm[0:nk, :, 0:wq], in0=ape[0:nk, :, 0:wq], in1=Et, op=ALU.mult),
                  [bpe, Etb], [bpm])
            tb, tbb = self.ps("tmp")
            tbf = tb.bitcast(BF16)
            self.tr(tbf[0:nk, 0:128], VTv[:, k0:k0 + nk, rho], self.ident, [r_["VTb"], self.identb], [tbb], inc=True)
            kb.op("dve", lambda e: e.tensor_copy(out=avz[0:nk, 0, 0:64], in_=tbf[0:nk, 0:64]), [tbb], [bvz])
            kb.op("dve", lambda e: e.tensor_copy(out=avz[0:nk, 1, 64:128], in_=tbf[0:nk, 64:128]), [tbb], [bvz])
            st.update(nk=nk, wq=wq, apm=apm, bpm=bpm, avz=avz, bvz=bvz)

        def B_():
            ch = g["chunk"]
            c0, nq, qlo, qhi, rho = g["c0"], g["nq"], g["qlo"], g["qhi"], g["rho"]
            if g["first"]:
                ch["num"] = self.ps("acc")
                ch["den"] = self.ps("acc")
            numb, numbb = ch["num"]
            denb, denbb = ch["den"]
            nk, wq, apm, bpm, avz, bvz = st["nk"], st["wq"], st["apm"], st["bpm"], st["avz"], st["bvz"]
            onz, onzb = r_["onz"], r_["onzb"]
            for hs in range(2):
                self.mm(numb[:, qlo - c0:qhi - c0], avz[0:nk, hs, :], apm[0:nk, hs, 0:wq], False, False,
                        [bvz, bpm], [numbb], inc=False)
                self.mm(denb[:, qlo - c0:qhi - c0], onz[0:nk, hs, :], apm[0:nk, hs, 0:wq], False, False,
                        [onzb, bpm], [denbb], inc=(hs == 1))
            if g["last"]:
                NUMv, DENv, NUMb, DENb = r_["NUMv"], r_["DENv"], r_["NUMb"], r_["DENb"]
                if g["first_pat"]:
                    kb.op("act", lambda e: e.activation(out=NUMv[:, c0:c0 + nq, rho], in_=numb[:, 0:nq], func=AF.Copy),
                          [numbb], [NUMb])
                    kb.op("dve", lambda e: e.tensor_copy(out=DENv[:, c0:c0 + nq, rho], in_=denb[:, 0:nq]),
                          [denbb], [DENb])
                else:
                    kb.op("dve", lambda e: e.tensor_tensor(out=NUMv[:, c0:c0 + nq, rho], in0=numb[:, 0:nq],
                                                            in1=NUMv[:, c0:c0 + nq, rho], op=ALU.add), [numbb, NUMb], [NUMb])
                    kb.op("dve", lambda e: e.tensor_tensor(out=DENv[:, c0:c0 + nq, rho], in0=denb[:, 0:nq],
                                                            in1=DENv[:, c0:c0 + nq, rho], op=ALU.add), [denbb, DENb], [DENb])
        return (A_, B_)

    def phase_att(self, xTname, off, next_, halo, Ename, ELname, ERname, MAname):
        A, kb, d = self.A, self.kb, self.d
        self.xi = 0
        E, Eb = A.alloc([128, 3, 8, 256], BF16, "E")
        kb.dma("sp", E.rearrange("p a b c -> p (a b c)"), d[Ename], writes=[Eb])
        if halo:
            EL, ELb = A.alloc([64, 3, 8, 64], BF16, "EL")
            kb.dma("sp", EL.rearrange("p a b c -> p (a b c)"), d[ELname], writes=[ELb])
            ER, ERb = A.alloc([64, 3, 8, 64], BF16, "ER")
            kb.dma("sp", ER.rearrange("p a b c -> p (a b c)"), d[ERname], writes=[ERb])
        onz, onzb = A.alloc([128, 2, 128], BF16, "onz")
        kb.dma("sp", onz.rearrange("p a b -> p (a b)"), d["onesz"], writes=[onzb])
        gcol, gcolb = A.alloc([128, 4], F32, "gcol")
        kb.dma("sp", gcol, d["attn_norm_g"].rearrange("(h p) -> p h", p=128), writes=[gcolb], slow=True)
        ATT, ATTb = A.alloc([128, 4, T], BF16, "ATT")
        GA, GAb = A.alloc([128, 4, T], BF16, "GA")
        QT, QTb = A.alloc([128, T], BF16, "QT")
        KT, KTb = A.alloc([128, next_], BF16, "KT")
        VT, VTb = A.alloc([128, next_], BF16, "VT")
        NUM, NUMb = A.alloc([128, T], F32, "NUM")
        DEN, DENb = A.alloc([128, T], F32, "DEN")
        w32 = A.alloc([128, 8, 128], F32, "w32")
        wbs = A.alloc([128, 8, 128], BF16, "wbs")
        xts = [A.alloc([128, 8, 512], BF16, f"xt_{i}") for i in range(2)]
        pes = [A.alloc([128, 2, 256], BF16, f"pe_{i}") for i in range(6)]
        pms = [A.alloc([128, 2, 256], BF16, f"pm_{i}") for i in range(6)]
        vzs = [A.alloc([128, 2, 128], BF16, f"vz_{i}") for i in range(6)]
        self.att_i = 0
        for (avz, bvz) in vzs:
            kb.op("pool", lambda e, avz=avz: e.memset(avz, 0.0), [], [bvz])
        for hp in range(4):
            def to_tile(dstt, dstb, base, fn=None):
                def consume(c0, n, bank, bankb):
                    if fn is None:
                        self.evac(dstt[:, base + c0: base + c0 + n], bank, [bankb], [dstb])
                    else:
                        kb.op("act", lambda e: e.activation(out=dstt[:, base + c0: base + c0 + n], in_=bank, func=fn),
                              [bankb], dwrites=[dstb])
                return consume
            self.inproj_fm(xTname, off, T, 0 + hp * 128, (w32, wbs), xts, to_tile(QT, QTb, 0))
            self.inproj_fm(xTname, 0, next_, 512 + hp * 128, (w32, wbs), xts, to_tile(KT, KTb, 0))
            self.inproj_fm(xTname, 0, next_, 1024 + hp * 128, (w32, wbs), xts, to_tile(VT, VTb, 0))
            self.inproj_fm(xTname, off, T, 1536 + hp * 128, (w32, wbs), xts, to_tile(GA[:, hp, :], GAb, 0, AF.Silu))
            jobs = []
            for pidx, (_, r) in enumerate(PATTERNS):
                n_r = T // r
                ntile = n_r // 128
                QTv = QT.rearrange("p (i r) -> p i r", r=r)
                KTv = KT.rearrange("p (i r) -> p i r", r=r)
                VTv = VT.rearrange("p (i r) -> p i r", r=r)
                NUMv = NUM.rearrange("p (i r) -> p i r", r=r)
                DENv = DEN.rearrange("p (i r) -> p i r", r=r)
                koff = off // r
                for rho in range(r):
                    for c0 in range(0, n_r, 512):
                        nq = min(512, n_r - c0)
                        segs = []
                        for J in range(c0 // 128 - 1, (c0 + nq) // 128 + 1):
                            if 0 <= J < ntile:
                                kind = "main"
                            elif halo and J == -1:
                                kind = "left"
                            elif halo and J == ntile:
                                kind = "right"
                            else:
                                continue
                            wlo, whi = 128 * J - 64, 128 * J + 192
                            if kind == "left":
                                wlo, whi = 0, 64
                            if kind == "right":
                                wlo, whi = n_r - 64, n_r
                            qlo, qhi = max(wlo, c0), min(whi, c0 + nq)
                            if qhi > qlo:
                                segs.append((kind, J, qlo, qhi))
                        cover = [0] * (nq // 64)
                        for (_, _, qlo, qhi) in segs:
                            for b_ in range((qlo - c0) // 64, (qhi - c0) // 64):
                                cover[b_] += 1
                        assert all(c_ > 0 for c_ in cover), cover
                        chunk = {}
                        for si, (kind, J, qlo, qhi) in enumerate(segs):
                            jobs.append(self._att_job(
                                dict(kind=kind, J=J, qlo=qlo, qhi=qhi, c0=c0, nq=nq, rho=rho, pidx=pidx, hp=hp,
                                     koff=koff, n_r=n_r, first=(si == 0), last=(si == len(segs) - 1),
                                     first_pat=(pidx == 0), chunk=chunk),
                                dict(QTv=QTv, KTv=KTv, VTv=VTv, NUMv=NUMv, DENv=DENv, QTb=QTb, KTb=KTb, VTb=VTb,
                                     NUMb=NUMb, DENb=DENb, E=E, Eb=Eb, EL=EL if halo else None, ELb=ELb if halo else None,
                                     ER=ER if halo else None, ERb=ERb if halo else None, onz=onz, onzb=onzb,
                                     pes=pes, pms=pms, vzs=vzs)))
            SK = 4
            for i in range(len(jobs) + SK):
                if i < len(jobs):
                    jobs[i][0]()
                if i - SK >= 0:
                    jobs[i - SK][1]()
            kb.op("dve", lambda e: e.reciprocal(out=DEN, in_=DEN), [DENb], [DENb])
            kb.op("dve", lambda e, hp=hp: e.tensor_tensor(out=ATT[:, hp, :], in0=NUM, in1=DEN, op=ALU.mult),
                  [NUMb, DENb], dwrites=[ATTb])
        ob = self.rms_gate(ATT, ATTb, GA, GAb, gcol, gcolb)
        kb.dma("act", d[MAname], ATT.rearrange("p a b -> p (a b)"), reads=[ATTb] + ob)

    def phase_mh(self, xTname, off, Z2name, MHname):
        A, kb, d = self.A, self.kb, self.d
        self.xi = 0
        ZF, ZFb = A.alloc([128, 4, T], BF16, "ZF")
        GH, GHb = A.alloc([128, 4, T], BF16, "GH")
        gcol, gcolb = A.alloc([128, 4], F32, "gcolh")
        kb.dma("sp", gcol, d["hyena_norm_g"].rearrange("(h p) -> p h", p=128), writes=[gcolb], slow=True)
        w32 = A.alloc([128, 8, 128], F32, "w32")
        wbs = A.alloc([128, 8, 128], BF16, "wbs")
        xts = [A.alloc([128, 8, 512], BF16, f"xt_{i}") for i in range(2)]
        zts = [A.alloc([64, 128, 64], BF16, f"ztm_{i}") for i in range(2)]
        ZFv = ZF.rearrange("p a (n m) -> p a n m", m=64)
        for cb in range(4):
            zt, ztb = zts[cb % 2]
            kb.dma("sp", zt.rearrange("p a b -> p (a b)"), d[Z2name][:, cb * 8192:(cb + 1) * 8192], writes=[ztb])
            for ng in range(4):
                bank, bankb = self.ps()
                bbf = bank.bitcast(BF16)
                for nn in range(16):
                    n2 = ng * 16 + nn
                    self.tr(bbf[:, nn * 64:(nn + 1) * 64], zt[:, :, n2], self.ident[0:64, 0:64], [ztb, self.identb], [bankb],
                            inc=(nn == 15))
                self.evac(ZFv[:, cb, :, ng * 16:(ng + 1) * 16], bbf.rearrange("p (n m) -> p m n", n=16), [bankb], [ZFb])

            def consume(c0, n, bank, bankb, cb=cb):
                kb.op("act", lambda e: e.activation(out=GH[:, cb, c0:c0 + n], in_=bank, func=AF.Silu), [bankb], dwrites=[GHb])
            self.inproj_fm(xTname, off, T, 3584 + cb * 128, (w32, wbs), xts, consume)
        ob = self.rms_gate(ZF, ZFb, GH, GHb, gcol, gcolb)
        kb.dma("act", d[MHname], ZF.rearrange("p a b -> p (a b)"), reads=[ZFb] + ob)

    def phase_out(self, xname, xrow0, MAname, MHname, yname):
        A, kb, d = self.A, self.kb, self.d
        MA, MAb = A.alloc([128, 4, T], BF16, "MA")
        kb.dma("sp", MA.rearrange("p a b -> p (a b)"), d[MAname], writes=[MAb])
        ZF, ZFb = A.alloc([128, 4, T], BF16, "MH")
        kb.dma("sp", ZF.rearrange("p a b -> p (a b)"), d[MHname], writes=[ZFb])
        wo, wob = A.alloc([128, 8, 1024], BF16, "wo")
        wst = [A.alloc([128, 1024], F32, f"wst_{i}") for i in range(2)]
        for kc in range(8):
            a, b = wst[kc % 2]
            kb.dma("sp", a, d["w_out"][kc * 128:(kc + 1) * 128, :], writes=[b])
            kb.op("pool", lambda e, a=a, kc=kc: e.tensor_copy(out=wo[:, kc, :], in_=a), [b], [wob])
        lg, lgb = A.alloc([128, 1024], F32, "lg")
        kb.dma("sp", lg, d["ln_g"].partition_broadcast(128), writes=[lgb])
        lb, lbb = A.alloc([128, 1024], F32, "lb")
        kb.dma("sp", lb, d["ln_b"].partition_broadcast(128), writes=[lbb])
        epsc, epsb = A.alloc([128, 1], F32, "epsl")
        kb.op("pool", lambda e: e.memset(epsc, LN_EPS), [], [epsb])
        xin = [A.alloc([128, 1024], F32, f"xin_{i}") for i in range(4)]
        hs = [A.alloc([128, 1024], F32, f"h_{i}") for i in range(2)]
        ys = [A.alloc([128, 1024], F32, f"y_{i}") for i in range(2)]
        sts = [A.alloc([128, 2, 6], F32, f"st_{i}") for i in range(2)]
        mvs = [A.alloc([128, 2], F32, f"mv_{i}") for i in range(2)]
        x, y = d[xname], d[yname]

        def ldx(t):
            if t < T // 128:
                kb.dma("sp", xin[t % 4][0], x[xrow0 + t * 128: xrow0 + (t + 1) * 128, :], writes=[xin[t % 4][1]])
        ldx(0)
        ldx(1)
        for t in range(T // 128):
            ax, bx = xin[t % 4]
            ah, bh = hs[t % 2]
            ay, by = ys[t % 2]
            ast_, bst_ = sts[t % 2]
            amv, bmv = mvs[t % 2]
            ldx(t + 2)
            for half in range(2):
                bank, bankb = self.ps()
                for kc in range(8):
                    src, srcb = (MA, MAb) if kc < 4 else (ZF, ZFb)
                    self.mm(bank, src[:, kc % 4, t * 128:(t + 1) * 128], wo[:, kc, half * 512:(half + 1) * 512],
                            kc == 0, kc == 7, [srcb, wob], [bankb], inc=(kc == 7))
                kb.op("dve", lambda e, ah=ah, ax=ax, bank=bank, half=half: e.scalar_tensor_tensor(
                    out=ah[:, half * 512:(half + 1) * 512], in0=ax[:, half * 512:(half + 1) * 512], scalar=ALPHA, in1=bank,
                    op0=ALU.mult, op1=ALU.add), [bx, bankb], [bh])
                kb.op("dve", lambda e, ast_=ast_, ah=ah, half=half: e.bn_stats(out=ast_[:, half, :], in_=ah[:, half * 512:(half + 1) * 512]),
                      [bh], [bst_])
            kb.op("dve", lambda e, amv=amv, ast_=ast_: e.bn_aggr(out=amv, in_=ast_.rearrange("p a b -> p (a b)")), [bst_], [bmv])
            kb.op("act", lambda e, amv=amv: e.activation(out=amv[:, 1:2], in_=amv[:, 1:2], func=AF.Sqrt, bias=epsc, scale=1.0),
                  [bmv, epsb], [bmv])
            kb.op("dve", lambda e, amv=amv: e.reciprocal(out=amv[:, 1:2], in_=amv[:, 1:2]), [bmv], [bmv])
            kb.op("dve", lambda e, ay=ay, ah=ah, amv=amv: e.tensor_scalar(out=ay, in0=ah, scalar1=amv[:, 0:1], scalar2=amv[:, 1:2],
                                                                        op0=ALU.subtract, op1=ALU.mult), [bh, bmv], [by])
            kb.op("pool", lambda e, ay=ay: e.tensor_tensor(out=ay, in0=ay, in1=lg, op=ALU.mult), [by, lgb], [by])
            kb.op("pool", lambda e, ay=ay: e.tensor_tensor(out=ay, in0=ay, in1=lb, op=ALU.add), [by, lbb], [by])
            kb.dma("act", y[t * 128:(t + 1) * 128, :], ay, reads=[by])

    def build(self, kb):
        self.kb = kb
        d = self.d
        ph = self.phases
        self.setup()
        if "xtp" in ph:
            self.phase_xt("xp", "xT_P", SEQ); self.reset()
        if "xts" in ph:
            self.phase_xt("xsf", "xT_SF", DEC_SEQ); self.reset()
            self.phase_xt("xso", "xT_SO", EXT); self.reset()
        if "h1p" in ph:
            gP = []
            for k in range(4):
                gP.append((2048 + k * 128, k * 128, d["UT0_P"][0], k))
            for k in range(4):
                gP.append((2560 + k * 128, 512 + k * 128, d["X1T_P"][0], k))
            for k in range(4):
                gP.append((3072 + k * 128, 1024 + k * 128, d["X2T_P"], k))
            self.phase_h1("xT_P", 0, True, True, gP); self.reset()
        if "h1s" in ph:
            for b in range(4):
                g = []
                for k in range(4):
                    g.append((2048 + k * 128, k * 128, d["UT0_S"][b], k))
                for k in range(4):
                    g.append((2560 + k * 128, 512 + k * 128, d["X1T_S"][b], k))
                self.phase_h1("xT_SF", b * T, b == 0, b == 3, g); self.reset()
            g = [(3072 + k * 128, 1024 + k * 128, d["X2T_S"], k) for k in range(4)]
            self.phase_h1("xT_SO", HALO, False, False, g); self.reset()
        slots = []
        if "hfp" in ph:
            slots.append((0, [(0, d["G_P"][0][0], d["G_P"][0][1]), (1, d["G_P"][1][0], d["G_P"][1][1])]))
        if "hfs" in ph:
            for i in range(7):
                slots.append((1 + i, [(0, d["G_S1"][i][0], d["G_S1"][i][1])]))
            for j in range(4):
                slots.append((8 + j, [(1, d["G_S2"][j][0], d["G_S2"][j][1])]))
        if slots:
            self.phase_hmlp([sl_[0] for sl_ in slots]); self.reset()
            self.phase_hf(slots); self.reset()
        if "hyp" in ph:
            self.phase_hx([d["UT0_P"][0]], [d["XS_P"][0]]); self.reset()
            self.phase_hp([([(d["G_P"][0][0], d["G_P"][0][1])], d["X1T_P"][0], d["Z1T_P"][0])], [d["XS_P"][0]], 1); self.reset()
            self.phase_hx([d["Z1T_P"][0]], [d["XS_P"][0]]); self.reset()
            self.phase_hp([([(d["G_P"][1][0], d["G_P"][1][1])], d["X2T_P"], d["Z2T_P"])], [d["XS_P"][0]], 1); self.reset()
        if "hys" in ph:
            self.phase_hx([d["UT0_S"][j] for j in range(4)], [d["XS_S"][j] for j in range(4)]); self.reset()
            outs = []
            for i in range(4):
                gl = [(d["G_S1"][i - j + 3][0], d["G_S1"][i - j + 3][1]) for j in range(4)]
                outs.append((gl, d["X1T_S"][i], d["Z1T_S"][i]))
            self.phase_hp(outs, [d["XS_S"][j] for j in range(4)], 4); self.reset()
            self.phase_hx([d["Z1T_S"][j] for j in range(4)], [d["XS_S"][j] for j in range(4)]); self.reset()
            gl = [(d["G_S2"][j][0], d["G_S2"][j][1]) for j in range(4)]
            self.phase_hp([(gl, d["X2T_S"], d["Z2T_S"])], [d["XS_S"][j] for j in range(4)], 4); self.reset()
        if "attp" in ph:
            self.phase_att("xT_P", 0, SEQ, False, "E_P", None, None, "MA_P"); self.reset()
        if "atts" in ph:
            self.phase_att("xT_SO", HALO, EXT, True, "E_P", "EL_S", "ER_S", "MA_S"); self.reset()
        if "outp" in ph:
            self.phase_mh("xT_P", 0, "Z2T_P", "MH_P"); self.reset()
            self.phase_out("xp", 0, "MA_P", "MH_P", "yp"); self.reset()
        if "outs" in ph:
            self.phase_mh("xT_SO", HALO, "Z2T_S", "MH_S"); self.reset()
            self.phase_out("xso", HALO, "MA_S", "MH_S", "ys"); self.reset()


ALL_PHASES = ("xtp", "xts", "h1p", "h1s", "hfp", "hfs", "hyp", "hys", "attp", "atts", "outp", "outs")


def build_program(phases=ALL_PHASES, extra=None, ext_in=(), ext_out=()):
    nc = bass.Bass("TRN2", target_bir_lowering=False)
    st = ExitStack()
    P = Prog(nc, st, phases, ext_in, ext_out)
    P.declare()
    if extra is not None:
        extra(P)
    words = 53000
    ar = st.enter_context(nc.sbuf_tensor("arena", [128, words], F32))
    P.A = Arena(ar, words)
    P.bank = []
    P.bankb = []
    for i in range(8):
        P.bank.append(st.enter_context(nc.psum_tensor(f"pb{i}", [128, 512], F32))[:, :])
        P.bankb.append(Buf(f"pb{i}"))
    kb = KB(nc)
    kb.run(st, P.build)
    st.close()
    return nc, P


_CONST_CACHE = {}


def const_inputs():
    if "c" in _CONST_CACHE:
        return _CONST_CACHE["c"]
    c = dict(fft_consts())
    c["ident"] = np.eye(128, dtype=np.float32).astype(NPBF)
    c["ones"] = np.ones((128, 128), np.float32).astype(NPBF)
    oz = np.zeros((128, 2, 128), np.float32)
    oz[:, 0, :64] = 1.0
    oz[:, 1, 64:] = 1.0
    c["onesz"] = oz.reshape(128, 256).astype(NPBF)
    c["delta"] = decay_deltas()
    _CONST_CACHE["c"] = c
    return c


def core_inputs(core, inputs):
    c = dict(const_inputs())
    sb, blk = core // 4, core % 4
    f32 = lambda a: np.ascontiguousarray(np.asarray(a, dtype=np.float32))
    c["xp"] = f32(inputs["x_prompt"][core])
    xs = np.asarray(inputs["x_sample"][sb], dtype=np.float32)
    c["xsf"] = np.ascontiguousarray(xs)
    ext = np.zeros((EXT, D_MODEL), np.float32)
    lo, hi = blk * T - HALO, (blk + 1) * T + HALO
    slo, shi = max(lo, 0), min(hi, DEC_SEQ)
    ext[slo - lo: shi - lo] = xs[slo:shi]
    c["xso"] = ext
    for k in ("w_in", "w_out", "conv_w", "conv_b", "filt_w1", "filt_b1", "filt_w2", "filt_b2", "filt_w3",
              "filt_b3", "filt_freq", "filt_w4", "hyena_d", "attn_norm_g", "hyena_norm_g", "ln_g", "ln_b"):
        c[k] = f32(inputs[k][0])
    lags = [(SEQ, 0)] + [(DEC_SEQ, dd) for dd in range(-3, 4)] + [(DEC_SEQ, blk - j) for j in range(4)]
    key = ("slots", blk)
    if key not in _CONST_CACHE:
        zT = np.zeros((12, 33, NFFT), np.float32)
        sel = np.zeros((12, 128, NFFT), NPBF)
        ntl = np.zeros((12, 128, 64), np.float32)
        flag = np.zeros((1, 12), np.float32)
        for s, (L, dd) in enumerate(lags):
            k2 = ("slot", L, dd)
            if k2 not in _CONST_CACHE:
                _CONST_CACHE[k2] = filter_slot_tables(L, dd)
            zT[s], sel[s], ntl[s], flag[0, s] = _CONST_CACHE[k2]
        _CONST_CACHE[key] = (zT, sel, ntl, flag)
    c["f_zT"], c["f_sel"], c["f_ntl"], c["f_flag"] = _CONST_CACHE[key]
    ek = ("E", blk)
    if ek not in _CONST_CACHE:
        _CONST_CACHE[ek] = attn_tables(blk > 0, blk < 3)
    c["E_P"], c["EL_S"], c["ER_S"] = _CONST_CACHE[ek]
    return c


_PROG = {}


def kernel(**inputs):
    if "nc" not in _PROG:
        _PROG["nc"] = build_program()[0]
    nc = _PROG["nc"]
    in_maps = [core_inputs(core, inputs) for core in range(8)]
    res = run_bass_kernel_spmd(nc, in_maps, core_ids=list(range(8)))
    yp = np.stack([np.asarray(res.results[c]["yp"], dtype=np.float32) for c in range(8)], axis=0)
    ys = np.zeros((2, DEC_SEQ, D_MODEL), np.float32)
    for c in range(8):
        ys[c // 4, (c % 4) * T:(c % 4 + 1) * T] = np.asarray(res.results[c]["ys"], dtype=np.float32)
    return (yp, ys)
```
